# Optimizing a Trainium2 kernel written in Bass

```python
import math
import jax, jax.numpy as jnp
from jax import lax
import numpy as np

D_MODEL = 1024
BATCH = 8
SEQ = 2048
DEPTH = 4

GRID_W = 64
CTX_LEN = 256

CHUNK = 128
GM_WIDTH = D_MODEL
GM_GROUPS = 8
GM_GROUP_DIM = GM_WIDTH // GM_GROUPS
DA_HEADS = 8
DA_HEAD_DIM = 64
DA_V_DIM = 2 * DA_HEAD_DIM
DA_QK_WIDTH = DA_HEADS * 2 * DA_HEAD_DIM
DA_V_WIDTH = DA_HEADS * DA_V_DIM
ATTN_BLOCK = 128
ROPE_BASE = 10000.0
RW_HEAD = 64
RW_WIDTH = D_MODEL
RW_HEADS = RW_WIDTH // RW_HEAD
DECAY_LORA = 64
AAA_LORA = 64
GATE_LORA = 160
N_BRANCH = 3
D_FF = ((8 * D_MODEL + 3 * 256 - 1) // (3 * 256)) * 256

RMS_EPS = 1e-6
GN_EPS = 64e-5

GM_COLS = 2 * GM_WIDTH
DA_COLS = 2 * DA_QK_WIDTH + DA_V_WIDTH
RW_COLS = 3 * RW_WIDTH + 2 * DECAY_LORA + 2 * AAA_LORA + GATE_LORA
GATE_COLS = N_BRANCH * D_MODEL
P_TOTAL = GM_COLS + DA_COLS + RW_COLS + GATE_COLS
P_SPLITS = [GM_COLS, GM_COLS + DA_COLS, GM_COLS + DA_COLS + RW_COLS]
RW_SPLITS = [RW_WIDTH, 2 * RW_WIDTH, 3 * RW_WIDTH, 3 * RW_WIDTH + 2 * DECAY_LORA,
             3 * RW_WIDTH + 2 * DECAY_LORA + 2 * AAA_LORA]

kernel_name = 'hybrid_gmlp_diffattn_rwkv7_dit_block'


def rms_norm(x, g):
    xf = x.astype(jnp.float32)
    y = xf * lax.rsqrt(jnp.mean(xf * xf, axis=-1, keepdims=True) + RMS_EPS)
    return (y * g.astype(jnp.float32)).astype(x.dtype)


def axial_rope_tables(n_rows):
    row = jnp.repeat(jnp.arange(n_rows), GRID_W).astype(jnp.float32)
    col = jnp.tile(jnp.arange(GRID_W), n_rows).astype(jnp.float32)
    axis_dim = DA_HEAD_DIM // 2
    inv_freq = ROPE_BASE ** (-jnp.arange(0, axis_dim, 2, dtype=jnp.float32) / axis_dim)
    ang = jnp.concatenate([row[:, None] * inv_freq, col[:, None] * inv_freq], axis=-1)
    ang = jnp.concatenate([ang, ang], axis=-1)
    return jnp.cos(ang), jnp.sin(ang)


def apply_rope(t, cos, sin):
    half = DA_HEAD_DIM // 2
    rot = jnp.concatenate([-t[..., half:], t[..., :half]], axis=-1)
    cs = cos[None, :, None, None, :]
    sn = sin[None, :, None, None, :]
    return (t * cs + rot * sn).astype(t.dtype)


def centred_shift(p):
    prev = jnp.pad(p[:, :-1], ((0, 0), (1, 0), (0, 0)))
    nxt = jnp.pad(p[:, 1:], ((0, 0), (0, 1), (0, 0)))
    return 0.5 * (prev + nxt)


def chunk_gmlp(p_gm, v_g, w_s, b_s):
    B, T, _ = p_gm.shape
    u = jax.nn.gelu(p_gm[..., :GM_WIDTH], approximate=False)
    v = rms_norm(jax.nn.gelu(p_gm[..., GM_WIDTH:], approximate=False), v_g)
    v = v.reshape(B, T // CHUNK, CHUNK, GM_GROUPS, GM_GROUP_DIM)
    f = jnp.einsum('gts,bnsgc->bntgc', w_s, v) + b_s.T[None, None, :, :, None]
    return u * f.reshape(B, T, GM_WIDTH)


def qk_heads(p, g):
    return rms_norm(p.reshape(p.shape[0], p.shape[1], DA_HEADS, 2, DA_HEAD_DIM), g)


def diff_attend(q, k, v, lam):
    s = jnp.einsum('bqhcd,bkhcd->bhcqk', q, k).astype(jnp.float32) * (DA_HEAD_DIM ** -0.5)
    p = jax.nn.softmax(s, axis=-1)
    pd = (p[:, :, 0] - lam * p[:, :, 1]).astype(v.dtype)
    return jnp.einsum('bhqk,bkhe->bqhe', pd, v)


def diff_out(o, subln_g, lam_init):
    o = rms_norm(o, subln_g) * (1.0 - lam_init)
    return o.reshape(o.shape[0], o.shape[1], DA_V_WIDTH)


def rwkv_prep(p_rw, lp):
    B, T, _ = p_rw.shape
    z = p_rw + lp['rw_mu'] * (centred_shift(p_rw) - p_rw)
    zr, zk, zv, zw, za, zg = jnp.split(z, RW_SPLITS, axis=-1)
    zw = zw.reshape(B, T, 2, DECAY_LORA)
    za = za.reshape(B, T, 2, AAA_LORA)
    w_log = -jax.nn.softplus(-(lp['rw_w0'] + jnp.einsum('btdr,drc->btdc', jnp.tanh(zw), lp['rw_w2']))) - 0.5
    decay = jnp.exp(-jnp.exp(w_log.astype(jnp.float32)))
    a = jax.nn.sigmoid((lp['rw_a0'] + jnp.einsum('btdr,drc->btdc', za, lp['rw_a2'])).astype(jnp.float32))
    kk = (zk * lp['rw_kk']).astype(jnp.float32).reshape(B, T, RW_HEADS, RW_HEAD)
    kk = kk / jnp.maximum(jnp.sqrt(jnp.sum(kk * kk, axis=-1, keepdims=True)), 1e-12)
    kd = zk.astype(jnp.float32)[:, :, None, :] * (1.0 + (a - 1.0) * lp['rw_ka'].astype(jnp.float32))
    heads = lambda t: t.reshape(t.shape[:-1] + (RW_HEADS, RW_HEAD))
    return dict(r=heads(zr.astype(jnp.float32)), v=heads(zv.astype(jnp.float32)), kk=kk,
                decay=heads(decay), kd=heads(kd), a=heads(a), zg=zg)


def rwkv_scan(w, k, v, a_vec, b_vec, r, state0, reverse, with_outputs):
    xs = (w, k, v, a_vec, b_vec) + ((r,) if with_outputs else ())
    xs = tuple(jnp.moveaxis(t, 1, 0) for t in xs)

    def step(S, inp):
        w_t, k_t, v_t, a_t, b_t = inp[:5]
        sa = jnp.einsum('bhij,bhj->bhi', S, a_t)
        S = S * w_t[:, :, None, :] + sa[..., None] * b_t[:, :, None, :] + v_t[..., None] * k_t[:, :, None, :]
        y = jnp.einsum('bhij,bhj->bhi', S, inp[5]) if with_outputs else None
        return S, y

    S, ys = lax.scan(step, state0, xs, reverse=reverse)
    return S, (jnp.moveaxis(ys, 0, 1) if with_outputs else None)


def rwkv_run(pr, states0, with_outputs):
    finals, ys = [], []
    for d in range(2):
        S, y = rwkv_scan(pr['decay'][:, :, d], pr['kd'][:, :, d], pr['v'], -pr['kk'],
                         pr['kk'] * pr['a'][:, :, d], pr['r'], states0[d], d == 1, with_outputs)
        finals.append(S)
        ys.append(y)
    return finals, (ys[0] + ys[1] if with_outputs else None)


def rwkv_out(pr, y, lp, dtype):
    B, T = y.shape[0], y.shape[1]
    mean = jnp.mean(y, axis=-1, keepdims=True)
    var = jnp.mean(jnp.square(y - mean), axis=-1, keepdims=True)
    yn = ((y - mean) * lax.rsqrt(var + GN_EPS)).reshape(B, T, RW_WIDTH) * lp['rw_ln_w'] + lp['rw_ln_b']
    kd_sum = pr['kd'][:, :, 0] + pr['kd'][:, :, 1]
    bonus = jnp.sum(pr['r'] * kd_sum * lp['rw_rk'].astype(jnp.float32), axis=-1, keepdims=True) * pr['v']
    g = jax.nn.sigmoid(pr['zg']) @ lp['rw_g2']
    return ((yn + bonus.reshape(B, T, RW_WIDTH)) * g).astype(dtype)


def merge_branches(a, b, c, gate_p, lp):
    gates = jax.nn.sigmoid(gate_p).reshape(gate_p.shape[0], gate_p.shape[1], N_BRANCH, D_MODEL)
    m = (gates[:, :, 0] * (a @ lp['w_br_a']) + gates[:, :, 1] * (b @ lp['w_br_b'])
         + gates[:, :, 2] * (c @ lp['w_br_c']))
    return m @ lp['w_o']


def token_mixers(h_lat, h_ctx, lp, cos, sin, lam_init, ctx_out):
    B, T, _ = h_lat.shape
    gm_l, da_l, rw_l, gt_l = jnp.split(h_lat @ lp['w_in'], P_SPLITS, axis=-1)
    gm_c, da_c, rw_c, gt_c = jnp.split(h_ctx @ lp['w_in'], P_SPLITS, axis=-1)

    a_l = chunk_gmlp(gm_l, lp['gm_v_g'], lp['gm_ws'], lp['gm_bs'])

    q_l = apply_rope(qk_heads(da_l[..., :DA_QK_WIDTH], lp['da_q_g']), cos, sin)
    k_l = apply_rope(qk_heads(da_l[..., DA_QK_WIDTH:2 * DA_QK_WIDTH], lp['da_k_g']), cos, sin)
    v_l = da_l[..., 2 * DA_QK_WIDTH:].reshape(B, T, DA_HEADS, DA_V_DIM)
    k_c = qk_heads(da_c[..., DA_QK_WIDTH:2 * DA_QK_WIDTH], lp['da_k_g'])
    v_c = da_c[..., 2 * DA_QK_WIDTH:].reshape(B, -1, DA_HEADS, DA_V_DIM)
    lq1, lk1, lq2, lk2 = lp['da_lambda'].astype(jnp.float32)
    lam = jnp.exp(jnp.sum(lq1 * lk1)) - jnp.exp(jnp.sum(lq2 * lk2)) + lam_init
    k_all = jnp.concatenate([k_l, k_c], axis=1)
    v_all = jnp.concatenate([v_l, v_c], axis=1)
    n_blk = T // ATTN_BLOCK
    qb = jnp.moveaxis(q_l.reshape(B, n_blk, ATTN_BLOCK, DA_HEADS, 2, DA_HEAD_DIM), 1, 0)
    ob = lax.map(lambda qi: diff_attend(qi, k_all, v_all, lam), qb)
    o_l = jnp.moveaxis(ob, 0, 1).reshape(B, T, DA_HEADS, DA_V_DIM)
    b_l = diff_out(o_l, lp['da_subln_g'], lam_init)

    pr_c = rwkv_prep(rw_c, lp)
    zero = jnp.zeros((B, RW_HEADS, RW_HEAD, RW_HEAD), jnp.float32)
    states_c, y_c = rwkv_run(pr_c, (zero, zero), ctx_out)
    pr_l = rwkv_prep(rw_l, lp)
    _, y_l = rwkv_run(pr_l, states_c, True)
    c_l = rwkv_out(pr_l, y_l, lp, h_lat.dtype)

    out_l = merge_branches(a_l, b_l, c_l, gt_l, lp)
    if not ctx_out:
        return out_l, None

    a_c = chunk_gmlp(gm_c, lp['gm_v_g'], lp['gm_ws'], lp['gm_bs'])
    q_c = qk_heads(da_c[..., :DA_QK_WIDTH], lp['da_q_g'])
    b_c = diff_out(diff_attend(q_c, k_c, v_c, lam), lp['da_subln_g'], lam_init)
    c_c = rwkv_out(pr_c, y_c, lp, h_ctx.dtype)
    out_c = merge_branches(a_c, b_c, c_c, gt_c, lp)
    return out_l, out_c


def swiglu(h, wi, wo):
    gt, up = jnp.split(h @ wi, 2, axis=-1)
    return (jax.nn.silu(gt) * up) @ wo


def setup_inputs(seed: int = 0) -> dict:
    key = jax.random.key(seed)
    ks = iter(jax.random.split(key, 40))
    nrm = lambda shape, scale: jax.random.normal(next(ks), shape, jnp.float32) * scale
    uni = lambda shape, lo, hi: jax.random.uniform(next(ks), shape, jnp.float32, lo, hi)
    L, D = DEPTH, D_MODEL
    return {
        'x': nrm((BATCH, SEQ, D), 1.0),
        'c': nrm((BATCH, D), 1.0),
        'ctx': nrm((BATCH, CTX_LEN, D), 1.0),
        'c_ctx': nrm((D,), 1.0),
        'ada_w': nrm((L, D, 6 * D), D ** -0.5),
        'ada_b': nrm((L, 6 * D), 0.02),
        'norm1_g': 1.0 + nrm((L, D), 0.02),
        'norm2_g': 1.0 + nrm((L, D), 0.02),
        'w_in': nrm((L, D, P_TOTAL), D ** -0.5),
        'gm_v_g': 1.0 + nrm((L, GM_WIDTH), 0.02),
        'gm_ws': nrm((L, GM_GROUPS, CHUNK, CHUNK), CHUNK ** -0.5),
        'gm_bs': 1.0 + nrm((L, GM_GROUPS, CHUNK), 0.02),
        'da_q_g': 1.0 + nrm((L, DA_HEAD_DIM), 0.02),
        'da_k_g': 1.0 + nrm((L, DA_HEAD_DIM), 0.02),
        'da_lambda': nrm((L, 4, DA_HEAD_DIM), 0.1),
        'da_subln_g': 1.0 + nrm((L, DA_V_DIM), 0.02),
        'rw_mu': uni((L, RW_COLS), 0.0, 1.0),
        'rw_w0': uni((L, 2, RW_WIDTH), -5.0, 1.0),
        'rw_w2': nrm((L, 2, DECAY_LORA, RW_WIDTH), 0.1),
        'rw_a0': nrm((L, 2, RW_WIDTH), 0.1),
        'rw_a2': nrm((L, 2, AAA_LORA, RW_WIDTH), 0.5 * AAA_LORA ** -0.5),
        'rw_g2': nrm((L, GATE_LORA, RW_WIDTH), GATE_LORA ** -0.5),
        'rw_kk': 0.85 + nrm((L, RW_WIDTH), 0.02),
        'rw_ka': 1.0 + nrm((L, RW_WIDTH), 0.02),
        'rw_rk': nrm((L, RW_HEADS, RW_HEAD), 0.1),
        'rw_ln_w': 1.0 + nrm((L, RW_WIDTH), 0.02),
        'rw_ln_b': nrm((L, RW_WIDTH), 0.02),
        'w_br_a': nrm((L, GM_WIDTH, D), GM_WIDTH ** -0.5),
        'w_br_b': nrm((L, DA_V_WIDTH, D), DA_V_WIDTH ** -0.5),
        'w_br_c': nrm((L, RW_WIDTH, D), RW_WIDTH ** -0.5),
        'w_o': nrm((L, D, D), D ** -0.5),
        'ffn_wi': nrm((L, D, 2 * D_FF), D ** -0.5),
        'ffn_wo': nrm((L, D_FF, D), D_FF ** -0.5),
    }


def reference(x, c, ctx, c_ctx, ada_w, ada_b, norm1_g, norm2_g, w_in, gm_v_g, gm_ws, gm_bs,
              da_q_g, da_k_g, da_lambda, da_subln_g, rw_mu, rw_w0, rw_w2, rw_a0, rw_a2, rw_g2,
              rw_kk, rw_ka, rw_rk, rw_ln_w, rw_ln_b, w_br_a, w_br_b, w_br_c, w_o, ffn_wi, ffn_wo):
    n_rows = x.shape[1] // GRID_W
    cos, sin = axial_rope_tables(n_rows)
    xc = ctx
    for l in range(DEPTH):
        ctx_out = l < DEPTH - 1
        lam_init = 0.8 - 0.6 * math.exp(-0.3 * l)
        mod_l = jax.nn.silu(c) @ ada_w[l] + ada_b[l]
        mod_c = jax.nn.silu(c_ctx) @ ada_w[l] + ada_b[l]
        sh1, sc1, g1, sh2, sc2, g2 = jnp.split(mod_l[:, None, :], 6, axis=-1)
        csh1, csc1, cg1, csh2, csc2, cg2 = jnp.split(mod_c, 6, axis=-1)
        lp = dict(w_in=w_in[l], gm_v_g=gm_v_g[l], gm_ws=gm_ws[l], gm_bs=gm_bs[l],
                  da_q_g=da_q_g[l], da_k_g=da_k_g[l], da_lambda=da_lambda[l], da_subln_g=da_subln_g[l],
                  rw_mu=rw_mu[l], rw_w0=rw_w0[l], rw_w2=rw_w2[l], rw_a0=rw_a0[l], rw_a2=rw_a2[l],
                  rw_g2=rw_g2[l], rw_kk=rw_kk[l], rw_ka=rw_ka[l], rw_rk=rw_rk[l],
                  rw_ln_w=rw_ln_w[l], rw_ln_b=rw_ln_b[l], w_br_a=w_br_a[l], w_br_b=w_br_b[l],
                  w_br_c=w_br_c[l], w_o=w_o[l])
        h_lat = rms_norm(x, norm1_g[l]) * (1.0 + sc1) + sh1
        h_ctx = rms_norm(xc, norm1_g[l]) * (1.0 + csc1) + csh1
        mix_l, mix_c = token_mixers(h_lat, h_ctx, lp, cos, sin, lam_init, ctx_out)
        x = x + g1 * mix_l
        x = x + g2 * swiglu(rms_norm(x, norm2_g[l]) * (1.0 + sc2) + sh2, ffn_wi[l], ffn_wo[l])
        if ctx_out:
            xc = xc + cg1 * mix_c
            xc = xc + cg2 * swiglu(rms_norm(xc, norm2_g[l]) * (1.0 + csc2) + csh2, ffn_wi[l], ffn_wo[l])
    return x
```

```python
import numpy as np
import concourse.bass as bass
import concourse.mybir as mybir
from concourse.bass_utils import run_bass_kernel_spmd
from contextlib import ExitStack

F32 = mybir.dt.float32
BF16 = mybir.dt.bfloat16
AF = mybir.ActivationFunctionType
ALU = mybir.AluOpType
AX = mybir.AxisListType

ENGS = ["tensor", "vector", "scalar", "gpsimd", "sync"]
DMA_QUEUES = ["sync", "scalar", "gpsimd"]
EPOCH = 8000
NPOOL = 6


class Prog:
    def __init__(self, nc, stack):
        self.nc = nc
        self.stack = stack
        self.q = {e: [] for e in ENGS}
        self.sems = {}
        self.tick = {e: 0 for e in ENGS}
        self.waited = {e: {} for e in ENGS}
        self.lastw = {}
        self.readers = {}
        self.dpool = {}
        self.dpool_next = {q: 0 for q in DMA_QUEUES}
        for q in DMA_QUEUES:
            self.dpool[q] = []
            for k in range(NPOOL):
                s = stack.enter_context(nc.semaphore(f"d_{q}_{k}"))
                self.dpool[q].append([s, 0])
        self.n_instr = 0

    def sb(self, name, shape, dt):
        return self.stack.enter_context(self.nc.sbuf_tensor(name, list(shape), dt))

    def ps(self, name, shape, dt=F32):
        return self.stack.enter_context(self.nc.psum_tensor(name, list(shape), dt))

    def _esem(self, e, epoch):
        key = (e, epoch)
        if key not in self.sems:
            self.sems[key] = self.stack.enter_context(self.nc.semaphore(f"s_{e}_{epoch}"))
        return self.sems[key]

    def _deps(self, reads, writes):
        toks = []
        for (n, sl) in reads:
            d = self.lastw.get(n)
            if d:
                for k, t in d.items():
                    if sl is None or k is None or k == sl:
                        toks.append(t)
        for (n, sl) in writes:
            d = self.lastw.get(n)
            if d:
                for k, t in d.items():
                    if sl is None or k is None or k == sl:
                        toks.append(t)
            d = self.readers.get(n)
            if d:
                for k, dd in d.items():
                    if sl is None or k is None or k == sl:
                        toks.extend(dd.values())
        return toks

    def _record(self, reads, writes, tok):
        skey = tok[0]
        for (n, sl) in writes:
            d = self.lastw.setdefault(n, {})
            r = self.readers.setdefault(n, {})
            if sl is None:
                d.clear()
                r.clear()
            else:
                r.pop(sl, None)
            d[sl] = tok
        for (n, sl) in reads:
            dd = self.readers.setdefault(n, {}).setdefault(sl, {})
            dd[skey] = tok

    def _waits(self, eng, toks):
        need = {}
        for (skey, sem, val) in toks:
            if self.waited[eng].get(skey, -1) >= val:
                continue
            if skey not in need or need[skey][1] < val:
                need[skey] = (sem, val)
        out = []
        for skey, (sem, val) in need.items():
            self.waited[eng][skey] = val
            out.append((sem, val))
        return out

    def _k(self, x):
        if isinstance(x, tuple):
            return x if len(x) == 2 else (x[0], tuple(x[1:]))
        return (x, None)

    def op(self, eng, fn, reads=(), writes=(), pe_acc=False):
        reads = [self._k(r) for r in reads]
        writes = [self._k(w) for w in writes]
        if eng != "tensor":
            writes = writes + [r for r in reads if r[0] == "ps" and r not in writes]
        toks = self._deps(reads, writes)
        if eng == "tensor":
            toks = [t for t in toks if t[0][0] != "tensor"]
        waits = self._waits(eng, toks)
        self.tick[eng] += 1
        t = self.tick[eng]
        epoch, val = divmod(t, EPOCH)
        if val == 0:
            epoch, val = epoch - 1, EPOCH
        sem = self._esem(eng, epoch)
        tok = ((eng, epoch), sem, val)
        self._record(reads, writes, tok)
        self.n_instr += 1

        def emit(e, waits=waits, fn=fn, sem=sem):
            for (s, v) in waits:
                e.wait_ge(s, v)
            fn(e).then_inc(sem, 1)
        self.q[eng].append(emit)
        return tok

    def dma(self, queue, out, in_, reads=(), writes=(), **kw):
        reads = [self._k(r) for r in reads]
        writes = [self._k(w) for w in writes]
        toks = self._deps(reads, writes)
        i = self.dpool_next[queue]
        self.dpool_next[queue] = (i + 1) % NPOOL
        slot = self.dpool[queue][i]
        sem, prev = slot
        skey = ("dma", queue, i)
        if prev > 0:
            toks.append((skey, sem, prev))
        waits = self._waits(queue, toks)
        val = prev + 16
        slot[1] = val
        tok = (skey, sem, val)
        self._record(reads, writes, tok)
        self.n_instr += 1

        def emit(e, waits=waits, sem=sem):
            for (s, v) in waits:
                e.wait_ge(s, v)
            e.dma_start(out=out, in_=in_, **kw).then_inc(sem, 16)
        self.q[queue].append(emit)
        return tok

    def wait_all(self, eng, keys):
        toks = []
        toks = self._deps([self._k(k) for k in keys], [])
        waits = self._waits(eng, toks)

        def emit(e, waits=waits):
            for (s, v) in waits:
                e.wait_ge(s, v)
        self.q[eng].append(emit)

    def barrier(self):
        toks = []
        for e in ENGS:
            t = self.tick[e]
            if t == 0:
                continue
            epoch, val = divmod(t, EPOCH)
            if val == 0:
                epoch, val = epoch - 1, EPOCH
            toks.append(((e, epoch), self._esem(e, epoch), val))
        for q in DMA_QUEUES:
            for i, (sem, val) in enumerate(self.dpool[q]):
                if val > 0:
                    toks.append((("dma", q, i), sem, val))
        for e in ENGS:
            mine = [t for t in toks if not (t[0][0] == e and e == "tensor")]
            waits = self._waits(e, mine)

            def emit(eng, waits=waits):
                for (s, v) in waits:
                    eng.wait_ge(s, v)
            self.q[e].append(emit)
        self.lastw.clear()
        self.readers.clear()

    def finish(self):
        nc = self.nc
        with nc.Block() as block:
            @block.tensor
            def _(e):
                for f in self.q["tensor"]:
                    f(e)

            @block.vector
            def _(e):
                for f in self.q["vector"]:
                    f(e)

            @block.scalar
            def _(e):
                for f in self.q["scalar"]:
                    f(e)

            @block.gpsimd
            def _(e):
                for f in self.q["gpsimd"]:
                    f(e)

            @block.sync
            def _(e):
                for f in self.q["sync"]:
                    f(e)


D = 1024
DC = 8
NCTX = 256
NLAT = 2048
T = NCTX + NLAT
DFF = 2816
FC = DFF // 128
PTOT = 11680
GM_OFF, DA_OFF, RW_OFF, GT_OFF = 0, 2048, 5120, 8608
TB = [(0, 256), (256, 512), (768, 512), (1280, 512), (1792, 512)]
RMS_EPS = 1e-6
NL = 4


def host_consts():
    c = {}
    c["ident"] = np.eye(128, dtype=np.float32)
    c["ones"] = np.ones((128, 128), dtype=np.float32)
    bd = np.zeros((128, 128), np.float32); bd[:64, :64] = 1; bd[64:, 64:] = 1
    c["bd64"] = bd
    rp = np.zeros((128, 128), np.float32)
    for m in range(128):
        if (m % 64) < 32:
            rp[m + 32, m] = -1.0
        else:
            rp[m - 32, m] = 1.0
    c["rperm"] = rp
    t = np.arange(NLAT)
    row = (t // 64).astype(np.float32); col = (t % 64).astype(np.float32)
    inv = (10000.0 ** (-np.arange(0, 32, 2, dtype=np.float32) / 32)).astype(np.float32)
    ang = np.concatenate([row[:, None] * inv, col[:, None] * inv], -1)
    ang = np.concatenate([ang, ang], -1)
    cosT = np.ones((128, T), np.float32); sinT = np.zeros((128, T), np.float32)
    cosT[:, NCTX:] = np.tile(np.cos(ang).T, (2, 1)); sinT[:, NCTX:] = np.tile(np.sin(ang).T, (2, 1))
    c["cosT"] = cosT.astype(np.float32); c["sinT"] = sinT.astype(np.float32)
    i = np.arange(128)
    IU_, IL_, SU_, SL_ = (i[:, None] <= i[None, :]), (i[:, None] >= i[None, :]), (i[:, None] < i[None, :]), (i[:, None] > i[None, :])
    BD_ = (i[:, None] // 16) == (i[None, :] // 16)
    c["masks"] = np.stack([IU_, IL_, SU_, SL_, SU_ & BD_, SL_ & BD_, SU_ & ~BD_, SL_ & ~BD_]).astype(np.float32)
    return c


class Model:
    def __init__(self, nlayers=NL, do_mix=True, do_ffn=True, dbg=None, nl_w=NL, branches=(1, 1, 1)):
        self.nlayers = nlayers
        self.nl_w = nl_w
        self.branches = branches
        self.do_mix = do_mix
        self.do_ffn = do_ffn
        self.dbg = dbg or {}

    def declare(self, nc):
        L = self.nl_w
        self.in_names = []

        def I(n, s):
            self.in_names.append(n)
            return nc.dram_tensor(n, list(s), F32, kind="ExternalInput").ap()
        self.x = I("x", [NLAT, D]); self.c = I("c", [D]); self.ctx = I("ctx", [NCTX, D]); self.c_ctx = I("c_ctx", [D])
        self.ada_w = I("ada_w", [L, D, 6 * D]); self.ada_b = I("ada_b", [L, 6 * D])
        self.norm1_g = I("norm1_g", [L, D]); self.norm2_g = I("norm2_g", [L, D])
        self.w_in = I("w_in", [L, D, PTOT])
        self.ffn_wi = I("ffn_wi", [L, D, 2 * DFF]); self.ffn_wo = I("ffn_wo", [L, DFF, D])
        self.gm_v_g = I("gm_v_g", [L, D]); self.gm_ws = I("gm_ws", [L, 8, 128, 128]); self.gm_bs = I("gm_bs", [L, 8, 128])
        self.w_br = [I("w_br_a", [L, D, D]), I("w_br_b", [L, D, D]), I("w_br_c", [L, D, D])]
        self.w_o = I("w_o", [L, D, D])
        self.da_q_g = I("da_q_g", [L, 64]); self.da_k_g = I("da_k_g", [L, 64]); self.da_lambda = I("da_lambda", [L, 4, 64]); self.da_subln_g = I("da_subln_g", [L, 128])
        self.rw_mu = I("rw_mu", [L, 3488]); self.rw_w0 = I("rw_w0", [L, 2, D]); self.rw_w2 = I("rw_w2", [L, 2, 64, D]); self.rw_a0 = I("rw_a0", [L, 2, D]); self.rw_a2 = I("rw_a2", [L, 2, 64, D])
        self.rw_g2 = I("rw_g2", [L, 160, D]); self.rw_kk = I("rw_kk", [L, D]); self.rw_ka = I("rw_ka", [L, D]); self.rw_rk = I("rw_rk", [L, 16, 64]); self.rw_ln_w = I("rw_ln_w", [L, D]); self.rw_ln_b = I("rw_ln_b", [L, D])
        self.masks_d = I("masks", [8, 128, 128])
        S = lambda n, sh, dt=F32: nc.dram_tensor(n, list(sh), dt, kind="Internal").ap()
        SD = (lambda n, sh: nc.dram_tensor(n, list(sh), F32, kind="ExternalOutput").ap()) if self.dbg.get("dump_yf") else S
        self.ztok_d = SD("ztok", [T, 3072]); self.smallT_d = S("smallT", [4, 128, T]); self.yf_d = SD("yf", [T, D]); self.xT_d = S("xTspill", [128, DC * T]); self.hT_d = S("hTspill", [128, DC * T], BF16)
        self.ident_d = I("ident", [128, 128]); self.ones_d = I("ones", [128, 128])
        self.bd64_d = I("bd64", [128, 128]); self.rperm_d = I("rperm", [128, 128]); self.cosT_d = I("cosT", [128, T]); self.sinT_d = I("sinT", [128, T])
        self.brT_d = [nc.dram_tensor(f"brT{i}", [DC, 128, T], BF16, kind="Internal").ap() for i in range(3)]
        self.out = nc.dram_tensor("out", [NLAT, D], F32, kind="ExternalOutput").ap()
        if self.dbg.get("dump_ctx"):
            self.outc = nc.dram_tensor("outc", [NCTX, D], F32, kind="ExternalOutput").ap()

    def build(self):
        nc = bass.Bass("TRN2", target_bir_lowering=False)
        self.nc = nc
        self.declare(nc)
        with ExitStack() as st:
            P = Prog(nc, st)
            self.P = P
            self.alloc()
            self.setup()
            for l in range(self.nlayers):
                self.layer(l)
            self.final()
            P.finish()
        return nc

    def alloc(self):
        P = self.P
        self.xT = P.sb("xT", [128, DC, T], F32)
        self.hT = P.sb("hT", [128, DC, T], BF16)
        self.ident = P.sb("identt", [128, 128], F32)
        self.ones = P.sb("onest", [128, 128], F32)
        self.vec = P.sb("vec", [128, 64], F32)
        self.modv = P.sb("modv", [128, 6, DC, 2], F32)
        self.lv = P.sb("lv", [128, 6, DC, 2], F32)
        self.cs = P.sb("cs", [128, DC, 2], F32)
        self.rstd = P.sb("rstd", [128, 512], F32)
        self.ntmp = [P.sb(f"ntmp{i}", [128, 512], F32) for i in range(2)]
        self.arena = P.sb("arena", [128, 17408], F32)
        self.arena_b = self.arena[:].bitcast(BF16)
        self.psum = [P.ps(f"ps{i}", [128, 512]) for i in range(8)]

    def setup(self):
        P = self.P
        xT, ident = self.xT, self.ident
        P.dma("sync", ident[:], self.ident_d[:, :], writes=["ident"])
        P.dma("sync", self.ones[:], self.ones_d[:, :], writes=["ones"])
        P.dma("sync", self.cs[:, :, 0], self.c.rearrange("(c p) -> p c", p=128), writes=["cs"], allow_slow_non_contiguous=True)
        P.dma("sync", self.cs[:, :, 1], self.c_ctx.rearrange("(c p) -> p c", p=128), writes=["cs"], allow_slow_non_contiguous=True)
        P.op("scalar", lambda e: e.activation(out=self.cs[:], in_=self.cs[:], func=AF.Silu), reads=["cs"], writes=["cs"])
        stage = [self.arena[:, i * 1024:(i + 1) * 1024] for i in range(2)]
        for ti in range(T // 128):
            src = self.ctx[ti * 128:(ti + 1) * 128, :] if ti < 2 else self.x[(ti - 2) * 128:(ti - 1) * 128, :]
            sg = stage[ti % 2]
            P.dma("sync", sg, src, writes=[("stage", ti % 2)])
            for half in range(2):
                ps = self.psum[(ti * 2 + half) % 8]
                pk = ("ps", (ti * 2 + half) % 8)
                for j in range(4):
                    dc = half * 4 + j
                    P.op("tensor", lambda e, ps=ps, sg=sg, dc=dc, j=j: e.transpose(ps[:, j * 128:(j + 1) * 128], sg[:, dc * 128:(dc + 1) * 128], ident[:]),
                         reads=[("stage", ti % 2), "ident"], writes=[pk])
                eng = "vector" if half == 0 else "scalar"
                dst = xT[:, half * 4:half * 4 + 4, ti * 128:(ti + 1) * 128]
                srcp = ps[:].rearrange("p (j t) -> p j t", j=4)
                if eng == "vector":
                    P.op("vector", lambda e, dst=dst, srcp=srcp: e.tensor_copy(out=dst, in_=srcp), reads=[pk], writes=[("xT", ti)])
                else:
                    P.op("scalar", lambda e, dst=dst, srcp=srcp: e.copy(out=dst, in_=srcp), reads=[pk], writes=[("xT", ti)])
        P.barrier()

    def final(self):
        P = self.P
        xT, ident = self.xT, self.ident
        stage = [self.arena[:, i * 1024:(i + 1) * 1024] for i in range(2)]
        for ti in range(0 if self.dbg.get("dump_ctx") else 2, T // 128):
            sg = stage[ti % 2]
            for half in range(2):
                ps = self.psum[(ti * 2 + half) % 8]
                pk = ("ps", (ti * 2 + half) % 8)
                for j in range(4):
                    dc = half * 4 + j
                    P.op("tensor", lambda e, ps=ps, dc=dc, j=j, ti=ti: e.transpose(ps[:, j * 128:(j + 1) * 128], xT[:, dc, ti * 128:(ti + 1) * 128], ident[:]),
                         reads=["xT", "ident"], writes=[pk])
                dst = sg[:, half * 512:(half + 1) * 512]
                if half == 0:
                    P.op("vector", lambda e, dst=dst, ps=ps: e.tensor_copy(out=dst, in_=ps[:]), reads=[pk], writes=[("stage", ti % 2)])
                else:
                    P.op("scalar", lambda e, dst=dst, ps=ps: e.copy(out=dst, in_=ps[:]), reads=[pk], writes=[("stage", ti % 2)])
            dst = self.outc[ti * 128:(ti + 1) * 128, :] if ti < 2 else self.out[(ti - 2) * 128:(ti - 1) * 128, :]
            P.dma("sync", dst, sg, reads=[("stage", ti % 2)], writes=["out"])
        P.wait_all("sync", ["out"])

    def load_vec(self, dst, src_1d, key):
        self.P.dma("sync", dst, src_1d.rearrange("(c p) -> p c", p=128), writes=[key], allow_slow_non_contiguous=True)

    def ada(self, l):
        P = self.P
        NB = 256
        wt = [self.arena[:, i * DC * NB:(i + 1) * DC * NB].rearrange("p (c n) -> p c n", c=DC) for i in range(2)]
        brow = self.arena[0:1, 2 * DC * NB:2 * DC * NB + 6 * D]
        P.dma("sync", brow, self.ada_b[l:l + 1, :], writes=["brow"])
        ps = self.psum[0]
        for blk in range(6 * D // NB):
            w = wt[blk % 2]
            wk = ("adaw", blk % 2)
            P.dma("sync", w, self.ada_w[l, :, blk * NB:(blk + 1) * NB].rearrange("(c p) n -> p c n", p=128), writes=[wk])
            for s in range(NB // 128):
                fch = blk * (NB // 128) + s
                o = ps[:, fch * 2:fch * 2 + 2]
                for dc in range(DC):
                    P.op("tensor", lambda e, o=o, w=w, s=s, dc=dc: e.matmul(o, lhsT=w[:, dc, s * 128:(s + 1) * 128], rhs=self.cs[:, dc, :], start=(dc == 0), stop=False),
                         reads=[wk, "cs"], writes=[("ps", 0)])
                P.op("tensor", lambda e, o=o, fch=fch: e.matmul(o, lhsT=brow[:, fch * 128:(fch + 1) * 128], rhs=self.ones[0:1, 0:2], start=False, stop=True),
                     reads=["brow", "ones"], writes=[("ps", 0)])
        P.op("vector", lambda e: e.tensor_copy(out=self.modv[:].rearrange("p k j w -> p (k j w)"), in_=ps[:, 0:96]), reads=[("ps", 0)], writes=["modv"])
        vec = self.vec
        self.load_vec(vec[:, 0:8], self.norm1_g[l], "vec")
        self.load_vec(vec[:, 8:16], self.norm2_g[l], "vec")
        lv, modv = self.lv, self.modv
        for (dst, sc, g0) in [(0, 1, 0), (3, 4, 8)]:
            P.op("vector", lambda e, dst=dst, sc=sc, g0=g0: e.scalar_tensor_tensor(
                out=lv[:, dst], in0=modv[:, sc], scalar=1.0, in1=vec[:, g0:g0 + 8].unsqueeze(2).broadcast_to([128, 8, 2]),
                op0=ALU.add, op1=ALU.mult), reads=["modv", "vec"], writes=["lv"])
        for (dst, src) in [(1, 0), (2, 2), (4, 3), (5, 5)]:
            P.op("vector", lambda e, dst=dst, src=src: e.tensor_copy(out=lv[:, dst], in_=modv[:, src]), reads=["modv"], writes=["lv"])
        P.barrier()

    def norm_mod(self, which, blocks):
        P = self.P
        xT, hT, lv = self.xT, self.hT, self.lv
        for bi in blocks:
            t0, n = TB[bi]
            who = 1 if bi == 0 else 0
            ps = self.psum[bi % 2]
            pk = ("ps", bi % 2)
            for dc in range(DC):
                tmp = self.ntmp[dc % 2]
                P.op("scalar", lambda e, tmp=tmp, dc=dc, t0=t0, n=n: e.activation(out=tmp[:, 0:n], in_=xT[:, dc, t0:t0 + n], func=AF.Square),
                     reads=[("xT", bi)], writes=[("ntmp", dc % 2)])
                P.op("tensor", lambda e, tmp=tmp, dc=dc, n=n, ps=ps: e.matmul(ps[:, 0:n], lhsT=self.ones[:], rhs=tmp[:, 0:n], start=(dc == 0), stop=(dc == DC - 1)),
                     reads=[("ntmp", dc % 2), "ones"], writes=[pk])
            rstd = self.rstd
            P.op("vector", lambda e, n=n, ps=ps: e.tensor_scalar(out=rstd[:, 0:n], in0=ps[:, 0:n], scalar1=1.0 / D, scalar2=RMS_EPS, op0=ALU.mult, op1=ALU.add),
                 reads=[pk], writes=["rstd"])
            P.op("scalar", lambda e, n=n: e.activation(out=rstd[:, 0:n], in_=rstd[:, 0:n], func=AF.Sqrt), reads=["rstd"], writes=["rstd"])
            P.op("vector", lambda e, n=n: e.reciprocal(out=rstd[:, 0:n], in_=rstd[:, 0:n]), reads=["rstd"], writes=["rstd"])
            for dc in range(DC):
                tmp = self.ntmp[dc % 2]
                P.op("vector", lambda e, tmp=tmp, dc=dc, t0=t0, n=n: e.tensor_tensor(out=tmp[:, 0:n], in0=xT[:, dc, t0:t0 + n], in1=rstd[:, 0:n], op=ALU.mult),
                     reads=[("xT", bi), "rstd"], writes=[("ntmp", dc % 2)])
                P.op("scalar", lambda e, tmp=tmp, dc=dc, t0=t0, n=n, who=who: e.activation(
                    out=hT[:, dc, t0:t0 + n], in_=tmp[:, 0:n], func=AF.Identity,
                    scale=lv[:, which * 3 + 0, dc, who:who + 1], bias=lv[:, which * 3 + 1, dc, who:who + 1]),
                    reads=[("ntmp", dc % 2), "lv"], writes=[("hT", bi)])

    def ffn(self, l, blocks):
        P = self.P
        xT, hT, lv = self.xT, self.hT, self.lv
        ab = self.arena_b
        act = [ab[:, g * FC * 512:(g + 1) * FC * 512].rearrange("p (f t) -> p f t", f=FC) for g in range(2)]
        o = 2 * FC * 512
        wi_t = [ab[:, o + i * 2048:o + (i + 1) * 2048].rearrange("p (c n) -> p c n", c=DC) for i in range(2)]
        o += 2 * 2048
        wo_t = [ab[:, o + i * FC * 128:o + (i + 1) * FC * 128].rearrange("p (f n) -> p f n", f=FC) for i in range(2)]
        o += 2 * FC * 128
        assert o <= 34816, o
        sg = [self.ntmp[0], self.ntmp[1]]
        groups = [blocks[i:i + 2] for i in range(0, len(blocks), 2)]
        for grp in groups:
            for f in range(FC):
                w = wi_t[f % 2]
                wk = ("wi", f % 2)
                P.dma("gpsimd", w[:, :, 0:128], self.ffn_wi[l, :, f * 128:(f + 1) * 128].rearrange("(c p) n -> p c n", p=128), writes=[wk])
                P.dma("gpsimd", w[:, :, 128:256], self.ffn_wi[l, :, DFF + f * 128:DFF + (f + 1) * 128].rearrange("(c p) n -> p c n", p=128), writes=[wk])
                for gi, bi in enumerate(grp):
                    t0, n = TB[bi]
                    b0 = (f % 2) * 4 + gi * 2
                    pg, pu = self.psum[b0], self.psum[b0 + 1]
                    kg, ku = ("ps", b0), ("ps", b0 + 1)
                    for (ps, pk, c0) in [(pg, kg, 0), (pu, ku, 128)]:
                        for dc in range(DC):
                            P.op("tensor", lambda e, ps=ps, w=w, c0=c0, dc=dc, t0=t0, n=n: e.matmul(ps[:, 0:n], lhsT=w[:, dc, c0:c0 + 128], rhs=hT[:, dc, t0:t0 + n], start=(dc == 0), stop=(dc == DC - 1)),
                                 reads=[wk, ("hT", bi)], writes=[pk])
                    s_ = sg[gi]
                    a_ = act[gi]
                    P.op("scalar", lambda e, s_=s_, pg=pg, n=n: e.activation(out=s_[:, 0:n], in_=pg[:, 0:n], func=AF.Silu), reads=[kg], writes=[("ntmp", gi)])
                    P.op("vector", lambda e, s_=s_, pu=pu, n=n, f=f, a_=a_: e.tensor_tensor(out=a_[:, f, 0:n], in0=s_[:, 0:n], in1=pu[:, 0:n], op=ALU.mult),
                         reads=[ku, ("ntmp", gi)], writes=[("act", gi, f)])
            for j in range(DC):
                w = wo_t[j % 2]
                wk = ("wo", j % 2)
                P.dma("gpsimd", w, self.ffn_wo[l, :, j * 128:(j + 1) * 128].rearrange("(f p) n -> p f n", p=128), writes=[wk])
                for gi, bi in enumerate(grp):
                    t0, n = TB[bi]
                    who = 1 if bi == 0 else 0
                    b0 = (j % 2) * 2 + gi
                    ps = self.psum[b0]
                    pk = ("ps", b0)
                    a_ = act[gi]
                    for f in range(FC):
                        P.op("tensor", lambda e, ps=ps, w=w, f=f, n=n, a_=a_: e.matmul(ps[:, 0:n], lhsT=w[:, f, :], rhs=a_[:, f, 0:n], start=(f == 0), stop=(f == FC - 1)),
                             reads=[wk, ("act", gi, f)], writes=[pk])
                    P.op("vector", lambda e, ps=ps, j=j, t0=t0, n=n, who=who: e.scalar_tensor_tensor(
                        out=xT[:, j, t0:t0 + n], in0=ps[:, 0:n], scalar=lv[:, 5, j, who:who + 1], in1=xT[:, j, t0:t0 + n], op0=ALU.mult, op1=ALU.add),
                        reads=[pk, "lv", ("xT", bi)], writes=[("xT", bi)])

    def layer(self, l):
        P = self.P
        last = (l == NL - 1) and not self.dbg.get('ctx_always')
        blocks_all = list(range(5))
        blocks_out = [1, 2, 3, 4] if last else blocks_all
        self.ada(l)
        if self.do_mix:
            self.norm_mod(0, blocks_all)
            P.barrier()
            self.mixers(l, blocks_out)
            P.barrier()
        if self.do_ffn:
            self.norm_mod(1, blocks_out)
            P.barrier()
            self.ffn(l, blocks_out)
            P.barrier()

    def mixers(self, l, blocks_out):
        P = self.P
        if self.branches[0]:
            self.gmlp(l, blocks_out)
        else:
            self.zero_branch(0)
        P.barrier()
        if self.branches[1]:
            self.attention(l, blocks_out)
        else:
            self.zero_branch(1)
        P.barrier()
        if self.branches[2]:
            self.rwkv(l, blocks_out)
        else:
            self.zero_branch(2)
        P.barrier()
        self.merge(l, blocks_out)

    def zero_branch(self, i):
        P = self.P
        z = self.arena_b[:, 0:T]
        P.op("vector", lambda e: e.memset(z, 0.0), writes=["z"])
        for k in range(DC):
            P.dma("sync", self.brT_d[i][k], z, reads=["z"], writes=[f"brT{i}"])

    def gmlp(self, l, blocks_out):
        P = self.P
        hT = self.hT
        ab, af = self.arena_b, self.arena
        Wu = ab[:, 0:8192].rearrange("p (c n) -> p c n", c=DC)
        Wv = ab[:, 8192:16384].rearrange("p (c n) -> p c n", c=DC)
        wsT = ab[:, 16384:17408].rearrange("p (g t) -> p g t", g=8)
        vn = ab[:, 17408:18432]
        aT = [ab[:, 18432 + i * 1024:18432 + (i + 1) * 1024].rearrange("p (g t) -> p g t", g=8) for i in range(2)]
        o = 20480 // 2
        gv = af[:, o:o + 1024]; o += 1024
        gu = [af[:, o + i * 512:o + (i + 1) * 512] for i in range(2)]; o += 1024
        ft = [af[:, o + i * 512:o + (i + 1) * 512] for i in range(2)]; o += 1024
        vg_rep = af[:, o:o + 1024]; o += 1024
        bs_rep = af[:, o:o + 1024]; o += 1024
        ws_ld = af[:, o:o + 1024].rearrange("p (g s) -> p g s", g=8); o += 1024
        sq = af[:, o:o + 1024]; o += 1024
        st = self.vec[:, 32:40]
        assert o <= 17408
        for c in range(2):
            P.dma("gpsimd", Wu[:, :, c * 512:(c + 1) * 512], self.w_in[l, :, GM_OFF + c * 512:GM_OFF + (c + 1) * 512].rearrange("(c p) n -> p c n", p=128), writes=["Wu"])
            P.dma("gpsimd", Wv[:, :, c * 512:(c + 1) * 512], self.w_in[l, :, GM_OFF + D + c * 512:GM_OFF + D + (c + 1) * 512].rearrange("(c p) n -> p c n", p=128), writes=["Wv"])
        P.dma("sync", vg_rep, self.gm_v_g[l].partition_broadcast(128), writes=["vg_rep"])
        P.dma("sync", bs_rep, self.gm_bs[l].rearrange("g t -> (g t)").partition_broadcast(128), writes=["bs_rep"])
        P.dma("sync", ws_ld, self.gm_ws[l].rearrange("g t s -> t g s"), writes=["ws_ld"])
        for half in range(2):
            ps = self.psum[half]
            for j in range(4):
                g = half * 4 + j
                P.op("tensor", lambda e, ps=ps, j=j, g=g: e.transpose(ps[:, j * 128:(j + 1) * 128], ws_ld[:, g, :], self.ident[:]),
                     reads=["ws_ld", "ident"], writes=[("ps", half)])
            P.op("vector", lambda e, ps=ps, half=half: e.tensor_copy(out=wsT[:, half * 4:half * 4 + 4, :], in_=ps[:].rearrange("p (j t) -> p j t", j=4)),
                 reads=[("ps", half)], writes=["wsT"])
        tiles = []
        for bi in blocks_out:
            t0, n = TB[bi]
            tiles += [(bi, t0 + i * 128) for i in range(n // 128)]
        for it, (bi, tt) in enumerate(tiles):
            for half in range(2):
                ps = self.psum[half]
                for dc in range(DC):
                    P.op("tensor", lambda e, ps=ps, dc=dc, tt=tt, half=half: e.matmul(ps[:], lhsT=hT[:, dc, tt:tt + 128], rhs=Wv[:, dc, half * 512:(half + 1) * 512], start=(dc == 0), stop=(dc == DC - 1)),
                         reads=["Wv", ("hT", bi)], writes=[("ps", half)])
                P.op("scalar", lambda e, ps=ps, half=half: e.activation(out=gv[:, half * 512:(half + 1) * 512], in_=ps[:], func=AF.Gelu),
                     reads=[("ps", half)], writes=[("gv", half)])
            P.op("scalar", lambda e: e.activation(out=sq, in_=gv, func=AF.Square, accum_out=st[:, 0:1]), reads=["gv"], writes=["sq", "st"])
            P.op("vector", lambda e: e.tensor_scalar(out=st[:, 1:2], in0=st[:, 0:1], scalar1=1.0 / D, scalar2=RMS_EPS, op0=ALU.mult, op1=ALU.add), reads=["st"], writes=["st"])
            P.op("scalar", lambda e: e.activation(out=st[:, 2:3], in_=st[:, 1:2], func=AF.Sqrt), reads=["st"], writes=["st"])
            P.op("vector", lambda e: e.reciprocal(out=st[:, 3:4], in_=st[:, 2:3]), reads=["st"], writes=["st"])
            P.op("vector", lambda e: e.scalar_tensor_tensor(out=vn, in0=gv, scalar=st[:, 3:4], in1=vg_rep, op0=ALU.mult, op1=ALU.mult),
                 reads=["gv", "st", "vg_rep"], writes=["vn"])
            a_t = aT[it % 2]
            ak = ("aT", it % 2)
            for half in range(2):
                pu, pf = self.psum[2 + half], self.psum[4 + half]
                ku, kf = ("ps", 2 + half), ("ps", 4 + half)
                for j in range(4):
                    g = half * 4 + j
                    for dc in range(DC):
                        P.op("tensor", lambda e, pu=pu, j=j, g=g, dc=dc, tt=tt: e.matmul(pu[:, j * 128:(j + 1) * 128], lhsT=Wu[:, dc, g * 128:(g + 1) * 128], rhs=hT[:, dc, tt:tt + 128], start=(dc == 0), stop=(dc == DC - 1)),
                             reads=["Wu", ("hT", bi)], writes=[ku])
                    P.op("tensor", lambda e, pf=pf, j=j, g=g: e.matmul(pf[:, j * 128:(j + 1) * 128], lhsT=vn[:, g * 128:(g + 1) * 128], rhs=wsT[:, g, :], start=True, stop=True),
                         reads=["vn", "wsT"], writes=[kf])
                P.op("scalar", lambda e, pu=pu, half=half: e.activation(out=gu[half], in_=pu[:], func=AF.Gelu), reads=[ku], writes=[("gu", half)])
                P.op("vector", lambda e, pf=pf, half=half: e.tensor_tensor(out=ft[half], in0=pf[:], in1=bs_rep[:, half * 512:(half + 1) * 512], op=ALU.add),
                     reads=[kf, "bs_rep"], writes=[("ft", half)])
                P.op("vector", lambda e, half=half, a_t=a_t: e.tensor_tensor(out=a_t[:, half * 4:half * 4 + 4, :], in0=ft[half].rearrange("p (j t) -> p j t", j=4), in1=gu[half].rearrange("p (j t) -> p j t", j=4), op=ALU.mult),
                     reads=[("ft", half), ("gu", half)], writes=[ak])
            P.dma("sync", self.brT_d[0][:, :, tt:tt + 128].rearrange("g p t -> p g t"), a_t, reads=[ak], writes=["brT0"])

    def attention(self, l, blocks_out):
        import math
        P = self.P
        hT = self.hT
        ab, af = self.arena_b, self.arena
        lam_init = 0.8 - 0.6 * math.exp(-0.3 * l)
        ctx_out = 0 in blocks_out
        o = 0
        def F(n):
            nonlocal o
            r = af[:, o:o + n]; o += n
            return r
        def B(n):
            nonlocal o
            r = ab[:, 2 * o:2 * o + n]; o += (n + 1) // 2
            return r
        cosT = F(T); sinT = F(T); qf = F(T)
        qT = B(T); kT = B(T)
        Vext = B(18 * 130).rearrange("p (k e) -> p k e", k=18)
        Wq = B(1024).rearrange("p (c n) -> p c n", c=DC); Wk = B(1024).rearrange("p (c n) -> p c n", c=DC); Wv = B(1024).rearrange("p (c n) -> p c n", c=DC)
        Et4 = [B(512) for _ in range(4)]
        tmp = [F(512) for _ in range(2)]
        o_sb = F(512).rearrange("p (s e) -> p s e", s=4)
        b_all = F(512).rearrange("p (s e) -> p s e", s=4)
        bT = B(512)
        g_rep = F(128)
        lamt = F(256)
        bd64 = F(128); rperm = F(128)
        zb = B(512)
        sc = F(32)
        gq = sc[:, 0:1]; gk = sc[:, 1:2]; lamv = sc[:, 2:3]; nlam = sc[:, 3:4]
        assert o <= 17408, o
        P.dma("sync", cosT, self.cosT_d[:, :], writes=["cosT"])
        P.dma("sync", sinT, self.sinT_d[:, :], writes=["sinT"])
        P.dma("sync", bd64, self.bd64_d[:, :], writes=["bd64"])
        P.dma("sync", rperm, self.rperm_d[:, :], writes=["rperm"])
        for h2 in range(2):
            P.dma("sync", sc[h2 * 64:(h2 + 1) * 64, 0:1], self.da_q_g[l].rearrange("(d o) -> d o", o=1), writes=["sc"])
            P.dma("sync", sc[h2 * 64:(h2 + 1) * 64, 1:2], self.da_k_g[l].rearrange("(d o) -> d o", o=1), writes=["sc"])
        P.dma("sync", g_rep, self.da_subln_g[l].partition_broadcast(128), writes=["g_rep"])
        P.dma("sync", lamt, self.da_lambda[l].rearrange("a d -> (a d)").partition_broadcast(128), writes=["lamt"])
        P.op("vector", lambda e: e.memset(zb, 0.0), writes=["zb"])
        P.op("vector", lambda e: e.memset(Vext[:, :, 128:129], 1.0), writes=["Vext"])
        P.op("vector", lambda e: e.tensor_scalar(out=g_rep, in0=g_rep, scalar1=(1.0 - lam_init), scalar2=None, op0=ALU.mult), reads=["g_rep"], writes=["g_rep"])
        for i in range(2):
            P.op("vector", lambda e, i=i: e.tensor_tensor(out=tmp[0][:, i * 64:(i + 1) * 64], in0=lamt[:, i * 128:i * 128 + 64], in1=lamt[:, i * 128 + 64:i * 128 + 128], op=ALU.mult),
                 reads=["lamt"], writes=[("tmp", 0)])
            P.op("vector", lambda e, i=i: e.reduce_sum(out=sc[:, 4 + i:5 + i], in_=tmp[0][:, i * 64:(i + 1) * 64], axis=AX.X), reads=[("tmp", 0)], writes=["sc"])
        P.op("scalar", lambda e: e.activation(out=sc[:, 6:8], in_=sc[:, 4:6], func=AF.Exp), reads=["sc"], writes=["sc"])
        P.op("vector", lambda e: e.tensor_tensor(out=sc[:, 8:9], in0=sc[:, 6:7], in1=sc[:, 7:8], op=ALU.subtract), reads=["sc"], writes=["sc"])
        P.op("vector", lambda e: e.tensor_scalar(out=nlam, in0=sc[:, 8:9], scalar1=lam_init, scalar2=-1.0, op0=ALU.add, op1=ALU.mult), reads=["sc"], writes=["sc"])

        qblocks = [(TB[bi][0], TB[bi][1], list(range(18))) for bi in (1, 2, 3, 4)]
        if ctx_out:
            qblocks.append((0, 256, [0, 1]))
        def load_w(hd):
            for (W, off, nm) in [(Wq, 0, "Wq"), (Wk, 1024, "Wk"), (Wv, 2048, "Wv")]:
                c0 = DA_OFF + off + hd * 128
                P.dma("gpsimd", W, self.w_in[l, :, c0:c0 + 128].rearrange("(c p) n -> p c n", p=128), writes=[nm])

        load_w(0)
        for hd in range(8):
            for (W, nm, gcol, dst, dnm) in [(Wq, "Wq", gq, qT, "qT"), (Wk, "Wk", gk, kT, "kT")]:
                for bi in range(5):
                    t0, n = TB[bi]
                    ps = self.psum[4 + bi % 2]; pk = ("ps", 4 + bi % 2)
                    for dc in range(DC):
                        P.op("tensor", lambda e, ps=ps, W=W, dc=dc, t0=t0, n=n: e.matmul(ps[:, 0:n], lhsT=W[:, dc, :], rhs=hT[:, dc, t0:t0 + n], start=(dc == 0), stop=(dc == DC - 1)),
                             reads=[nm, ("hT", bi)], writes=[pk])
                    P.op("scalar", lambda e, ps=ps, t0=t0, n=n: e.copy(out=qf[:, t0:t0 + n], in_=ps[:, 0:n]), reads=[pk], writes=[("qf", bi)])
                    tq = tmp[bi % 2]; tk = ("tmp", bi % 2)
                    P.op("scalar", lambda e, tq=tq, t0=t0, n=n: e.activation(out=tq[:, 0:n], in_=qf[:, t0:t0 + n], func=AF.Square), reads=[("qf", bi)], writes=[tk])
                    p2 = self.psum[6]; k2 = ("ps", 6)
                    P.op("tensor", lambda e, p2=p2, tq=tq, n=n: e.matmul(p2[:, 0:n], lhsT=bd64, rhs=tq[:, 0:n], start=True, stop=True), reads=[tk, "bd64"], writes=[k2])
                    P.op("vector", lambda e, p2=p2, tq=tq, n=n: e.tensor_scalar(out=tq[:, 0:n], in0=p2[:, 0:n], scalar1=1.0 / 64, scalar2=RMS_EPS, op0=ALU.mult, op1=ALU.add), reads=[k2], writes=[tk])
                    P.op("scalar", lambda e, tq=tq, n=n: e.activation(out=tq[:, 0:n], in_=tq[:, 0:n], func=AF.Sqrt), reads=[tk], writes=[tk])
                    P.op("vector", lambda e, tq=tq, n=n: e.reciprocal(out=tq[:, 0:n], in_=tq[:, 0:n]), reads=[tk], writes=[tk])
                    P.op("vector", lambda e, tq=tq, t0=t0, n=n, gcol=gcol: e.scalar_tensor_tensor(out=qf[:, t0:t0 + n], in0=qf[:, t0:t0 + n], scalar=gcol, in1=tq[:, 0:n], op0=ALU.mult, op1=ALU.mult),
                         reads=[("qf", bi), tk, "sc"], writes=[("qf", bi)])
                    p3 = self.psum[7]; k3 = ("ps", 7)
                    P.op("tensor", lambda e, p3=p3, t0=t0, n=n: e.matmul(p3[:, 0:n], lhsT=rperm, rhs=qf[:, t0:t0 + n], start=True, stop=True), reads=[("qf", bi), "rperm"], writes=[k3])
                    P.op("vector", lambda e, p3=p3, tq=tq, t0=t0, n=n: e.tensor_tensor(out=tq[:, 0:n], in0=p3[:, 0:n], in1=sinT[:, t0:t0 + n], op=ALU.mult), reads=[k3, "sinT"], writes=[tk])
                    P.op("gpsimd", lambda e, t0=t0, n=n: e.tensor_tensor(out=qf[:, t0:t0 + n], in0=qf[:, t0:t0 + n], in1=cosT[:, t0:t0 + n], op=ALU.mult), reads=[("qf", bi), "cosT"], writes=[("qf", bi)])
                    P.op("vector", lambda e, tq=tq, dst=dst, t0=t0, n=n: e.tensor_tensor(out=dst[:, t0:t0 + n], in0=qf[:, t0:t0 + n], in1=tq[:, 0:n], op=ALU.add), reads=[("qf", bi), tk], writes=[(dnm, bi)])
            for g4 in range(5):
                tiles = list(range(g4 * 4, min(18, g4 * 4 + 4)))
                ps = self.psum[4 + g4 % 2]; pk = ("ps", 4 + g4 % 2)
                for j, ti in enumerate(tiles):
                    for dc in range(DC):
                        P.op("tensor", lambda e, ps=ps, j=j, ti=ti, dc=dc: e.matmul(ps[:, j * 128:(j + 1) * 128], lhsT=hT[:, dc, ti * 128:(ti + 1) * 128], rhs=Wv[:, dc, :], start=(dc == 0), stop=(dc == DC - 1)),
                             reads=["Wv", "hT"], writes=[pk])
                nt = len(tiles)
                P.op("vector", lambda e, ps=ps, g4=g4, nt=nt: e.tensor_copy(out=Vext[:, g4 * 4:g4 * 4 + nt, 0:128], in_=ps[:, 0:nt * 128].rearrange("p (j e) -> p j e", j=nt)),
                     reads=[pk], writes=["Vext"])
            if hd + 1 < 8:
                load_w(hd + 1)
            for (q0, nq, kts) in qblocks:
                nsub = nq // 128
                accs2 = [[self.psum[2 + 2 * c], self.psum[3 + 2 * c]][:(nsub + 1) // 2] for c in range(2)]
                for c in range(2):
                    for ai, A in enumerate(accs2[c]):
                        P.op("tensor", lambda e, A=A: e.matmul(A[:, 0:512], lhsT=zb[0:1, 0:128], rhs=zb[0:1, 0:512], start=True, stop=False, skip_group_check=True), reads=["zb"], writes=[("ps", 2 + 2 * c + ai)])
                sbank = [[0, 1], [6, 7]]

                def emit_qk(ki, q0=q0, nq=nq, kts=kts):
                    kt = kts[ki]
                    for c in range(2):
                        bnk = sbank[ki % 2][c]
                        pS = self.psum[bnk]
                        P.op("tensor", lambda e, pS=pS, kt=kt, c=c: e.matmul(pS[:, 0:nq], lhsT=kT[c * 64:(c + 1) * 64, kt * 128:(kt + 1) * 128], rhs=qT[c * 64:(c + 1) * 64, q0:q0 + nq], start=True, stop=True),
                             reads=["kT", "qT"], writes=[("ps", bnk)])

                def emit_pv(ki, nq=nq, kts=kts, nsub=nsub, accs2=accs2):
                    kt = kts[ki]
                    for c in range(2):
                        bnk = sbank[ki % 2][c]
                        pS = self.psum[bnk]
                        E = Et4[(ki % 2) * 2 + c]; kE = ("Et", (ki % 2) * 2 + c)
                        P.op("scalar", lambda e, pS=pS, E=E: e.activation(out=E[:, 0:nq], in_=pS[:, 0:nq], func=AF.Exp, scale=0.125), reads=[("ps", bnk)], writes=[kE])
                        for qs in range(nsub):
                            A = accs2[c][qs // 2]; col = (qs % 2) * 129
                            P.op("tensor", lambda e, A=A, col=col, E=E, qs=qs, kt=kt, last=(ki == len(kts) - 1): e.matmul(A[:, col:col + 129], lhsT=E[:, qs * 128:(qs + 1) * 128], rhs=Vext[:, kt, 0:129], start=False, stop=last, skip_group_check=True),
                                 reads=[kE, "Vext"], writes=[("ps", 2 + 2 * c + qs // 2)])

                emit_qk(0)
                for ki in range(len(kts)):
                    if ki + 1 < len(kts):
                        emit_qk(ki + 1)
                    emit_pv(ki)
                for c in range(2):
                    ab0 = 2 + 2 * c
                    accs = accs2[c]
                    for ai, A in enumerate(accs):
                        kA = ("ps", ab0 + ai)
                        Av = A[:, 0:258].rearrange("p (s e) -> p s e", s=2)
                        rz = sc[:, 10 + 2 * ai:12 + 2 * ai]
                        rzb = rz.unsqueeze(2).broadcast_to([128, 2, 128])
                        ov = o_sb[:, 2 * ai:2 * ai + 2, :]
                        P.op("vector", lambda e, Av=Av, rz=rz: e.reciprocal(out=rz.unsqueeze(2), in_=Av[:, :, 128:129]), reads=[kA], writes=[("rz", ai)])
                        if c == 0:
                            P.op("vector", lambda e, Av=Av, rzb=rzb, ov=ov: e.tensor_tensor(out=ov, in0=Av[:, :, 0:128], in1=rzb, op=ALU.mult), reads=[kA, ("rz", ai)], writes=[("o_sb", ai)])
                        else:
                            t2v = tmp[ai][:, 0:256].rearrange("p (s e) -> p s e", s=2)
                            P.op("vector", lambda e, rz=rz: e.tensor_scalar(out=rz, in0=rz, scalar1=nlam, scalar2=None, op0=ALU.mult), reads=[("rz", ai), "sc"], writes=[("rz", ai)])
                            P.op("vector", lambda e, Av=Av, rzb=rzb, t2v=t2v: e.tensor_tensor(out=t2v, in0=Av[:, :, 0:128], in1=rzb, op=ALU.mult), reads=[kA, ("rz", ai)], writes=[("tmp", ai)])
                            P.op("gpsimd", lambda e, ov=ov, t2v=t2v: e.tensor_tensor(out=ov, in0=ov, in1=t2v, op=ALU.add), reads=[("tmp", ai), ("o_sb", ai)], writes=[("o_sb", ai)])
                pT = self.psum[6]; kT_ = ("ps", 6)
                ov = o_sb[:, 0:nsub, :]
                sqv = tmp[0][:, 0:nsub * 128]
                ssv = sc[:, 16:16 + nsub]
                bv = b_all[:, 0:nsub, :]
                P.op("scalar", lambda e, ov=ov, sqv=sqv, nsub=nsub: e.activation(out=sqv.rearrange("p (s e) -> p s e", s=nsub), in_=ov, func=AF.Square), reads=["o_sb"], writes=[("tmp", 0)])
                P.op("vector", lambda e, sqv=sqv, ssv=ssv, nsub=nsub: e.reduce_sum(out=ssv, in_=sqv.rearrange("p (s e) -> p s e", s=nsub), axis=AX.X), reads=[("tmp", 0)], writes=["ss"])
                P.op("vector", lambda e, ssv=ssv: e.tensor_scalar(out=ssv, in0=ssv, scalar1=1.0 / 128, scalar2=RMS_EPS, op0=ALU.mult, op1=ALU.add), reads=["ss"], writes=["ss"])
                P.op("scalar", lambda e, ssv=ssv: e.activation(out=ssv, in_=ssv, func=AF.Sqrt), reads=["ss"], writes=["ss"])
                P.op("vector", lambda e, ssv=ssv: e.reciprocal(out=ssv, in_=ssv), reads=["ss"], writes=["ss"])
                P.op("vector", lambda e, ov=ov, bv=bv, ssv=ssv, nsub=nsub: e.tensor_tensor(out=bv, in0=ov, in1=ssv.unsqueeze(2).broadcast_to([128, nsub, 128]), op=ALU.mult), reads=["o_sb", "ss"], writes=["b_all"])
                P.op("gpsimd", lambda e, bv=bv, nsub=nsub: e.tensor_tensor(out=bv, in0=bv, in1=g_rep.unsqueeze(1).broadcast_to([128, nsub, 128]), op=ALU.mult), reads=["b_all", "g_rep"], writes=["b_all"])
                for qs in range(nsub):
                    P.op("tensor", lambda e, pT=pT, qs=qs: e.transpose(pT[:, qs * 128:(qs + 1) * 128], b_all[:, qs, :], self.ident[:]), reads=["b_all", "ident"], writes=[kT_])
                P.op("vector", lambda e, pT=pT, nq=nq: e.tensor_copy(out=bT[:, 0:nq], in_=pT[:, 0:nq]), reads=[kT_], writes=["bT"])
                P.dma("sync", self.brT_d[1][hd, :, q0:q0 + nq], bT[:, 0:nq], reads=["bT"], writes=["brT1"])
        if not ctx_out:
            pass

    def rwkv(self, l, blocks_out):
        P = self.P
        stage = self.dbg.get("rw_stage", 99)
        self.rwkv_proj(l)
        P.barrier()
        if stage <= 1:
            return
        for dc in range(DC):
            P.dma("sync", self.xT_d[:, dc * T:(dc + 1) * T], self.xT[:, dc, :], reads=["xT"], writes=["xT_d"])
        P.dma("sync", self.hT_d[:, :], self.hT[:].rearrange("p c t -> p (c t)"), reads=["hT"], writes=["hT_d"])
        P.barrier()
        if stage >= 3:
            self.rwkv_pass(l, 0, blocks_out)
            P.barrier()
        if stage >= 11:
            self.rwkv_pass(l, 1, blocks_out)
            P.barrier()
        for dc in range(DC):
            P.dma("sync", self.xT[:, dc, :], self.xT_d[:, dc * T:(dc + 1) * T], reads=["xT_d"], writes=["xT"])
        P.dma("sync", self.hT[:].rearrange("p c t -> p (c t)"), self.hT_d[:, :], reads=["hT_d"], writes=["hT"])
        P.barrier()

    def rwkv_proj(self, l):
        P = self.P
        hT = self.hT
        ab, af = self.arena_b, self.arena
        hsT = ab[:, 0:DC * T].rearrange("p (c t) -> p c t", c=DC)
        o = DC * T // 2
        W = [ab[:, 2 * o + i * 4096:2 * o + (i + 1) * 4096].rearrange("p (c n) -> p c n", c=DC) for i in range(2)]; o += 4096
        mu_rep = [af[:, o + i * 512:o + (i + 1) * 512] for i in range(2)]; o += 1024
        t1 = [af[:, o + i * 512:o + (i + 1) * 512] for i in range(2)]; o += 1024
        t2 = [af[:, o + i * 512:o + (i + 1) * 512] for i in range(2)]; o += 1024
        mucol = af[:, o:o + 4]; o += 4
        assert o <= 17408, o
        for (s0, n) in [(0, NCTX), (NCTX, NLAT)]:
            P.op("vector", lambda e, s0=s0, n=n: e.tensor_tensor(out=hsT[:, :, s0 + 1:s0 + n - 1], in0=hT[:, :, s0:s0 + n - 2], in1=hT[:, :, s0 + 2:s0 + n], op=ALU.add), reads=["hT"], writes=["hsT"])
            P.op("vector", lambda e, s0=s0: e.tensor_copy(out=hsT[:, :, s0:s0 + 1], in_=hT[:, :, s0 + 1:s0 + 2]), reads=["hT"], writes=["hsT"])
            P.op("vector", lambda e, s0=s0, n=n: e.tensor_copy(out=hsT[:, :, s0 + n - 1:s0 + n], in_=hT[:, :, s0 + n - 2:s0 + n - 1]), reads=["hT"], writes=["hsT"])
        P.op("scalar", lambda e: e.mul(out=hsT, in_=hsT, mul=0.5), reads=["hsT"], writes=["hsT"])
        for cb in range(6):
            w = W[cb % 2]; wk = ("W", cb % 2)
            P.dma("gpsimd", w, self.w_in[l, :, RW_OFF + cb * 512:RW_OFF + (cb + 1) * 512].rearrange("(c p) n -> p c n", p=128), writes=[wk])
            mr = mu_rep[cb % 2]; mk = ("mu", cb % 2)
            P.dma("sync", mr, self.rw_mu[l, cb * 512:(cb + 1) * 512].partition_broadcast(128), writes=[mk])
            for ti in range(T // 128):
                pp, pS = self.psum[(ti % 2) * 2], self.psum[(ti % 2) * 2 + 1]
                kp, kS = ("ps", (ti % 2) * 2), ("ps", (ti % 2) * 2 + 1)
                for (ps, pk, src, sk) in [(pp, kp, hT, "hT"), (pS, kS, hsT, "hsT")]:
                    for dc in range(DC):
                        P.op("tensor", lambda e, ps=ps, src=src, dc=dc, ti=ti, w=w: e.matmul(ps[:], lhsT=src[:, dc, ti * 128:(ti + 1) * 128], rhs=w[:, dc, :], start=(dc == 0), stop=(dc == DC - 1)),
                             reads=[sk, wk], writes=[pk])
                a, b_ = t1[ti % 2], t2[ti % 2]
                ka, kb = ("t1", ti % 2), ("t2", ti % 2)
                P.op("scalar", lambda e, a=a, pp=pp: e.copy(out=a, in_=pp[:]), reads=[kp], writes=[ka])
                P.op("vector", lambda e, a=a, b_=b_, pS=pS: e.tensor_tensor(out=b_, in0=pS[:], in1=a, op=ALU.subtract), reads=[kS, ka], writes=[kb])
                P.op("gpsimd", lambda e, b_=b_, mr=mr: e.tensor_tensor(out=b_, in0=b_, in1=mr, op=ALU.mult), reads=[kb, mk], writes=[kb])
                P.op("vector", lambda e, a=a, b_=b_: e.tensor_tensor(out=a, in0=a, in1=b_, op=ALU.add), reads=[ka, kb], writes=[ka])
                P.dma("sync", self.ztok_d[ti * 128:(ti + 1) * 128, cb * 512:(cb + 1) * 512], a, reads=[ka], writes=["ztok"])
        for ci, (c0, ncol) in enumerate([(3072, 128), (3200, 128), (3328, 128), (3456, 32)]):
            w = W[ci % 2]; wk = ("W", ci % 2)
            P.dma("gpsimd", w[:, :, 0:ncol], self.w_in[l, :, RW_OFF + c0:RW_OFF + c0 + ncol].rearrange("(c p) n -> p c n", p=128), writes=[wk])
            P.dma("sync", mucol[0:ncol, ci:ci + 1], self.rw_mu[l, c0:c0 + ncol].rearrange("(d o) -> d o", o=1), writes=[("mucol", ci)])
            func = [AF.Tanh, AF.Identity, AF.Sigmoid, AF.Sigmoid][ci]
            for bi in range(5):
                t0, n = TB[bi]
                pp, pS = self.psum[4 + (bi % 2) * 2], self.psum[5 + (bi % 2) * 2]
                kp, kS = ("ps", 4 + (bi % 2) * 2), ("ps", 5 + (bi % 2) * 2)
                for (ps, pk, src, sk) in [(pp, kp, hT, "hT"), (pS, kS, hsT, "hsT")]:
                    for dc in range(DC):
                        P.op("tensor", lambda e, ps=ps, src=src, dc=dc, t0=t0, n=n, w=w, ncol=ncol: e.matmul(ps[0:ncol, 0:n], lhsT=w[:, dc, 0:ncol], rhs=src[:, dc, t0:t0 + n], start=(dc == 0), stop=(dc == DC - 1)),
                             reads=[sk, wk], writes=[pk])
                a, b_ = t1[bi % 2], t2[bi % 2]
                ka, kb = ("t1", bi % 2), ("t2", bi % 2)
                P.op("scalar", lambda e, a=a, pp=pp, n=n, ncol=ncol: e.copy(out=a[0:ncol, 0:n], in_=pp[0:ncol, 0:n]), reads=[kp], writes=[ka])
                P.op("vector", lambda e, a=a, b_=b_, pS=pS, n=n, ncol=ncol: e.tensor_tensor(out=b_[0:ncol, 0:n], in0=pS[0:ncol, 0:n], in1=a[0:ncol, 0:n], op=ALU.subtract), reads=[kS, ka], writes=[kb])
                P.op("vector", lambda e, a=a, b_=b_, n=n, ncol=ncol, ci=ci: e.scalar_tensor_tensor(out=a[0:ncol, 0:n], in0=b_[0:ncol, 0:n], scalar=mucol[0:ncol, ci:ci + 1], in1=a[0:ncol, 0:n], op0=ALU.mult, op1=ALU.add),
                     reads=[ka, kb, ("mucol", ci)], writes=[ka])
                P.op("scalar", lambda e, a=a, n=n, ncol=ncol, func=func: e.activation(out=a[0:ncol, 0:n], in_=a[0:ncol, 0:n], func=func), reads=[ka], writes=[ka])
                P.dma("sync", self.smallT_d[ci, 0:ncol, t0:t0 + n], a[0:ncol, 0:n], reads=[ka], writes=["smallT"])

    def rwkv_pass(self, l, d, blocks_out):
        import math
        P = self.P
        C0 = math.exp(-0.5)
        ctx_out = 0 in blocks_out
        regions = [[self.arena, 0, 17408], [self.hT[:].rearrange("p c t -> p (c t)").bitcast(F32), 0, DC * T // 2], [self.xT[:].rearrange("p c t -> p (c t)"), 0, DC * T]]

        def F(n):
            for r in regions:
                if r[1] + n <= r[2]:
                    v = r[0][:, r[1]:r[1] + n]; r[1] += n
                    return v
            raise RuntimeError("rwkv scratch exhausted")
        H = F(512).rearrange("p (k i) -> p k i", k=8)
        kkp_rep = F(1024); ka_rep = F(1024); w0_rep = F(1024); a0_rep = F(1024)
        w2t = F(1024); a2t = F(1024)
        masks = F(1024).rearrange("p (m t) -> p m t", m=8)
        IU, IL, SU, SL, SUd, SLd, SUo, SLo = (masks[:, i, :] for i in range(8))
        INCL = IU if d == 0 else IL
        MS_st = SU if d == 0 else SL
        MI_st = INCL
        MXd_ts = SLd if d == 0 else SUd
        MXd_st = SUd if d == 0 else SLd
        MLo_ts = SLo if d == 0 else SUo
        ztile = F(3072); zr = ztile[:, 0:1024]; zk = ztile[:, 1024:2048]; zv = ztile[:, 2048:3072]
        tz_t = F(128); za_t = F(128)
        sgw = F(1024); alpha = F(1024); tbuf = F(1024); nkk = F(1024); kd = F(1024); bb = F(1024); Ep = F(1024)
        Em = alpha; Ex = tbuf; U = zk
        XT4 = [F(1024).rearrange("p (k t) -> p k t", k=8) for _ in range(4)]
        AtT, BtT, KtT, RtT = XT4
        big = [F(1024).rearrange("p (h t) -> p h t", h=8) for _ in range(12)]
        yt = F(1024)
        st = F(64)
        Gam = st[:, 0:8]
        if d == 1:
            a00_rep = F(1024); alpha0 = F(1024); kd0 = F(1024)
            lnw_rep = F(1024); lnb_rep = F(1024); rk_rep = F(1024)
            g2a = F(1024); g2b = F(1024)
            sgA = F(128); sgB = F(128)
            yf_t = alpha0
            c_tok = kd0[:, 0:512].bitcast(BF16)
            cT = F(512).bitcast(BF16).rearrange("p (c t) -> p c t", c=8)
            identb = F(64).bitcast(BF16)
        ident, ones = self.ident, self.ones
        if self.dbg.get("print_regions"):
            print("rwkv_pass d=%d region usage:" % d, [(r[1], r[2]) for r in regions])

        P.dma("sync", masks, self.masks_d.rearrange("m p t -> p m t"), writes=["masks"])
        P.dma("sync", kkp_rep, self.rw_kk[l].partition_broadcast(128), writes=["kkp_rep"])
        P.dma("sync", ka_rep, self.rw_ka[l].partition_broadcast(128), writes=["ka_rep"])
        P.dma("sync", w0_rep, self.rw_w0[l, d].partition_broadcast(128), writes=["w0_rep"])
        P.dma("sync", a0_rep, self.rw_a0[l, d].partition_broadcast(128), writes=["a0_rep"])
        P.dma("sync", w2t, self.rw_w2[l].rearrange("d r c -> (d r) c"), writes=["w2t"])
        P.dma("sync", a2t, self.rw_a2[l].rearrange("d r c -> (d r) c"), writes=["a2t"])
        P.op("vector", lambda e: e.memset(H, 0.0), writes=["H"])
        if d == 1:
            P.dma("sync", a00_rep, self.rw_a0[l, 0].partition_broadcast(128), writes=["a00_rep"])
            P.dma("sync", lnw_rep, self.rw_ln_w[l].partition_broadcast(128), writes=["lnw_rep"])
            P.dma("sync", lnb_rep, self.rw_ln_b[l].partition_broadcast(128), writes=["lnb_rep"])
            P.dma("sync", rk_rep, self.rw_rk[l].rearrange("h j -> (h j)").partition_broadcast(128), writes=["rk_rep"])
            P.dma("sync", g2a, self.rw_g2[l, 0:128, :], writes=["g2a"])
            P.dma("sync", g2b[0:32, :], self.rw_g2[l, 128:160, :], writes=["g2b"])
            P.op("vector", lambda e: e.tensor_copy(out=identb, in_=ident[:]), reads=["ident"], writes=["identb"])

        order = list(range(18)) if d == 0 else [1, 0] + list(range(17, 1, -1))
        F32R = mybir.dt.float32r
        r32 = (lambda a: a.bitcast(F32R)) if self.dbg.get("fp32r") else (lambda a: a)
        stage = self.dbg.get("rw_stage", 99)
        if stage < 10:
            order = order[:1]
        R2 = [slice(0, 64), slice(64, 128)]

        def lora(dst, src_t, wt, rep, dd, keyd, wkey, rkey, extra_w=(), skey="tzt"):
            for half in range(2):
                ps = self.psum[half]; pk = ("ps", half)
                P.op("tensor", lambda e, ps=ps, half=half: e.matmul(ps[:], lhsT=src_t[R2[dd], :], rhs=wt[R2[dd], half * 512:(half + 1) * 512], start=True, stop=True),
                     reads=[skey, wkey], writes=[pk])
                P.op("vector", lambda e, ps=ps, half=half: e.tensor_tensor(out=dst[:, half * 512:(half + 1) * 512], in0=ps[:], in1=rep[:, half * 512:(half + 1) * 512], op=ALU.add),
                     reads=[pk, rkey], writes=[(keyd, half)] + list(extra_w))
                P.op("scalar", lambda e, half=half: e.activation(out=dst[:, half * 512:(half + 1) * 512], in_=dst[:, half * 512:(half + 1) * 512], func=AF.Sigmoid),
                     reads=[(keyd, half)], writes=[(keyd, half)])

        def kdcalc(dst, al, keya, keyd):
            P.op("vector", lambda e: e.scalar_tensor_tensor(out=dst, in0=al, scalar=-1.0, in1=ka_rep, op0=ALU.add, op1=ALU.mult), reads=[keya, "ka_rep"], writes=[keyd] + (["c_tok"] if keyd == "kd0" else []))
            P.op("vector", lambda e: e.scalar_tensor_tensor(out=dst, in0=dst, scalar=1.0, in1=zk, op0=ALU.add, op1=ALU.mult), reads=[keyd, "ztile"], writes=[keyd])

        for n in order:
            tt = n * 128
            P.dma("sync", ztile, self.ztok_d[tt:tt + 128, :], reads=["ztok"], writes=["ztile", "U"])
            P.dma("sync", tz_t, self.smallT_d[0, :, tt:tt + 128], writes=["tzt"])
            P.dma("sync", za_t, self.smallT_d[1, :, tt:tt + 128], writes=["zat"])
            if d == 1:
                P.dma("sync", sgA, self.smallT_d[2, :, tt:tt + 128], writes=["sgA"])
                P.dma("sync", sgB[0:32, :], self.smallT_d[3, 0:32, tt:tt + 128], writes=["sgB"])
            lora(sgw, tz_t, w2t, w0_rep, d, "sgw", "w2t", "w0_rep")
            lora(alpha, za_t, a2t, a0_rep, d, "alpha", "a2t", "a0_rep", extra_w=["Em"], skey="zat")
            if d == 1:
                lora(alpha0, za_t, a2t, a00_rep, 0, "alpha0", "a2t", "a00_rep", skey="zat")
            P.op("vector", lambda e: e.tensor_tensor(out=tbuf, in0=zk, in1=kkp_rep, op=ALU.mult), reads=["ztile", "kkp_rep"], writes=["tbuf", "Ex"])
            P.op("scalar", lambda e: e.activation(out=nkk, in_=tbuf, func=AF.Square), reads=["tbuf"], writes=["nkk"])
            P.op("vector", lambda e: e.reduce_sum(out=st[:, 16:32], in_=nkk.rearrange("p (h j) -> p h j", h=16), axis=AX.X), reads=["nkk"], writes=["st"])
            P.op("scalar", lambda e: e.activation(out=st[:, 16:32], in_=st[:, 16:32], func=AF.Sqrt), reads=["st"], writes=["st"])
            P.op("vector", lambda e: e.tensor_scalar(out=st[:, 16:32], in0=st[:, 16:32], scalar1=1e-12, scalar2=None, op0=ALU.max), reads=["st"], writes=["st"])
            P.op("vector", lambda e: e.reciprocal(out=st[:, 16:32], in_=st[:, 16:32]), reads=["st"], writes=["st"])
            P.op("vector", lambda e: e.scalar_tensor_tensor(out=nkk.rearrange("p (h j) -> p h j", h=16), in0=tbuf.rearrange("p (h j) -> p h j", h=16), scalar=-1.0,
                                                             in1=st[:, 16:32].unsqueeze(2).broadcast_to([128, 16, 64]), op0=ALU.mult, op1=ALU.mult), reads=["tbuf", "st"], writes=["nkk"])
            kdcalc(kd, alpha, "alpha", "kd")
            if d == 1:
                kdcalc(kd0, alpha0, "alpha0", "kd0")
                P.op("gpsimd", lambda e: e.tensor_tensor(out=kd0, in0=kd0, in1=kd, op=ALU.add), reads=["kd0", "kd"], writes=["kd0"])
            P.op("vector", lambda e: e.scalar_tensor_tensor(out=bb, in0=nkk, scalar=-1.0, in1=alpha, op0=ALU.mult, op1=ALU.mult), reads=["nkk", "alpha"], writes=["bb"])
            if d == 1:
                P.dma("sync", yf_t, self.yf_d[tt:tt + 128, :], reads=["yf"], writes=["alpha0"])
            if stage <= 3:
                continue
            for pr in range(8):
                P.op("tensor", lambda e, pr=pr: e.matmul(self.psum[7][:, pr:pr + 1], lhsT=sgw[:, pr * 128:(pr + 1) * 128], rhs=ones[:, 0:1], start=True, stop=True), reads=["sgw", "ones"], writes=[("ps", 7)])
            P.op("scalar", lambda e: e.activation(out=Gam, in_=self.psum[7][:, 0:8], func=AF.Exp, scale=-C0), reads=[("ps", 7)], writes=["Gam"])
            for half in range(2):
                ps = self.psum[half]; pk = ("ps", half)
                hs = slice(half * 512, (half + 1) * 512)
                P.op("tensor", lambda e, ps=ps, hs=hs: e.matmul(ps[:], lhsT=INCL, rhs=sgw[:, hs], start=True, stop=True), reads=["masks", "sgw"], writes=[pk])
                P.op("scalar", lambda e, ps=ps, hs=hs: e.activation(out=Ep[:, hs], in_=ps[:], func=AF.Exp, scale=-C0), reads=[pk], writes=[("Ep", half)])
                P.op("vector", lambda e, ps=ps, hs=hs: e.tensor_tensor(out=Ex[:, hs], in0=ps[:], in1=sgw[:, hs], op=ALU.subtract), reads=[pk, "sgw", "tbuf", "nkk"], writes=[("Ex", half)])
                P.op("scalar", lambda e, ps=ps, hs=hs: e.activation(out=Em[:, hs], in_=ps[:], func=AF.Exp, scale=C0), reads=[pk, "alpha", "bb", "kd"], writes=[("Em", half)])
                P.op("scalar", lambda e, hs=hs: e.activation(out=Ex[:, hs], in_=Ex[:, hs], func=AF.Exp, scale=-C0), reads=[("Ex", half)], writes=[("Ex", half)])
            P.op("vector", lambda e: e.tensor_tensor(out=Ex, in0=Ex, in1=nkk, op=ALU.mult), reads=["Ex", "nkk"], writes=["Ex"])
            P.op("gpsimd", lambda e: e.tensor_tensor(out=bb, in0=bb, in1=Em, op=ALU.mult), reads=["bb", "Em"], writes=["bb"])
            P.op("gpsimd", lambda e: e.tensor_tensor(out=kd, in0=kd, in1=Em, op=ALU.mult), reads=["kd", "Em", "kd0"], writes=["kd"])
            P.op("vector", lambda e: e.tensor_tensor(out=Ep, in0=Ep, in1=zr, op=ALU.mult), reads=["Ep", "ztile"], writes=["Ep"])
            if stage <= 4:
                continue
            for xi, (src, sk) in enumerate([(Ex, "Ex"), (bb, "bb"), (kd, "kd"), (Ep, "Ep")]):
                for half in range(2):
                    bank = 2 + (xi * 2 + half) % 2
                    ps = self.psum[bank]; pk = ("ps", bank)
                    for j in range(4):
                        pr = half * 4 + j
                        P.op("tensor", lambda e, ps=ps, j=j, pr=pr, src=src: e.transpose(ps[:, j * 128:(j + 1) * 128], src[:, pr * 128:(pr + 1) * 128], ident[:]), reads=[sk, "ident"], writes=[pk])
                    dst = XT4[xi][:, half * 4:half * 4 + 4, :]
                    if half == 0:
                        P.op("vector", lambda e, ps=ps, dst=dst: e.tensor_copy(out=dst, in_=ps[:].rearrange("p (j t) -> p j t", j=4)), reads=[pk], writes=[("XT4", xi)])
                    else:
                        P.op("scalar", lambda e, ps=ps, dst=dst: e.copy(out=dst, in_=ps[:].rearrange("p (j t) -> p j t", j=4)), reads=[pk], writes=[("XT4", xi)])

            def pairmm(dst, lT, lk, rT, rk, mask, dk, banks, hg, dst2=None, mask2=None, dk2=None):
                for j in range(8):
                    h = hg * 8 + j
                    bank = banks[h % 2]
                    ps = self.psum[bank]
                    P.op("tensor", lambda e, ps=ps, j=j, h=h: e.matmul(ps[:, (j // 2) * 128:(j // 2 + 1) * 128], lhsT=r32(lT[R2[h % 2], h // 2, :]), rhs=r32(rT[R2[h % 2], h // 2, :]), start=True, stop=True),
                         reads=[("XT4", lk), ("XT4", rk)], writes=[("ps", bank)])
                for par in range(2):
                    bank = banks[par]
                    ps = self.psum[bank]
                    P.op("vector", lambda e, ps=ps, par=par: e.tensor_tensor(out=dst[:, par:8:2, :], in0=ps[:].rearrange("p (j t) -> p j t", j=4), in1=mask.unsqueeze(1).broadcast_to([128, 4, 128]), op=ALU.mult),
                         reads=[("ps", bank), "masks"], writes=[dk])
                    if dst2 is not None:
                        P.op("vector", lambda e, ps=ps, par=par: e.tensor_tensor(out=dst2[:, par:8:2, :], in0=ps[:].rearrange("p (j t) -> p j t", j=4), in1=mask2.unsqueeze(1).broadcast_to([128, 4, 128]), op=ALU.mult),
                             reads=[("ps", bank), "masks"], writes=[dk2])

            def headmm(dstbuf, dk, lbuf, lkey, rbuf, rkey, banks, mode, accbuf=None):
                for q in range(2):
                    bank = banks[q]
                    ps = self.psum[bank]
                    for jj in range(4):
                        j = q * 4 + jj
                        P.op("tensor", lambda e, ps=ps, jj=jj, j=j: e.matmul(ps[:, jj * 128:(jj + 1) * 128], lhsT=r32(lbuf[:, j, :]), rhs=r32(rbuf[:, j, :]), start=True, stop=True),
                             reads=[lkey, rkey], writes=[("ps", bank)])
                    dv = dstbuf[:, q * 4:q * 4 + 4, :]
                    pv = ps[:].rearrange("p (j t) -> p j t", j=4)
                    if mode == "copy":
                        if q == 0:
                            P.op("scalar", lambda e, dv=dv, pv=pv: e.copy(out=dv, in_=pv), reads=[("ps", bank)], writes=[(dk, q)])
                        else:
                            P.op("vector", lambda e, dv=dv, pv=pv: e.tensor_copy(out=dv, in_=pv), reads=[("ps", bank)], writes=[(dk, q)])
                    else:
                        P.op("vector", lambda e, dv=dv, pv=pv: e.tensor_tensor(out=dv, in0=pv, in1=dv, op=ALU.add), reads=[("ps", bank), (dk, q)], writes=[(dk, q)])

            Wsb, Vsb = Ep, Ex
            identb8 = ident[:].unsqueeze(1).broadcast_to([128, 8, 128])

            def solve_half(hg):
                hs = slice(hg * 512, (hg + 1) * 512)
                Xd, XTd, X2, XT2, PTm, Lo = big[hg * 6:(hg + 1) * 6]
                kX, kXT, kX2, kXT2, kPT, kLo = (f"b{hg}_{n}" for n in ("X", "XT", "X2", "XT2", "PT", "Lo"))
                Lk, kLk = X2, kX2
                bk = [2, 3] if hg == 0 else [6, 7]
                pairmm(XTd, BtT, 1, AtT, 0, MXd_st, kXT, bk, hg); yield
                pairmm(Xd, AtT, 0, BtT, 1, MXd_ts, kX, bk, hg); yield
                pairmm(Lo, AtT, 0, BtT, 1, MLo_ts, kLo, bk, hg); yield
                pairmm(Lk, KtT, 2, AtT, 0, MS_st, kLk, bk, hg); yield
                ps = self.psum[4 + hg]; pk = ("ps", 4 + hg)
                for j in range(8):
                    h = hg * 8 + j
                    P.op("tensor", lambda e, ps=ps, j=j, h=h: e.matmul(ps[:, j * 64:(j + 1) * 64], lhsT=AtT[R2[h % 2], h // 2, :], rhs=H[R2[h % 2], h // 2, :], start=True, stop=False),
                         reads=[("XT4", 0), "H"], writes=[pk])
                    P.op("tensor", lambda e, ps=ps, j=j, h=h, Lk=Lk: e.matmul(ps[:, j * 64:(j + 1) * 64], lhsT=Lk[:, j, :], rhs=zv[:, h * 64:(h + 1) * 64], start=False, stop=True),
                         reads=[kLk, "ztile"], writes=[pk])
                P.op("scalar", lambda e, ps=ps, hs=hs: e.copy(out=Wsb[:, hs], in_=ps[:]), reads=[pk, "Ep"], writes=[("Ep", hg)])
                yield
                P.op("gpsimd", lambda e, PTm=PTm, XTd=XTd: e.tensor_tensor(out=PTm, in0=XTd, in1=identb8, op=ALU.add), reads=[kXT, "ident"], writes=[kPT])
                cur = (Xd, kX, XTd, kXT)
                nxt = (X2, kX2, XT2, kXT2)
                for lev in range(3):
                    Xc, xk, XTc, xtk = cur
                    Xn, xnk, XTn, xtnk = nxt
                    headmm(Xn, xnk, XTc, xtk, Xc, xk, bk, "copy"); yield
                    if lev < 2:
                        headmm(XTn, xtnk, Xc, xk, XTc, xtk, bk, "copy"); yield
                    headmm(PTm, kPT, Xn, xnk, PTm, kPT, bk, "acc"); yield
                    cur, nxt = nxt, cur
                for j in range(8):
                    h = hg * 8 + j
                    P.op("tensor", lambda e, ps=ps, j=j, h=h, PTm=PTm: e.matmul(ps[:, j * 64:(j + 1) * 64], lhsT=PTm[:, j, :], rhs=Wsb[:, h * 64:(h + 1) * 64], start=True, stop=True),
                         reads=[kPT, ("Ep", hg)], writes=[pk])
                P.op("scalar", lambda e, ps=ps, hs=hs: e.copy(out=Vsb[:, hs], in_=ps[:]), reads=[pk, "Ex"], writes=[("Ex", hg)])
                yield
                MT = Xd
                headmm(MT, kX, Lo, kLo, PTm, kPT, bk, "copy"); yield
                for it in range(7):
                    src = Vsb if it == 0 else U
                    srck = ("Ex", hg) if it == 0 else ("U", hg)
                    for j in range(8):
                        h = hg * 8 + j
                        P.op("tensor", lambda e, ps=ps, j=j, h=h, src=src, MT=MT: e.matmul(ps[:, j * 64:(j + 1) * 64], lhsT=MT[:, j, :], rhs=src[:, h * 64:(h + 1) * 64], start=True, stop=True),
                             reads=[kX, srck], writes=[pk])
                    rd = [pk, ("Ex", hg), ("U", hg)] + (["tbuf", "kd", "kd0"] if it == 0 else [])
                    P.op("vector", lambda e, ps=ps, hs=hs: e.tensor_tensor(out=U[:, hs], in0=ps[:], in1=Vsb[:, hs], op=ALU.add), reads=rd, writes=[("U", hg)])
                    yield
                Mb, Mk = X2, XT2
                pairmm(Mb, BtT, 1, RtT, 3, MI_st, kX2, bk, hg); yield
                pairmm(Mk, KtT, 2, RtT, 3, MI_st, kXT2, bk, hg); yield
                for j in range(8):
                    h = hg * 8 + j
                    P.op("tensor", lambda e, ps=ps, j=j, h=h: e.matmul(ps[:, j * 64:(j + 1) * 64], lhsT=RtT[R2[h % 2], h // 2, :], rhs=H[R2[h % 2], h // 2, :], start=True, stop=False),
                         reads=[("XT4", 3), "H"], writes=[pk])
                    P.op("tensor", lambda e, ps=ps, j=j, h=h, Mb=Mb: e.matmul(ps[:, j * 64:(j + 1) * 64], lhsT=Mb[:, j, :], rhs=U[:, h * 64:(h + 1) * 64], start=False, stop=False),
                         reads=[kX2, ("U", hg)], writes=[pk])
                    P.op("tensor", lambda e, ps=ps, j=j, h=h, Mk=Mk: e.matmul(ps[:, j * 64:(j + 1) * 64], lhsT=Mk[:, j, :], rhs=zv[:, h * 64:(h + 1) * 64], start=False, stop=True),
                         reads=[kXT2, "ztile"], writes=[pk])
                if d == 0:
                    P.op("scalar", lambda e, ps=ps, hs=hs: e.copy(out=yt[:, hs], in_=ps[:]), reads=[pk], writes=[("yt", hg)])
                else:
                    P.op("vector", lambda e, ps=ps, hs=hs: e.tensor_tensor(out=yt[:, hs], in0=ps[:], in1=yf_t[:, hs], op=ALU.add), reads=[pk, "alpha0"], writes=[("yt", hg)])

            gens = [solve_half(0), solve_half(1)]
            while gens:
                for g in list(gens):
                    try:
                        next(g)
                    except StopIteration:
                        gens.remove(g)
            if d == 0:
                P.dma("sync", self.yf_d[tt:tt + 128, :], yt, reads=["yt"], writes=["yf"])
            for half in range(2):
                ps = self.psum[half]; pk = ("ps", half)
                for j in range(4):
                    pr = half * 4 + j
                    cs_ = slice(pr * 128, (pr + 1) * 128)
                    P.op("tensor", lambda e, ps=ps, j=j, cs_=cs_: e.matmul(ps[:, j * 128:(j + 1) * 128], lhsT=r32(bb[:, cs_]), rhs=r32(U[:, cs_]), start=True, stop=False), reads=["bb", "U"], writes=[pk])
                    P.op("tensor", lambda e, ps=ps, j=j, cs_=cs_: e.matmul(ps[:, j * 128:(j + 1) * 128], lhsT=r32(kd[:, cs_]), rhs=r32(zv[:, cs_]), start=False, stop=True), reads=["kd", "ztile"], writes=[pk])
                for hh in range(2):
                    P.op("vector", lambda e, ps=ps, hh=hh, half=half: e.tensor_tensor(out=H[R2[hh], half * 4:half * 4 + 4, :], in0=ps[R2[hh], :].rearrange("p (j c) -> p j c", j=4)[:, :, hh * 64:(hh + 1) * 64],
                                                                                  in1=H[R2[hh], half * 4:half * 4 + 4, :], op=ALU.add), reads=[pk, "H"], writes=["H"])
            P.op("vector", lambda e: e.tensor_tensor(out=H, in0=H, in1=Gam.unsqueeze(2).broadcast_to([128, 8, 64]), op=ALU.mult), reads=["H", "Gam"], writes=["H"])
            if d == 1 and (ctx_out or n >= 2):
                y3 = yt.rearrange("p (h i) -> p h i", h=16)
                P.op("vector", lambda e: e.reduce_sum(out=st[:, 32:48], in_=y3, axis=AX.X), reads=["yt"], writes=["st2"])
                P.op("scalar", lambda e: e.activation(out=Ep, in_=yt, func=AF.Square), reads=["yt", "Ep"], writes=["Ep"])
                P.op("vector", lambda e: e.reduce_sum(out=st[:, 48:64], in_=Ep.rearrange("p (h i) -> p h i", h=16), axis=AX.X), reads=["Ep"], writes=["st2"])
                P.op("vector", lambda e: e.tensor_scalar(out=st[:, 32:48], in0=st[:, 32:48], scalar1=1.0 / 64, scalar2=None, op0=ALU.mult), reads=["st2"], writes=["st2"])
                P.op("vector", lambda e: e.tensor_tensor(out=st[:, 0:16], in0=st[:, 32:48], in1=st[:, 32:48], op=ALU.mult), reads=["st2", "Gam"], writes=["Gam"])
                P.op("vector", lambda e: e.scalar_tensor_tensor(out=st[:, 48:64], in0=st[:, 48:64], scalar=1.0 / 64, in1=st[:, 0:16], op0=ALU.mult, op1=ALU.subtract), reads=["st2", "Gam"], writes=["st2"])
                P.op("vector", lambda e: e.tensor_scalar(out=st[:, 48:64], in0=st[:, 48:64], scalar1=64e-5, scalar2=None, op0=ALU.add), reads=["st2"], writes=["st2"])
                P.op("scalar", lambda e: e.activation(out=st[:, 48:64], in_=st[:, 48:64], func=AF.Sqrt), reads=["st2"], writes=["st2"])
                P.op("vector", lambda e: e.reciprocal(out=st[:, 48:64], in_=st[:, 48:64]), reads=["st2"], writes=["st2"])
                P.op("vector", lambda e: e.tensor_tensor(out=y3, in0=y3, in1=st[:, 32:48].unsqueeze(2).broadcast_to([128, 16, 64]), op=ALU.subtract), reads=["yt", "st2"], writes=["yt"])
                P.op("vector", lambda e: e.tensor_tensor(out=y3, in0=y3, in1=st[:, 48:64].unsqueeze(2).broadcast_to([128, 16, 64]), op=ALU.mult), reads=["yt", "st2"], writes=["yt"])
                P.op("gpsimd", lambda e: e.tensor_tensor(out=yt, in0=yt, in1=lnw_rep, op=ALU.mult), reads=["yt", "lnw_rep"], writes=["yt"])
                P.op("gpsimd", lambda e: e.tensor_tensor(out=yt, in0=yt, in1=lnb_rep, op=ALU.add), reads=["yt", "lnb_rep"], writes=["yt"])
                P.op("vector", lambda e: e.tensor_tensor(out=kd0, in0=kd0, in1=zr, op=ALU.mult), reads=["kd0", "ztile"], writes=["kd0"])
                P.op("vector", lambda e: e.tensor_tensor(out=kd0, in0=kd0, in1=rk_rep, op=ALU.mult), reads=["kd0", "rk_rep"], writes=["kd0"])
                P.op("vector", lambda e: e.reduce_sum(out=st[:, 32:48], in_=kd0.rearrange("p (h j) -> p h j", h=16), axis=AX.X), reads=["kd0", "yt"], writes=["st2"])
                P.op("vector", lambda e: e.tensor_tensor(out=kd0.rearrange("p (h j) -> p h j", h=16), in0=zv.rearrange("p (h j) -> p h j", h=16), in1=st[:, 32:48].unsqueeze(2).broadcast_to([128, 16, 64]), op=ALU.mult),
                     reads=["ztile", "st2"], writes=["kd0"])
                P.op("vector", lambda e: e.tensor_tensor(out=yt, in0=yt, in1=kd0, op=ALU.add), reads=["yt", "kd0"], writes=["yt"])
                for half in range(2):
                    ps = self.psum[half]; pk = ("ps", half)
                    hs = slice(half * 512, (half + 1) * 512)
                    P.op("tensor", lambda e, ps=ps, hs=hs: e.matmul(ps[:], lhsT=sgA, rhs=g2a[:, hs], start=True, stop=False), reads=["sgA", "g2a"], writes=[pk])
                    P.op("tensor", lambda e, ps=ps, hs=hs: e.matmul(ps[:], lhsT=sgB[0:32, :], rhs=g2b[0:32, hs], start=False, stop=True), reads=["sgB", "g2b"], writes=[pk])
                    P.op("vector", lambda e, ps=ps, hs=hs: e.tensor_tensor(out=c_tok[:, hs], in0=ps[:], in1=yt[:, hs], op=ALU.mult), reads=[pk, "yt"], writes=[("c_tok", half), "kd0"])
                for half in range(2):
                    ps = self.psum[2 + half]; pk = ("ps", 2 + half)
                    for j in range(4):
                        fc = half * 4 + j
                        P.op("tensor", lambda e, ps=ps, j=j, fc=fc: e.matmul(ps[:, j * 128:(j + 1) * 128], lhsT=c_tok[:, fc * 128:(fc + 1) * 128], rhs=identb, start=True, stop=True), reads=["c_tok", "identb"], writes=[pk])
                    P.op("scalar", lambda e, ps=ps, half=half: e.copy(out=cT[:, half * 4:half * 4 + 4, :], in_=ps[:].rearrange("p (j t) -> p j t", j=4)), reads=[pk], writes=[("cT", half)])
                P.dma("sync", self.brT_d[2][:, :, tt:tt + 128].rearrange("c p t -> p c t"), cT, reads=["cT"], writes=["brT2"])

    def merge(self, l, blocks_out):
        P = self.P
        hT, xT, lv = self.hT, self.xT, self.lv
        ab, af = self.arena_b, self.arena
        brb = [ab[:, i * 4096:(i + 1) * 4096].rearrange("p (c t) -> p c t", c=DC) for i in range(3)]
        mT = ab[:, 12288:16384].rearrange("p (c t) -> p c t", c=DC)
        NW = 14
        wt = [ab[:, 16384 + i * 1024:16384 + (i + 1) * 1024].rearrange("p (c n) -> p c n", c=DC) for i in range(NW)]
        o = (16384 + NW * 1024) // 2
        sig = [af[:, o + i * 512:o + (i + 1) * 512] for i in range(3)]; o += 1536
        macc = af[:, o:o + 512]; o += 512
        assert o <= 17408
        wi = [0]

        def wload(src):
            i = wi[0] % NW
            wi[0] += 1
            P.dma("gpsimd", wt[i], src.rearrange("(c p) n -> p c n", p=128), writes=[("wt", i)])
            return wt[i], ("wt", i)

        for bi in blocks_out:
            t0, n = TB[bi]
            who = 1 if bi == 0 else 0
            for i in range(3):
                P.dma("sync", brb[i][:, :, 0:n], self.brT_d[i][:, :, t0:t0 + n].rearrange("c p t -> p c t"), reads=[f"brT{i}"], writes=[("brb", i)])
            for j in range(DC):
                for i in range(3):
                    wb, wbk = wload(self.w_br[i][l, :, j * 128:(j + 1) * 128])
                    wg, wgk = wload(self.w_in[l, :, GT_OFF + i * D + j * 128:GT_OFF + i * D + (j + 1) * 128])
                    pb, pg = self.psum[2 * i], self.psum[2 * i + 1]
                    kb, kg = ("ps", 2 * i), ("ps", 2 * i + 1)
                    for k in range(DC):
                        P.op("tensor", lambda e, pb=pb, wb=wb, k=k, i=i, n=n: e.matmul(pb[:, 0:n], lhsT=wb[:, k, :], rhs=brb[i][:, k, 0:n], start=(k == 0), stop=(k == DC - 1)),
                             reads=[wbk, ("brb", i)], writes=[kb])
                    for k in range(DC):
                        P.op("tensor", lambda e, pg=pg, wg=wg, k=k, t0=t0, n=n: e.matmul(pg[:, 0:n], lhsT=wg[:, k, :], rhs=hT[:, k, t0:t0 + n], start=(k == 0), stop=(k == DC - 1)),
                             reads=[wgk, ("hT", bi)], writes=[kg])
                    P.op("scalar", lambda e, pg=pg, i=i, n=n: e.activation(out=sig[i][:, 0:n], in_=pg[:, 0:n], func=AF.Sigmoid), reads=[kg], writes=[("sig", i)])
                    if i == 0:
                        P.op("vector", lambda e, pb=pb, n=n: e.tensor_tensor(out=macc[:, 0:n], in0=pb[:, 0:n], in1=sig[0][:, 0:n], op=ALU.mult),
                             reads=[kb, ("sig", 0)], writes=["macc"])
                    else:
                        P.op("vector", lambda e, pb=pb, i=i, n=n: e.tensor_tensor(out=sig[i][:, 0:n], in0=pb[:, 0:n], in1=sig[i][:, 0:n], op=ALU.mult),
                             reads=[kb, ("sig", i)], writes=[("sig", i)])
                        if i == 1:
                            P.op("vector", lambda e, n=n: e.tensor_tensor(out=macc[:, 0:n], in0=macc[:, 0:n], in1=sig[1][:, 0:n], op=ALU.add),
                                 reads=["macc", ("sig", 1)], writes=["macc"])
                        else:
                            P.op("vector", lambda e, n=n, j=j: e.tensor_tensor(out=mT[:, j, 0:n], in0=macc[:, 0:n], in1=sig[2][:, 0:n], op=ALU.add),
                                 reads=["macc", ("sig", 2)], writes=[("mT", j)])
            for j in range(DC):
                wo, wok = wload(self.w_o[l, :, j * 128:(j + 1) * 128])
                ps = self.psum[6 + j % 2]
                pk = ("ps", 6 + j % 2)
                for k in range(DC):
                    P.op("tensor", lambda e, ps=ps, wo=wo, k=k, n=n: e.matmul(ps[:, 0:n], lhsT=wo[:, k, :], rhs=mT[:, k, 0:n], start=(k == 0), stop=(k == DC - 1)),
                         reads=[wok, ("mT", k)], writes=[pk])
                P.op("vector", lambda e, ps=ps, j=j, t0=t0, n=n, who=who: e.scalar_tensor_tensor(
                    out=xT[:, j, t0:t0 + n], in0=ps[:, 0:n], scalar=lv[:, 2, j, who:who + 1], in1=xT[:, j, t0:t0 + n], op0=ALU.mult, op1=ALU.add),
                    reads=[pk, "lv", ("xT", bi)], writes=[("xT", bi)])


def m_names_ok(inputs):
    return inputs.keys()


def make_inputs_for_core(inputs, b, consts):
    m = {}
    for k, v in inputs.items():
        v = np.asarray(v)
        if k not in m_names_ok(inputs):
            continue
        if k in ("x", "c", "ctx"):
            m[k] = np.ascontiguousarray(v[b])
        else:
            m[k] = np.ascontiguousarray(v)
    m.update(consts)
    return m


_CACHE = {}


def kernel(**inputs):
    if "nc" not in _CACHE:
        m = Model()
        _CACHE["nc"] = m.build()
        _CACHE["names"] = list(m.in_names)
    nc = _CACHE["nc"]
    consts = host_consts()
    in_maps = []
    for b in range(8):
        d = {}
        for k in _CACHE["names"]:
            if k in consts:
                d[k] = consts[k]
            elif k in ("x", "c", "ctx"):
                d[k] = np.ascontiguousarray(np.asarray(inputs[k])[b], dtype=np.float32)
            else:
                d[k] = np.ascontiguousarray(np.asarray(inputs[k]), dtype=np.float32)
        in_maps.append(d)
    res = run_bass_kernel_spmd(nc, in_maps, core_ids=list(range(8)))
    return np.stack([np.asarray(r["out"], dtype=np.float32) for r in res.results], axis=0)
```

```python
import numpy as np
import concourse.bass as bass
import concourse.mybir as mybir
from concourse.bass_utils import run_bass_kernel_spmd
from contextlib import ExitStack

F32 = mybir.dt.float32
BF16 = mybir.dt.bfloat16
AF = mybir.ActivationFunctionType
ALU = mybir.AluOpType
AX = mybir.AxisListType

ENGS = ["tensor", "vector", "scalar", "gpsimd", "sync"]
DMA_QUEUES = ["sync", "scalar", "gpsimd"]
EPOCH = 8000
NPOOL = 6


class Prog:
    def __init__(self, nc, stack):
        self.nc = nc
        self.stack = stack
        self.q = {e: [] for e in ENGS}
        self.sems = {}
        self.tick = {e: 0 for e in ENGS}
        self.waited = {e: {} for e in ENGS}
        self.lastw = {}
        self.readers = {}
        self.dpool = {}
        self.dpool_next = {q: 0 for q in DMA_QUEUES}
        for q in DMA_QUEUES:
            self.dpool[q] = []
            for k in range(NPOOL):
                s = stack.enter_context(nc.semaphore(f"d_{q}_{k}"))
                self.dpool[q].append([s, 0])
        self.n_instr = 0

    def sb(self, name, shape, dt):
        return self.stack.enter_context(self.nc.sbuf_tensor(name, list(shape), dt))

    def ps(self, name, shape, dt=F32):
        return self.stack.enter_context(self.nc.psum_tensor(name, list(shape), dt))

    def _esem(self, e, epoch):
        key = (e, epoch)
        if key not in self.sems:
            self.sems[key] = self.stack.enter_context(self.nc.semaphore(f"s_{e}_{epoch}"))
        return self.sems[key]

    def _deps(self, reads, writes):
        toks = []
        for (n, sl) in reads:
            d = self.lastw.get(n)
            if d:
                for k, t in d.items():
                    if sl is None or k is None or k == sl:
                        toks.append(t)
        for (n, sl) in writes:
            d = self.lastw.get(n)
            if d:
                for k, t in d.items():
                    if sl is None or k is None or k == sl:
                        toks.append(t)
            d = self.readers.get(n)
            if d:
                for k, dd in d.items():
                    if sl is None or k is None or k == sl:
                        toks.extend(dd.values())
        return toks

    def _record(self, reads, writes, tok):
        skey = tok[0]
        for (n, sl) in writes:
            d = self.lastw.setdefault(n, {})
            r = self.readers.setdefault(n, {})
            if sl is None:
                d.clear()
                r.clear()
            else:
                r.pop(sl, None)
            d[sl] = tok
        for (n, sl) in reads:
            dd = self.readers.setdefault(n, {}).setdefault(sl, {})
            dd[skey] = tok

    def _waits(self, eng, toks):
        need = {}
        for (skey, sem, val) in toks:
            if self.waited[eng].get(skey, -1) >= val:
                continue
            if skey not in need or need[skey][1] < val:
                need[skey] = (sem, val)
        out = []
        for skey, (sem, val) in need.items():
            self.waited[eng][skey] = val
            out.append((sem, val))
        return out

    def _k(self, x):
        if isinstance(x, tuple):
            return x if len(x) == 2 else (x[0], tuple(x[1:]))
        return (x, None)

    def op(self, eng, fn, reads=(), writes=(), pe_acc=False):
        reads = [self._k(r) for r in reads]
        writes = [self._k(w) for w in writes]
        if eng != "tensor":
            writes = writes + [r for r in reads if r[0] == "ps" and r not in writes]
        toks = self._deps(reads, writes)
        if eng == "tensor":
            toks = [t for t in toks if t[0][0] != "tensor"]
        waits = self._waits(eng, toks)
        self.tick[eng] += 1
        t = self.tick[eng]
        epoch, val = divmod(t, EPOCH)
        if val == 0:
            epoch, val = epoch - 1, EPOCH
        sem = self._esem(eng, epoch)
        tok = ((eng, epoch), sem, val)
        self._record(reads, writes, tok)
        self.n_instr += 1

        def emit(e, waits=waits, fn=fn, sem=sem):
            for (s, v) in waits:
                e.wait_ge(s, v)
            fn(e).then_inc(sem, 1)
        self.q[eng].append(emit)
        return tok

    def dma(self, queue, out, in_, reads=(), writes=(), **kw):
        reads = [self._k(r) for r in reads]
        writes = [self._k(w) for w in writes]
        toks = self._deps(reads, writes)
        i = self.dpool_next[queue]
        self.dpool_next[queue] = (i + 1) % NPOOL
        slot = self.dpool[queue][i]
        sem, prev = slot
        skey = ("dma", queue, i)
        if prev > 0:
            toks.append((skey, sem, prev))
        waits = self._waits(queue, toks)
        val = prev + 16
        slot[1] = val
        tok = (skey, sem, val)
        self._record(reads, writes, tok)
        self.n_instr += 1

        def emit(e, waits=waits, sem=sem):
            for (s, v) in waits:
                e.wait_ge(s, v)
            e.dma_start(out=out, in_=in_, **kw).then_inc(sem, 16)
        self.q[queue].append(emit)
        return tok

    def wait_all(self, eng, keys):
        toks = []
        toks = self._deps([self._k(k) for k in keys], [])
        waits = self._waits(eng, toks)

        def emit(e, waits=waits):
            for (s, v) in waits:
                e.wait_ge(s, v)
        self.q[eng].append(emit)

    def barrier(self):
        toks = []
        for e in ENGS:
            t = self.tick[e]
            if t == 0:
                continue
            epoch, val = divmod(t, EPOCH)
            if val == 0:
                epoch, val = epoch - 1, EPOCH
            toks.append(((e, epoch), self._esem(e, epoch), val))
        for q in DMA_QUEUES:
            for i, (sem, val) in enumerate(self.dpool[q]):
                if val > 0:
                    toks.append((("dma", q, i), sem, val))
        for e in ENGS:
            mine = [t for t in toks if not (t[0][0] == e and e == "tensor")]
            waits = self._waits(e, mine)

            def emit(eng, waits=waits):
                for (s, v) in waits:
                    eng.wait_ge(s, v)
            self.q[e].append(emit)
        self.lastw.clear()
        self.readers.clear()

    def finish(self):
        nc = self.nc
        with nc.Block() as block:
            @block.tensor
            def _(e):
                for f in self.q["tensor"]:
                    f(e)

            @block.vector
            def _(e):
                for f in self.q["vector"]:
                    f(e)

            @block.scalar
            def _(e):
                for f in self.q["scalar"]:
                    f(e)

            @block.gpsimd
            def _(e):
                for f in self.q["gpsimd"]:
                    f(e)

            @block.sync
            def _(e):
                for f in self.q["sync"]:
                    f(e)


D = 1024
DC = 8
NCTX = 256
NLAT = 2048
T = NCTX + NLAT
DFF = 2816
FC = DFF // 128
PTOT = 11680
GM_OFF, DA_OFF, RW_OFF, GT_OFF = 0, 2048, 5120, 8608
TB = [(0, 256), (256, 512), (768, 512), (1280, 512), (1792, 512)]
RMS_EPS = 1e-6
NL = 4


def host_consts():
    c = {}
    c["ident"] = np.eye(128, dtype=np.float32)
    c["ones"] = np.ones((128, 128), dtype=np.float32)
    bd = np.zeros((128, 128), np.float32); bd[:64, :64] = 1; bd[64:, 64:] = 1
    c["bd64"] = bd
    rp = np.zeros((128, 128), np.float32)
    for m in range(128):
        if (m % 64) < 32:
            rp[m + 32, m] = -1.0
        else:
            rp[m - 32, m] = 1.0
    c["rperm"] = rp
    t = np.arange(NLAT)
    row = (t // 64).astype(np.float32); col = (t % 64).astype(np.float32)
    inv = (10000.0 ** (-np.arange(0, 32, 2, dtype=np.float32) / 32)).astype(np.float32)
    ang = np.concatenate([row[:, None] * inv, col[:, None] * inv], -1)
    ang = np.concatenate([ang, ang], -1)
    cosT = np.ones((128, T), np.float32); sinT = np.zeros((128, T), np.float32)
    cosT[:, NCTX:] = np.tile(np.cos(ang).T, (2, 1)); sinT[:, NCTX:] = np.tile(np.sin(ang).T, (2, 1))
    c["cosT"] = cosT.astype(np.float32); c["sinT"] = sinT.astype(np.float32)
    i = np.arange(128)
    IU_, IL_, SU_, SL_ = (i[:, None] <= i[None, :]), (i[:, None] >= i[None, :]), (i[:, None] < i[None, :]), (i[:, None] > i[None, :])
    BD_ = (i[:, None] // 16) == (i[None, :] // 16)
    c["masks"] = np.stack([IU_, IL_, SU_, SL_, SU_ & BD_, SL_ & BD_, SU_ & ~BD_, SL_ & ~BD_]).astype(np.float32)
    return c


class Model:
    def __init__(self, nlayers=NL, do_mix=True, do_ffn=True, dbg=None, nl_w=NL, branches=(1, 1, 1)):
        self.nlayers = nlayers
        self.nl_w = nl_w
        self.branches = branches
        self.do_mix = do_mix
        self.do_ffn = do_ffn
        self.dbg = dbg or {}

    def declare(self, nc):
        L = self.nl_w
        self.in_names = []

        def I(n, s):
            self.in_names.append(n)
            return nc.dram_tensor(n, list(s), F32, kind="ExternalInput").ap()
        self.x = I("x", [NLAT, D]); self.c = I("c", [D]); self.ctx = I("ctx", [NCTX, D]); self.c_ctx = I("c_ctx", [D])
        self.ada_w = I("ada_w", [L, D, 6 * D]); self.ada_b = I("ada_b", [L, 6 * D])
        self.norm1_g = I("norm1_g", [L, D]); self.norm2_g = I("norm2_g", [L, D])
        self.w_in = I("w_in", [L, D, PTOT])
        self.ffn_wi = I("ffn_wi", [L, D, 2 * DFF]); self.ffn_wo = I("ffn_wo", [L, DFF, D])
        self.gm_v_g = I("gm_v_g", [L, D]); self.gm_ws = I("gm_ws", [L, 8, 128, 128]); self.gm_bs = I("gm_bs", [L, 8, 128])
        self.w_br = [I("w_br_a", [L, D, D]), I("w_br_b", [L, D, D]), I("w_br_c", [L, D, D])]
        self.w_o = I("w_o", [L, D, D])
        self.da_q_g = I("da_q_g", [L, 64]); self.da_k_g = I("da_k_g", [L, 64]); self.da_lambda = I("da_lambda", [L, 4, 64]); self.da_subln_g = I("da_subln_g", [L, 128])
        self.rw_mu = I("rw_mu", [L, 3488]); self.rw_w0 = I("rw_w0", [L, 2, D]); self.rw_w2 = I("rw_w2", [L, 2, 64, D]); self.rw_a0 = I("rw_a0", [L, 2, D]); self.rw_a2 = I("rw_a2", [L, 2, 64, D])
        self.rw_g2 = I("rw_g2", [L, 160, D]); self.rw_kk = I("rw_kk", [L, D]); self.rw_ka = I("rw_ka", [L, D]); self.rw_rk = I("rw_rk", [L, 16, 64]); self.rw_ln_w = I("rw_ln_w", [L, D]); self.rw_ln_b = I("rw_ln_b", [L, D])
        self.masks_d = I("masks", [8, 128, 128])
        S = lambda n, sh, dt=F32: nc.dram_tensor(n, list(sh), dt, kind="Internal").ap()
        SD = (lambda n, sh: nc.dram_tensor(n, list(sh), F32, kind="ExternalOutput").ap()) if self.dbg.get("dump_yf") else S
        self.ztok_d = SD("ztok", [T, 3072]); self.smallT_d = S("smallT", [4, 128, T]); self.yf_d = SD("yf", [T, D]); self.xT_d = S("xTspill", [128, DC * T]); self.hT_d = S("hTspill", [128, DC * T], BF16)
        self.ident_d = I("ident", [128, 128]); self.ones_d = I("ones", [128, 128])
        self.bd64_d = I("bd64", [128, 128]); self.rperm_d = I("rperm", [128, 128]); self.cosT_d = I("cosT", [128, T]); self.sinT_d = I("sinT", [128, T])
        self.brT_d = [nc.dram_tensor(f"brT{i}", [DC, 128, T], BF16, kind="Internal").ap() for i in range(3)]
        self.out = nc.dram_tensor("out", [NLAT, D], F32, kind="ExternalOutput").ap()
        if self.dbg.get("dump_ctx"):
            self.outc = nc.dram_tensor("outc", [NCTX, D], F32, kind="ExternalOutput").ap()

    def build(self):
        nc = bass.Bass("TRN2", target_bir_lowering=False)
        self.nc = nc
        self.declare(nc)
        with ExitStack() as st:
            P = Prog(nc, st)
            self.P = P
            self.alloc()
            self.setup()
            for l in range(self.nlayers):
                self.layer(l)
            self.final()
            P.finish()
        return nc

    def alloc(self):
        P = self.P
        self.xT = P.sb("xT", [128, DC, T], F32)
        self.hT = P.sb("hT", [128, DC, T], BF16)
        self.ident = P.sb("identt", [128, 128], F32)
        self.ones = P.sb("onest", [128, 128], F32)
        self.vec = P.sb("vec", [128, 64], F32)
        self.modv = P.sb("modv", [128, 6, DC, 2], F32)
        self.lv = P.sb("lv", [128, 6, DC, 2], F32)
        self.cs = P.sb("cs", [128, DC, 2], F32)
        self.rstd = P.sb("rstd", [128, 512], F32)
        self.ntmp = [P.sb(f"ntmp{i}", [128, 512], F32) for i in range(2)]
        self.arena = P.sb("arena", [128, 17408], F32)
        self.arena_b = self.arena[:].bitcast(BF16)
        self.psum = [P.ps(f"ps{i}", [128, 512]) for i in range(8)]

    def setup(self):
        P = self.P
        xT, ident = self.xT, self.ident
        P.dma("sync", ident[:], self.ident_d[:, :], writes=["ident"])
        P.dma("sync", self.ones[:], self.ones_d[:, :], writes=["ones"])
        P.dma("sync", self.cs[:, :, 0], self.c.rearrange("(c p) -> p c", p=128), writes=["cs"], allow_slow_non_contiguous=True)
        P.dma("sync", self.cs[:, :, 1], self.c_ctx.rearrange("(c p) -> p c", p=128), writes=["cs"], allow_slow_non_contiguous=True)
        P.op("scalar", lambda e: e.activation(out=self.cs[:], in_=self.cs[:], func=AF.Silu), reads=["cs"], writes=["cs"])
        stage = [self.arena[:, i * 1024:(i + 1) * 1024] for i in range(2)]
        for ti in range(T // 128):
            src = self.ctx[ti * 128:(ti + 1) * 128, :] if ti < 2 else self.x[(ti - 2) * 128:(ti - 1) * 128, :]
            sg = stage[ti % 2]
            P.dma("sync", sg, src, writes=[("stage", ti % 2)])
            for half in range(2):
                ps = self.psum[(ti * 2 + half) % 8]
                pk = ("ps", (ti * 2 + half) % 8)
                for j in range(4):
                    dc = half * 4 + j
                    P.op("tensor", lambda e, ps=ps, sg=sg, dc=dc, j=j: e.transpose(ps[:, j * 128:(j + 1) * 128], sg[:, dc * 128:(dc + 1) * 128], ident[:]),
                         reads=[("stage", ti % 2), "ident"], writes=[pk])
                eng = "vector" if half == 0 else "scalar"
                dst = xT[:, half * 4:half * 4 + 4, ti * 128:(ti + 1) * 128]
                srcp = ps[:].rearrange("p (j t) -> p j t", j=4)
                if eng == "vector":
                    P.op("vector", lambda e, dst=dst, srcp=srcp: e.tensor_copy(out=dst, in_=srcp), reads=[pk], writes=[("xT", ti)])
                else:
                    P.op("scalar", lambda e, dst=dst, srcp=srcp: e.copy(out=dst, in_=srcp), reads=[pk], writes=[("xT", ti)])
        P.barrier()

    def final(self):
        P = self.P
        xT, ident = self.xT, self.ident
        stage = [self.arena[:, i * 1024:(i + 1) * 1024] for i in range(2)]
        for ti in range(0 if self.dbg.get("dump_ctx") else 2, T // 128):
            sg = stage[ti % 2]
            for half in range(2):
                ps = self.psum[(ti * 2 + half) % 8]
                pk = ("ps", (ti * 2 + half) % 8)
                for j in range(4):
                    dc = half * 4 + j
                    P.op("tensor", lambda e, ps=ps, dc=dc, j=j, ti=ti: e.transpose(ps[:, j * 128:(j + 1) * 128], xT[:, dc, ti * 128:(ti + 1) * 128], ident[:]),
                         reads=["xT", "ident"], writes=[pk])
                dst = sg[:, half * 512:(half + 1) * 512]
                if half == 0:
                    P.op("vector", lambda e, dst=dst, ps=ps: e.tensor_copy(out=dst, in_=ps[:]), reads=[pk], writes=[("stage", ti % 2)])
                else:
                    P.op("scalar", lambda e, dst=dst, ps=ps: e.copy(out=dst, in_=ps[:]), reads=[pk], writes=[("stage", ti % 2)])
            dst = self.outc[ti * 128:(ti + 1) * 128, :] if ti < 2 else self.out[(ti - 2) * 128:(ti - 1) * 128, :]
            P.dma("sync", dst, sg, reads=[("stage", ti % 2)], writes=["out"])
        P.wait_all("sync", ["out"])

    def load_vec(self, dst, src_1d, key):
        self.P.dma("sync", dst, src_1d.rearrange("(c p) -> p c", p=128), writes=[key], allow_slow_non_contiguous=True)

    def ada(self, l):
        P = self.P
        NB = 256
        wt = [self.arena[:, i * DC * NB:(i + 1) * DC * NB].rearrange("p (c n) -> p c n", c=DC) for i in range(2)]
        brow = self.arena[0:1, 2 * DC * NB:2 * DC * NB + 6 * D]
        P.dma("sync", brow, self.ada_b[l:l + 1, :], writes=["brow"])
        ps = self.psum[0]
        for blk in range(6 * D // NB):
            w = wt[blk % 2]
            wk = ("adaw", blk % 2)
            P.dma("sync", w, self.ada_w[l, :, blk * NB:(blk + 1) * NB].rearrange("(c p) n -> p c n", p=128), writes=[wk])
            for s in range(NB // 128):
                fch = blk * (NB // 128) + s
                o = ps[:, fch * 2:fch * 2 + 2]
                for dc in range(DC):
                    P.op("tensor", lambda e, o=o, w=w, s=s, dc=dc: e.matmul(o, lhsT=w[:, dc, s * 128:(s + 1) * 128], rhs=self.cs[:, dc, :], start=(dc == 0), stop=False),
                         reads=[wk, "cs"], writes=[("ps", 0)])
                P.op("tensor", lambda e, o=o, fch=fch: e.matmul(o, lhsT=brow[:, fch * 128:(fch + 1) * 128], rhs=self.ones[0:1, 0:2], start=False, stop=True),
                     reads=["brow", "ones"], writes=[("ps", 0)])
        P.op("vector", lambda e: e.tensor_copy(out=self.modv[:].rearrange("p k j w -> p (k j w)"), in_=ps[:, 0:96]), reads=[("ps", 0)], writes=["modv"])
        vec = self.vec
        self.load_vec(vec[:, 0:8], self.norm1_g[l], "vec")
        self.load_vec(vec[:, 8:16], self.norm2_g[l], "vec")
        lv, modv = self.lv, self.modv
        for (dst, sc, g0) in [(0, 1, 0), (3, 4, 8)]:
            P.op("vector", lambda e, dst=dst, sc=sc, g0=g0: e.scalar_tensor_tensor(
                out=lv[:, dst], in0=modv[:, sc], scalar=1.0, in1=vec[:, g0:g0 + 8].unsqueeze(2).broadcast_to([128, 8, 2]),
                op0=ALU.add, op1=ALU.mult), reads=["modv", "vec"], writes=["lv"])
        for (dst, src) in [(1, 0), (2, 2), (4, 3), (5, 5)]:
            P.op("vector", lambda e, dst=dst, src=src: e.tensor_copy(out=lv[:, dst], in_=modv[:, src]), reads=["modv"], writes=["lv"])
        P.barrier()

    def norm_mod(self, which, blocks):
        P = self.P
        xT, hT, lv = self.xT, self.hT, self.lv
        for bi in blocks:
            t0, n = TB[bi]
            who = 1 if bi == 0 else 0
            ps = self.psum[bi % 2]
            pk = ("ps", bi % 2)
            for dc in range(DC):
                tmp = self.ntmp[dc % 2]
                P.op("scalar", lambda e, tmp=tmp, dc=dc, t0=t0, n=n: e.activation(out=tmp[:, 0:n], in_=xT[:, dc, t0:t0 + n], func=AF.Square),
                     reads=[("xT", bi)], writes=[("ntmp", dc % 2)])
                P.op("tensor", lambda e, tmp=tmp, dc=dc, n=n, ps=ps: e.matmul(ps[:, 0:n], lhsT=self.ones[:], rhs=tmp[:, 0:n], start=(dc == 0), stop=(dc == DC - 1)),
                     reads=[("ntmp", dc % 2), "ones"], writes=[pk])
            rstd = self.rstd
            P.op("vector", lambda e, n=n, ps=ps: e.tensor_scalar(out=rstd[:, 0:n], in0=ps[:, 0:n], scalar1=1.0 / D, scalar2=RMS_EPS, op0=ALU.mult, op1=ALU.add),
                 reads=[pk], writes=["rstd"])
            P.op("scalar", lambda e, n=n: e.activation(out=rstd[:, 0:n], in_=rstd[:, 0:n], func=AF.Sqrt), reads=["rstd"], writes=["rstd"])
            P.op("vector", lambda e, n=n: e.reciprocal(out=rstd[:, 0:n], in_=rstd[:, 0:n]), reads=["rstd"], writes=["rstd"])
            for dc in range(DC):
                tmp = self.ntmp[dc % 2]
                P.op("vector", lambda e, tmp=tmp, dc=dc, t0=t0, n=n: e.tensor_tensor(out=tmp[:, 0:n], in0=xT[:, dc, t0:t0 + n], in1=rstd[:, 0:n], op=ALU.mult),
                     reads=[("xT", bi), "rstd"], writes=[("ntmp", dc % 2)])
                P.op("scalar", lambda e, tmp=tmp, dc=dc, t0=t0, n=n, who=who: e.activation(
                    out=hT[:, dc, t0:t0 + n], in_=tmp[:, 0:n], func=AF.Identity,
                    scale=lv[:, which * 3 + 0, dc, who:who + 1], bias=lv[:, which * 3 + 1, dc, who:who + 1]),
                    reads=[("ntmp", dc % 2), "lv"], writes=[("hT", bi)])

    def ffn(self, l, blocks):
        P = self.P
        xT, hT, lv = self.xT, self.hT, self.lv
        ab = self.arena_b
        act = [ab[:, g * FC * 512:(g + 1) * FC * 512].rearrange("p (f t) -> p f t", f=FC) for g in range(2)]
        o = 2 * FC * 512
        wi_t = [ab[:, o + i * 2048:o + (i + 1) * 2048].rearrange("p (c n) -> p c n", c=DC) for i in range(2)]
        o += 2 * 2048
        wo_t = [ab[:, o + i * FC * 128:o + (i + 1) * FC * 128].rearrange("p (f n) -> p f n", f=FC) for i in range(2)]
        o += 2 * FC * 128
        assert o <= 34816, o
        sg = [self.ntmp[0], self.ntmp[1]]
        groups = [blocks[i:i + 2] for i in range(0, len(blocks), 2)]
        for grp in groups:
            for f in range(FC):
                w = wi_t[f % 2]
                wk = ("wi", f % 2)
                P.dma("gpsimd", w[:, :, 0:128], self.ffn_wi[l, :, f * 128:(f + 1) * 128].rearrange("(c p) n -> p c n", p=128), writes=[wk])
                P.dma("gpsimd", w[:, :, 128:256], self.ffn_wi[l, :, DFF + f * 128:DFF + (f + 1) * 128].rearrange("(c p) n -> p c n", p=128), writes=[wk])
                for gi, bi in enumerate(grp):
                    t0, n = TB[bi]
                    b0 = (f % 2) * 4 + gi * 2
                    pg, pu = self.psum[b0], self.psum[b0 + 1]
                    kg, ku = ("ps", b0), ("ps", b0 + 1)
                    for (ps, pk, c0) in [(pg, kg, 0), (pu, ku, 128)]:
                        for dc in range(DC):
                            P.op("tensor", lambda e, ps=ps, w=w, c0=c0, dc=dc, t0=t0, n=n: e.matmul(ps[:, 0:n], lhsT=w[:, dc, c0:c0 + 128], rhs=hT[:, dc, t0:t0 + n], start=(dc == 0), stop=(dc == DC - 1)),
                                 reads=[wk, ("hT", bi)], writes=[pk])
                    s_ = sg[gi]
                    a_ = act[gi]
                    P.op("scalar", lambda e, s_=s_, pg=pg, n=n: e.activation(out=s_[:, 0:n], in_=pg[:, 0:n], func=AF.Silu), reads=[kg], writes=[("ntmp", gi)])
                    P.op("vector", lambda e, s_=s_, pu=pu, n=n, f=f, a_=a_: e.tensor_tensor(out=a_[:, f, 0:n], in0=s_[:, 0:n], in1=pu[:, 0:n], op=ALU.mult),
                         reads=[ku, ("ntmp", gi)], writes=[("act", gi, f)])
            for j in range(DC):
                w = wo_t[j % 2]
                wk = ("wo", j % 2)
                P.dma("gpsimd", w, self.ffn_wo[l, :, j * 128:(j + 1) * 128].rearrange("(f p) n -> p f n", p=128), writes=[wk])
                for gi, bi in enumerate(grp):
                    t0, n = TB[bi]
                    who = 1 if bi == 0 else 0
                    b0 = (j % 2) * 2 + gi
                    ps = self.psum[b0]
                    pk = ("ps", b0)
                    a_ = act[gi]
                    for f in range(FC):
                        P.op("tensor", lambda e, ps=ps, w=w, f=f, n=n, a_=a_: e.matmul(ps[:, 0:n], lhsT=w[:, f, :], rhs=a_[:, f, 0:n], start=(f == 0), stop=(f == FC - 1)),
                             reads=[wk, ("act", gi, f)], writes=[pk])
                    P.op("vector", lambda e, ps=ps, j=j, t0=t0, n=n, who=who: e.scalar_tensor_tensor(
                        out=xT[:, j, t0:t0 + n], in0=ps[:, 0:n], scalar=lv[:, 5, j, who:who + 1], in1=xT[:, j, t0:t0 + n], op0=ALU.mult, op1=ALU.add),
                        reads=[pk, "lv", ("xT", bi)], writes=[("xT", bi)])

    def layer(self, l):
        P = self.P
        last = (l == NL - 1) and not self.dbg.get('ctx_always')
        blocks_all = list(range(5))
        blocks_out = [1, 2, 3, 4] if last else blocks_all
        self.ada(l)
        if self.do_mix:
            self.norm_mod(0, blocks_all)
            P.barrier()
            self.mixers(l, blocks_out)
            P.barrier()
        if self.do_ffn:
            self.norm_mod(1, blocks_out)
            P.barrier()
            self.ffn(l, blocks_out)
            P.barrier()

    def mixers(self, l, blocks_out):
        P = self.P
        if self.branches[0]:
            self.gmlp(l, blocks_out)
        else:
            self.zero_branch(0)
        P.barrier()
        if self.branches[1]:
            self.attention(l, blocks_out)
        else:
            self.zero_branch(1)
        P.barrier()
        if self.branches[2]:
            self.rwkv(l, blocks_out)
        else:
            self.zero_branch(2)
        P.barrier()
        self.merge(l, blocks_out)

    def zero_branch(self, i):
        P = self.P
        z = self.arena_b[:, 0:T]
        P.op("vector", lambda e: e.memset(z, 0.0), writes=["z"])
        for k in range(DC):
            P.dma("sync", self.brT_d[i][k], z, reads=["z"], writes=[f"brT{i}"])

    def gmlp(self, l, blocks_out):
        P = self.P
        hT = self.hT
        ab, af = self.arena_b, self.arena
        Wu = ab[:, 0:8192].rearrange("p (c n) -> p c n", c=DC)
        Wv = ab[:, 8192:16384].rearrange("p (c n) -> p c n", c=DC)
        wsT = ab[:, 16384:17408].rearrange("p (g t) -> p g t", g=8)
        vn = ab[:, 17408:18432]
        aT = [ab[:, 18432 + i * 1024:18432 + (i + 1) * 1024].rearrange("p (g t) -> p g t", g=8) for i in range(2)]
        o = 20480 // 2
        gv = af[:, o:o + 1024]; o += 1024
        gu = [af[:, o + i * 512:o + (i + 1) * 512] for i in range(2)]; o += 1024
        ft = [af[:, o + i * 512:o + (i + 1) * 512] for i in range(2)]; o += 1024
        vg_rep = af[:, o:o + 1024]; o += 1024
        bs_rep = af[:, o:o + 1024]; o += 1024
        ws_ld = af[:, o:o + 1024].rearrange("p (g s) -> p g s", g=8); o += 1024
        sq = af[:, o:o + 1024]; o += 1024
        st = self.vec[:, 32:40]
        assert o <= 17408
        for c in range(2):
            P.dma("gpsimd", Wu[:, :, c * 512:(c + 1) * 512], self.w_in[l, :, GM_OFF + c * 512:GM_OFF + (c + 1) * 512].rearrange("(c p) n -> p c n", p=128), writes=["Wu"])
            P.dma("gpsimd", Wv[:, :, c * 512:(c + 1) * 512], self.w_in[l, :, GM_OFF + D + c * 512:GM_OFF + D + (c + 1) * 512].rearrange("(c p) n -> p c n", p=128), writes=["Wv"])
        P.dma("sync", vg_rep, self.gm_v_g[l].partition_broadcast(128), writes=["vg_rep"])
        P.dma("sync", bs_rep, self.gm_bs[l].rearrange("g t -> (g t)").partition_broadcast(128), writes=["bs_rep"])
        P.dma("sync", ws_ld, self.gm_ws[l].rearrange("g t s -> t g s"), writes=["ws_ld"])
        for half in range(2):
            ps = self.psum[half]
            for j in range(4):
                g = half * 4 + j
                P.op("tensor", lambda e, ps=ps, j=j, g=g: e.transpose(ps[:, j * 128:(j + 1) * 128], ws_ld[:, g, :], self.ident[:]),
                     reads=["ws_ld", "ident"], writes=[("ps", half)])
            P.op("vector", lambda e, ps=ps, half=half: e.tensor_copy(out=wsT[:, half * 4:half * 4 + 4, :], in_=ps[:].rearrange("p (j t) -> p j t", j=4)),
                 reads=[("ps", half)], writes=["wsT"])
        tiles = []
        for bi in blocks_out:
            t0, n = TB[bi]
            tiles += [(bi, t0 + i * 128) for i in range(n // 128)]
        for it, (bi, tt) in enumerate(tiles):
            for half in range(2):
                ps = self.psum[half]
                for dc in range(DC):
                    P.op("tensor", lambda e, ps=ps, dc=dc, tt=tt, half=half: e.matmul(ps[:], lhsT=hT[:, dc, tt:tt + 128], rhs=Wv[:, dc, half * 512:(half + 1) * 512], start=(dc == 0), stop=(dc == DC - 1)),
                         reads=["Wv", ("hT", bi)], writes=[("ps", half)])
                P.op("scalar", lambda e, ps=ps, half=half: e.activation(out=gv[:, half * 512:(half + 1) * 512], in_=ps[:], func=AF.Gelu),
                     reads=[("ps", half)], writes=[("gv", half)])
            P.op("scalar", lambda e: e.activation(out=sq, in_=gv, func=AF.Square, accum_out=st[:, 0:1]), reads=["gv"], writes=["sq", "st"])
            P.op("vector", lambda e: e.tensor_scalar(out=st[:, 1:2], in0=st[:, 0:1], scalar1=1.0 / D, scalar2=RMS_EPS, op0=ALU.mult, op1=ALU.add), reads=["st"], writes=["st"])
            P.op("scalar", lambda e: e.activation(out=st[:, 2:3], in_=st[:, 1:2], func=AF.Sqrt), reads=["st"], writes=["st"])
            P.op("vector", lambda e: e.reciprocal(out=st[:, 3:4], in_=st[:, 2:3]), reads=["st"], writes=["st"])
            P.op("vector", lambda e: e.scalar_tensor_tensor(out=vn, in0=gv, scalar=st[:, 3:4], in1=vg_rep, op0=ALU.mult, op1=ALU.mult),
                 reads=["gv", "st", "vg_rep"], writes=["vn"])
            a_t = aT[it % 2]
            ak = ("aT", it % 2)
            for half in range(2):
                pu, pf = self.psum[2 + half], self.psum[4 + half]
                ku, kf = ("ps", 2 + half), ("ps", 4 + half)
                for j in range(4):
                    g = half * 4 + j
                    for dc in range(DC):
                        P.op("tensor", lambda e, pu=pu, j=j, g=g, dc=dc, tt=tt: e.matmul(pu[:, j * 128:(j + 1) * 128], lhsT=Wu[:, dc, g * 128:(g + 1) * 128], rhs=hT[:, dc, tt:tt + 128], start=(dc == 0), stop=(dc == DC - 1)),
                             reads=["Wu", ("hT", bi)], writes=[ku])
                    P.op("tensor", lambda e, pf=pf, j=j, g=g: e.matmul(pf[:, j * 128:(j + 1) * 128], lhsT=vn[:, g * 128:(g + 1) * 128], rhs=wsT[:, g, :], start=True, stop=True),
                         reads=["vn", "wsT"], writes=[kf])
                P.op("scalar", lambda e, pu=pu, half=half: e.activation(out=gu[half], in_=pu[:], func=AF.Gelu), reads=[ku], writes=[("gu", half)])
                P.op("vector", lambda e, pf=pf, half=half: e.tensor_tensor(out=ft[half], in0=pf[:], in1=bs_rep[:, half * 512:(half + 1) * 512], op=ALU.add),
                     reads=[kf, "bs_rep"], writes=[("ft", half)])
                P.op("vector", lambda e, half=half, a_t=a_t: e.tensor_tensor(out=a_t[:, half * 4:half * 4 + 4, :], in0=ft[half].rearrange("p (j t) -> p j t", j=4), in1=gu[half].rearrange("p (j t) -> p j t", j=4), op=ALU.mult),
                     reads=[("ft", half), ("gu", half)], writes=[ak])
            P.dma("sync", self.brT_d[0][:, :, tt:tt + 128].rearrange("g p t -> p g t"), a_t, reads=[ak], writes=["brT0"])

    def attention(self, l, blocks_out):
        import math
        P = self.P
        hT = self.hT
        ab, af = self.arena_b, self.arena
        lam_init = 0.8 - 0.6 * math.exp(-0.3 * l)
        ctx_out = 0 in blocks_out
        o = 0
        def F(n):
            nonlocal o
            r = af[:, o:o + n]; o += n
            return r
        def B(n):
            nonlocal o
            r = ab[:, 2 * o:2 * o + n]; o += (n + 1) // 2
            return r
        cosT = F(T); sinT = F(T); qf = F(T)
        qT = B(T); kT = B(T)
        Vext = B(18 * 130).rearrange("p (k e) -> p k e", k=18)
        Wq = B(1024).rearrange("p (c n) -> p c n", c=DC); Wk = B(1024).rearrange("p (c n) -> p c n", c=DC); Wv = B(1024).rearrange("p (c n) -> p c n", c=DC)
        Et4 = [B(512) for _ in range(4)]
        tmp = [F(512) for _ in range(2)]
        o_sb = F(512).rearrange("p (s e) -> p s e", s=4)
        b_all = F(512).rearrange("p (s e) -> p s e", s=4)
        bT = B(512)
        g_rep = F(128)
        lamt = F(256)
        bd64 = F(128); rperm = F(128)
        zb = B(512)
        sc = F(32)
        gq = sc[:, 0:1]; gk = sc[:, 1:2]; lamv = sc[:, 2:3]; nlam = sc[:, 3:4]
        assert o <= 17408, o
        P.dma("sync", cosT, self.cosT_d[:, :], writes=["cosT"])
        P.dma("sync", sinT, self.sinT_d[:, :], writes=["sinT"])
        P.dma("sync", bd64, self.bd64_d[:, :], writes=["bd64"])
        P.dma("sync", rperm, self.rperm_d[:, :], writes=["rperm"])
        for h2 in range(2):
            P.dma("sync", sc[h2 * 64:(h2 + 1) * 64, 0:1], self.da_q_g[l].rearrange("(d o) -> d o", o=1), writes=["sc"])
            P.dma("sync", sc[h2 * 64:(h2 + 1) * 64, 1:2], self.da_k_g[l].rearrange("(d o) -> d o", o=1), writes=["sc"])
        P.dma("sync", g_rep, self.da_subln_g[l].partition_broadcast(128), writes=["g_rep"])
        P.dma("sync", lamt, self.da_lambda[l].rearrange("a d -> (a d)").partition_broadcast(128), writes=["lamt"])
        P.op("vector", lambda e: e.memset(zb, 0.0), writes=["zb"])
        P.op("vector", lambda e: e.memset(Vext[:, :, 128:129], 1.0), writes=["Vext"])
        P.op("vector", lambda e: e.tensor_scalar(out=g_rep, in0=g_rep, scalar1=(1.0 - lam_init), scalar2=None, op0=ALU.mult), reads=["g_rep"], writes=["g_rep"])
        for i in range(2):
            P.op("vector", lambda e, i=i: e.tensor_tensor(out=tmp[0][:, i * 64:(i + 1) * 64], in0=lamt[:, i * 128:i * 128 + 64], in1=lamt[:, i * 128 + 64:i * 128 + 128], op=ALU.mult),
                 reads=["lamt"], writes=[("tmp", 0)])
            P.op("vector", lambda e, i=i: e.reduce_sum(out=sc[:, 4 + i:5 + i], in_=tmp[0][:, i * 64:(i + 1) * 64], axis=AX.X), reads=[("tmp", 0)], writes=["sc"])
        P.op("scalar", lambda e: e.activation(out=sc[:, 6:8], in_=sc[:, 4:6], func=AF.Exp), reads=["sc"], writes=["sc"])
        P.op("vector", lambda e: e.tensor_tensor(out=sc[:, 8:9], in0=sc[:, 6:7], in1=sc[:, 7:8], op=ALU.subtract), reads=["sc"], writes=["sc"])
        P.op("vector", lambda e: e.tensor_scalar(out=nlam, in0=sc[:, 8:9], scalar1=lam_init, scalar2=-1.0, op0=ALU.add, op1=ALU.mult), reads=["sc"], writes=["sc"])

        qblocks = [(TB[bi][0], TB[bi][1], list(range(18))) for bi in (1, 2, 3, 4)]
        if ctx_out:
            qblocks.append((0, 256, [0, 1]))
        def load_w(hd):
            for (W, off, nm) in [(Wq, 0, "Wq"), (Wk, 1024, "Wk"), (Wv, 2048, "Wv")]:
                c0 = DA_OFF + off + hd * 128
                P.dma("gpsimd", W, self.w_in[l, :, c0:c0 + 128].rearrange("(c p) n -> p c n", p=128), writes=[nm])

        load_w(0)
        for hd in range(8):
            for (W, nm, gcol, dst, dnm) in [(Wq, "Wq", gq, qT, "qT"), (Wk, "Wk", gk, kT, "kT")]:
                for bi in range(5):
                    t0, n = TB[bi]
                    ps = self.psum[4 + bi % 2]; pk = ("ps", 4 + bi % 2)
                    for dc in range(DC):
                        P.op("tensor", lambda e, ps=ps, W=W, dc=dc, t0=t0, n=n: e.matmul(ps[:, 0:n], lhsT=W[:, dc, :], rhs=hT[:, dc, t0:t0 + n], start=(dc == 0), stop=(dc == DC - 1)),
                             reads=[nm, ("hT", bi)], writes=[pk])
                    P.op("scalar", lambda e, ps=ps, t0=t0, n=n: e.copy(out=qf[:, t0:t0 + n], in_=ps[:, 0:n]), reads=[pk], writes=[("qf", bi)])
                    tq = tmp[bi % 2]; tk = ("tmp", bi % 2)
                    P.op("scalar", lambda e, tq=tq, t0=t0, n=n: e.activation(out=tq[:, 0:n], in_=qf[:, t0:t0 + n], func=AF.Square), reads=[("qf", bi)], writes=[tk])
                    p2 = self.psum[6]; k2 = ("ps", 6)
                    P.op("tensor", lambda e, p2=p2, tq=tq, n=n: e.matmul(p2[:, 0:n], lhsT=bd64, rhs=tq[:, 0:n], start=True, stop=True), reads=[tk, "bd64"], writes=[k2])
                    P.op("vector", lambda e, p2=p2, tq=tq, n=n: e.tensor_scalar(out=tq[:, 0:n], in0=p2[:, 0:n], scalar1=1.0 / 64, scalar2=RMS_EPS, op0=ALU.mult, op1=ALU.add), reads=[k2], writes=[tk])
                    P.op("scalar", lambda e, tq=tq, n=n: e.activation(out=tq[:, 0:n], in_=tq[:, 0:n], func=AF.Sqrt), reads=[tk], writes=[tk])
                    P.op("vector", lambda e, tq=tq, n=n: e.reciprocal(out=tq[:, 0:n], in_=tq[:, 0:n]), reads=[tk], writes=[tk])
                    P.op("vector", lambda e, tq=tq, t0=t0, n=n, gcol=gcol: e.scalar_tensor_tensor(out=qf[:, t0:t0 + n], in0=qf[:, t0:t0 + n], scalar=gcol, in1=tq[:, 0:n], op0=ALU.mult, op1=ALU.mult),
                         reads=[("qf", bi), tk, "sc"], writes=[("qf", bi)])
                    p3 = self.psum[7]; k3 = ("ps", 7)
                    P.op("tensor", lambda e, p3=p3, t0=t0, n=n: e.matmul(p3[:, 0:n], lhsT=rperm, rhs=qf[:, t0:t0 + n], start=True, stop=True), reads=[("qf", bi), "rperm"], writes=[k3])
                    P.op("vector", lambda e, p3=p3, tq=tq, t0=t0, n=n: e.tensor_tensor(out=tq[:, 0:n], in0=p3[:, 0:n], in1=sinT[:, t0:t0 + n], op=ALU.mult), reads=[k3, "sinT"], writes=[tk])
                    P.op("gpsimd", lambda e, t0=t0, n=n: e.tensor_tensor(out=qf[:, t0:t0 + n], in0=qf[:, t0:t0 + n], in1=cosT[:, t0:t0 + n], op=ALU.mult), reads=[("qf", bi), "cosT"], writes=[("qf", bi)])
                    P.op("vector", lambda e, tq=tq, dst=dst, t0=t0, n=n: e.tensor_tensor(out=dst[:, t0:t0 + n], in0=qf[:, t0:t0 + n], in1=tq[:, 0:n], op=ALU.add), reads=[("qf", bi), tk], writes=[(dnm, bi)])
            for g4 in range(5):
                tiles = list(range(g4 * 4, min(18, g4 * 4 + 4)))
                ps = self.psum[4 + g4 % 2]; pk = ("ps", 4 + g4 % 2)
                for j, ti in enumerate(tiles):
                    for dc in range(DC):
                        P.op("tensor", lambda e, ps=ps, j=j, ti=ti, dc=dc: e.matmul(ps[:, j * 128:(j + 1) * 128], lhsT=hT[:, dc, ti * 128:(ti + 1) * 128], rhs=Wv[:, dc, :], start=(dc == 0), stop=(dc == DC - 1)),
                             reads=["Wv", "hT"], writes=[pk])
                nt = len(tiles)
                P.op("vector", lambda e, ps=ps, g4=g4, nt=nt: e.tensor_copy(out=Vext[:, g4 * 4:g4 * 4 + nt, 0:128], in_=ps[:, 0:nt * 128].rearrange("p (j e) -> p j e", j=nt)),
                     reads=[pk], writes=["Vext"])
            if hd + 1 < 8:
                load_w(hd + 1)
            for (q0, nq, kts) in qblocks:
                nsub = nq // 128
                accs2 = [[self.psum[2 + 2 * c], self.psum[3 + 2 * c]][:(nsub + 1) // 2] for c in range(2)]
                for c in range(2):
                    for ai, A in enumerate(accs2[c]):
                        P.op("tensor", lambda e, A=A: e.matmul(A[:, 0:512], lhsT=zb[0:1, 0:128], rhs=zb[0:1, 0:512], start=True, stop=False, skip_group_check=True), reads=["zb"], writes=[("ps", 2 + 2 * c + ai)])
                sbank = [[0, 1], [6, 7]]

                def emit_qk(ki, q0=q0, nq=nq, kts=kts):
                    kt = kts[ki]
                    for c in range(2):
                        bnk = sbank[ki % 2][c]
                        pS = self.psum[bnk]
                        P.op("tensor", lambda e, pS=pS, kt=kt, c=c: e.matmul(pS[:, 0:nq], lhsT=kT[c * 64:(c + 1) * 64, kt * 128:(kt + 1) * 128], rhs=qT[c * 64:(c + 1) * 64, q0:q0 + nq], start=True, stop=True),
                             reads=["kT", "qT"], writes=[("ps", bnk)])

                def emit_pv(ki, nq=nq, kts=kts, nsub=nsub, accs2=accs2):
                    kt = kts[ki]
                    for c in range(2):
                        bnk = sbank[ki % 2][c]
                        pS = self.psum[bnk]
                        E = Et4[(ki % 2) * 2 + c]; kE = ("Et", (ki % 2) * 2 + c)
                        P.op("scalar", lambda e, pS=pS, E=E: e.activation(out=E[:, 0:nq], in_=pS[:, 0:nq], func=AF.Exp, scale=0.125), reads=[("ps", bnk)], writes=[kE])
                        for qs in range(nsub):
                            A = accs2[c][qs // 2]; col = (qs % 2) * 129
                            P.op("tensor", lambda e, A=A, col=col, E=E, qs=qs, kt=kt, last=(ki == len(kts) - 1): e.matmul(A[:, col:col + 129], lhsT=E[:, qs * 128:(qs + 1) * 128], rhs=Vext[:, kt, 0:129], start=False, stop=last, skip_group_check=True),
                                 reads=[kE, "Vext"], writes=[("ps", 2 + 2 * c + qs // 2)])

                emit_qk(0)
                for ki in range(len(kts)):
                    if ki + 1 < len(kts):
                        emit_qk(ki + 1)
                    emit_pv(ki)
                for c in range(2):
                    ab0 = 2 + 2 * c
                    accs = accs2[c]
                    for ai, A in enumerate(accs):
                        kA = ("ps", ab0 + ai)
                        Av = A[:, 0:258].rearrange("p (s e) -> p s e", s=2)
                        rz = sc[:, 10 + 2 * ai:12 + 2 * ai]
                        rzb = rz.unsqueeze(2).broadcast_to([128, 2, 128])
                        ov = o_sb[:, 2 * ai:2 * ai + 2, :]
                        P.op("vector", lambda e, Av=Av, rz=rz: e.reciprocal(out=rz.unsqueeze(2), in_=Av[:, :, 128:129]), reads=[kA], writes=[("rz", ai)])
                        if c == 0:
                            P.op("vector", lambda e, Av=Av, rzb=rzb, ov=ov: e.tensor_tensor(out=ov, in0=Av[:, :, 0:128], in1=rzb, op=ALU.mult), reads=[kA, ("rz", ai)], writes=[("o_sb", ai)])
                        else:
                            t2v = tmp[ai][:, 0:256].rearrange("p (s e) -> p s e", s=2)
                            P.op("vector", lambda e, rz=rz: e.tensor_scalar(out=rz, in0=rz, scalar1=nlam, scalar2=None, op0=ALU.mult), reads=[("rz", ai), "sc"], writes=[("rz", ai)])
                            P.op("vector", lambda e, Av=Av, rzb=rzb, t2v=t2v: e.tensor_tensor(out=t2v, in0=Av[:, :, 0:128], in1=rzb, op=ALU.mult), reads=[kA, ("rz", ai)], writes=[("tmp", ai)])
                            P.op("gpsimd", lambda e, ov=ov, t2v=t2v: e.tensor_tensor(out=ov, in0=ov, in1=t2v, op=ALU.add), reads=[("tmp", ai), ("o_sb", ai)], writes=[("o_sb", ai)])
                pT = self.psum[6]; kT_ = ("ps", 6)
                ov = o_sb[:, 0:nsub, :]
                sqv = tmp[0][:, 0:nsub * 128]
                ssv = sc[:, 16:16 + nsub]
                bv = b_all[:, 0:nsub, :]
                P.op("scalar", lambda e, ov=ov, sqv=sqv, nsub=nsub: e.activation(out=sqv.rearrange("p (s e) -> p s e", s=nsub), in_=ov, func=AF.Square), reads=["o_sb"], writes=[("tmp", 0)])
                P.op("vector", lambda e, sqv=sqv, ssv=ssv, nsub=nsub: e.reduce_sum(out=ssv, in_=sqv.rearrange("p (s e) -> p s e", s=nsub), axis=AX.X), reads=[("tmp", 0)], writes=["ss"])
                P.op("vector", lambda e, ssv=ssv: e.tensor_scalar(out=ssv, in0=ssv, scalar1=1.0 / 128, scalar2=RMS_EPS, op0=ALU.mult, op1=ALU.add), reads=["ss"], writes=["ss"])
                P.op("scalar", lambda e, ssv=ssv: e.activation(out=ssv, in_=ssv, func=AF.Sqrt), reads=["ss"], writes=["ss"])
                P.op("vector", lambda e, ssv=ssv: e.reciprocal(out=ssv, in_=ssv), reads=["ss"], writes=["ss"])
                P.op("vector", lambda e, ov=ov, bv=bv, ssv=ssv, nsub=nsub: e.tensor_tensor(out=bv, in0=ov, in1=ssv.unsqueeze(2).broadcast_to([128, nsub, 128]), op=ALU.mult), reads=["o_sb", "ss"], writes=["b_all"])
                P.op("gpsimd", lambda e, bv=bv, nsub=nsub: e.tensor_tensor(out=bv, in0=bv, in1=g_rep.unsqueeze(1).broadcast_to([128, nsub, 128]), op=ALU.mult), reads=["b_all", "g_rep"], writes=["b_all"])
                for qs in range(nsub):
                    P.op("tensor", lambda e, pT=pT, qs=qs: e.transpose(pT[:, qs * 128:(qs + 1) * 128], b_all[:, qs, :], self.ident[:]), reads=["b_all", "ident"], writes=[kT_])
                P.op("vector", lambda e, pT=pT, nq=nq: e.tensor_copy(out=bT[:, 0:nq], in_=pT[:, 0:nq]), reads=[kT_], writes=["bT"])
                P.dma("sync", self.brT_d[1][hd, :, q0:q0 + nq], bT[:, 0:nq], reads=["bT"], writes=["brT1"])
        if not ctx_out:
            pass

    def rwkv(self, l, blocks_out):
        P = self.P
        stage = self.dbg.get("rw_stage", 99)
        self.rwkv_proj(l)
        P.barrier()
        if stage <= 1:
            return
        for dc in range(DC):
            P.dma("sync", self.xT_d[:, dc * T:(dc + 1) * T], self.xT[:, dc, :], reads=["xT"], writes=["xT_d"])
        P.dma("sync", self.hT_d[:, :], self.hT[:].rearrange("p c t -> p (c t)"), reads=["hT"], writes=["hT_d"])
        P.barrier()
        if stage >= 3:
            self.rwkv_pass(l, 0, blocks_out)
            P.barrier()
        if stage >= 11:
            self.rwkv_pass(l, 1, blocks_out)
            P.barrier()
        for dc in range(DC):
            P.dma("sync", self.xT[:, dc, :], self.xT_d[:, dc * T:(dc + 1) * T], reads=["xT_d"], writes=["xT"])
        P.dma("sync", self.hT[:].rearrange("p c t -> p (c t)"), self.hT_d[:, :], reads=["hT_d"], writes=["hT"])
        P.barrier()

    def rwkv_proj(self, l):
        P = self.P
        hT = self.hT
        ab, af = self.arena_b, self.arena
        hsT = ab[:, 0:DC * T].rearrange("p (c t) -> p c t", c=DC)
        o = DC * T // 2
        W = [ab[:, 2 * o + i * 4096:2 * o + (i + 1) * 4096].rearrange("p (c n) -> p c n", c=DC) for i in range(2)]; o += 4096
        mu_rep = [af[:, o + i * 512:o + (i + 1) * 512] for i in range(2)]; o += 1024
        t1 = [af[:, o + i * 512:o + (i + 1) * 512] for i in range(2)]; o += 1024
        t2 = [af[:, o + i * 512:o + (i + 1) * 512] for i in range(2)]; o += 1024
        mucol = af[:, o:o + 4]; o += 4
        assert o <= 17408, o
        for (s0, n) in [(0, NCTX), (NCTX, NLAT)]:
            P.op("vector", lambda e, s0=s0, n=n: e.tensor_tensor(out=hsT[:, :, s0 + 1:s0 + n - 1], in0=hT[:, :, s0:s0 + n - 2], in1=hT[:, :, s0 + 2:s0 + n], op=ALU.add), reads=["hT"], writes=["hsT"])
            P.op("vector", lambda e, s0=s0: e.tensor_copy(out=hsT[:, :, s0:s0 + 1], in_=hT[:, :, s0 + 1:s0 + 2]), reads=["hT"], writes=["hsT"])
            P.op("vector", lambda e, s0=s0, n=n: e.tensor_copy(out=hsT[:, :, s0 + n - 1:s0 + n], in_=hT[:, :, s0 + n - 2:s0 + n - 1]), reads=["hT"], writes=["hsT"])
        P.op("scalar", lambda e: e.mul(out=hsT, in_=hsT, mul=0.5), reads=["hsT"], writes=["hsT"])
        for cb in range(6):
            w = W[cb % 2]; wk = ("W", cb % 2)
            P.dma("gpsimd", w, self.w_in[l, :, RW_OFF + cb * 512:RW_OFF + (cb + 1) * 512].rearrange("(c p) n -> p c n", p=128), writes=[wk])
            mr = mu_rep[cb % 2]; mk = ("mu", cb % 2)
            P.dma("sync", mr, self.rw_mu[l, cb * 512:(cb + 1) * 512].partition_broadcast(128), writes=[mk])
            for ti in range(T // 128):
                pp, pS = self.psum[(ti % 2) * 2], self.psum[(ti % 2) * 2 + 1]
                kp, kS = ("ps", (ti % 2) * 2), ("ps", (ti % 2) * 2 + 1)
                for (ps, pk, src, sk) in [(pp, kp, hT, "hT"), (pS, kS, hsT, "hsT")]:
                    for dc in range(DC):
                        P.op("tensor", lambda e, ps=ps, src=src, dc=dc, ti=ti, w=w: e.matmul(ps[:], lhsT=src[:, dc, ti * 128:(ti + 1) * 128], rhs=w[:, dc, :], start=(dc == 0), stop=(dc == DC - 1)),
                             reads=[sk, wk], writes=[pk])
                a, b_ = t1[ti % 2], t2[ti % 2]
                ka, kb = ("t1", ti % 2), ("t2", ti % 2)
                P.op("scalar", lambda e, a=a, pp=pp: e.copy(out=a, in_=pp[:]), reads=[kp], writes=[ka])
                P.op("vector", lambda e, a=a, b_=b_, pS=pS: e.tensor_tensor(out=b_, in0=pS[:], in1=a, op=ALU.subtract), reads=[kS, ka], writes=[kb])
                P.op("gpsimd", lambda e, b_=b_, mr=mr: e.tensor_tensor(out=b_, in0=b_, in1=mr, op=ALU.mult), reads=[kb, mk], writes=[kb])
                P.op("vector", lambda e, a=a, b_=b_: e.tensor_tensor(out=a, in0=a, in1=b_, op=ALU.add), reads=[ka, kb], writes=[ka])
                P.dma("sync", self.ztok_d[ti * 128:(ti + 1) * 128, cb * 512:(cb + 1) * 512], a, reads=[ka], writes=["ztok"])
        for ci, (c0, ncol) in enumerate([(3072, 128), (3200, 128), (3328, 128), (3456, 32)]):
            w = W[ci % 2]; wk = ("W", ci % 2)
            P.dma("gpsimd", w[:, :, 0:ncol], self.w_in[l, :, RW_OFF + c0:RW_OFF + c0 + ncol].rearrange("(c p) n -> p c n", p=128), writes=[wk])
            P.dma("sync", mucol[0:ncol, ci:ci + 1], self.rw_mu[l, c0:c0 + ncol].rearrange("(d o) -> d o", o=1), writes=[("mucol", ci)])
            func = [AF.Tanh, AF.Identity, AF.Sigmoid, AF.Sigmoid][ci]
            for bi in range(5):
                t0, n = TB[bi]
                pp, pS = self.psum[4 + (bi % 2) * 2], self.psum[5 + (bi % 2) * 2]
                kp, kS = ("ps", 4 + (bi % 2) * 2), ("ps", 5 + (bi % 2) * 2)
                for (ps, pk, src, sk) in [(pp, kp, hT, "hT"), (pS, kS, hsT, "hsT")]:
                    for dc in range(DC):
                        P.op("tensor", lambda e, ps=ps, src=src, dc=dc, t0=t0, n=n, w=w, ncol=ncol: e.matmul(ps[0:ncol, 0:n], lhsT=w[:, dc, 0:ncol], rhs=src[:, dc, t0:t0 + n], start=(dc == 0), stop=(dc == DC - 1)),
                             reads=[sk, wk], writes=[pk])
                a, b_ = t1[bi % 2], t2[bi % 2]
                ka, kb = ("t1", bi % 2), ("t2", bi % 2)
                P.op("scalar", lambda e, a=a, pp=pp, n=n, ncol=ncol: e.copy(out=a[0:ncol, 0:n], in_=pp[0:ncol, 0:n]), reads=[kp], writes=[ka])
                P.op("vector", lambda e, a=a, b_=b_, pS=pS, n=n, ncol=ncol: e.tensor_tensor(out=b_[0:ncol, 0:n], in0=pS[0:ncol, 0:n], in1=a[0:ncol, 0:n], op=ALU.subtract), reads=[kS, ka], writes=[kb])
                P.op("vector", lambda e, a=a, b_=b_, n=n, ncol=ncol, ci=ci: e.scalar_tensor_tensor(out=a[0:ncol, 0:n], in0=b_[0:ncol, 0:n], scalar=mucol[0:ncol, ci:ci + 1], in1=a[0:ncol, 0:n], op0=ALU.mult, op1=ALU.add),
                     reads=[ka, kb, ("mucol", ci)], writes=[ka])
                P.op("scalar", lambda e, a=a, n=n, ncol=ncol, func=func: e.activation(out=a[0:ncol, 0:n], in_=a[0:ncol, 0:n], func=func), reads=[ka], writes=[ka])
                P.dma("sync", self.smallT_d[ci, 0:ncol, t0:t0 + n], a[0:ncol, 0:n], reads=[ka], writes=["smallT"])

    def rwkv_pass(self, l, d, blocks_out):
        import math
        P = self.P
        C0 = math.exp(-0.5)
        ctx_out = 0 in blocks_out
        regions = [[self.arena, 0, 17408], [self.hT[:].rearrange("p c t -> p (c t)").bitcast(F32), 0, DC * T // 2], [self.xT[:].rearrange("p c t -> p (c t)"), 0, DC * T]]

        def F(n):
            for r in regions:
                if r[1] + n <= r[2]:
                    v = r[0][:, r[1]:r[1] + n]; r[1] += n
                    return v
            raise RuntimeError("rwkv scratch exhausted")
        H = F(512).rearrange("p (k i) -> p k i", k=8)
        kkp_rep = F(1024); ka_rep = F(1024); w0_rep = F(1024); a0_rep = F(1024)
        w2t = F(1024); a2t = F(1024)
        masks = F(1024).rearrange("p (m t) -> p m t", m=8)
        IU, IL, SU, SL, SUd, SLd, SUo, SLo = (masks[:, i, :] for i in range(8))
        INCL = IU if d == 0 else IL
        MS_st = SU if d == 0 else SL
        MI_st = INCL
        MXd_ts = SLd if d == 0 else SUd
        MXd_st = SUd if d == 0 else SLd
        MLo_ts = SLo if d == 0 else SUo
        ztile = F(3072); zr = ztile[:, 0:1024]; zk = ztile[:, 1024:2048]; zv = ztile[:, 2048:3072]
        tz_t = F(128); za_t = F(128)
        sgw = F(1024); alpha = F(1024); tbuf = F(1024); nkk = F(1024); kd = F(1024); bb = F(1024); Ep = F(1024)
        Em = alpha; Ex = tbuf; U = zk
        XT4 = [F(1024).rearrange("p (k t) -> p k t", k=8) for _ in range(4)]
        AtT, BtT, KtT, RtT = XT4
        big = [F(1024).rearrange("p (h t) -> p h t", h=8) for _ in range(12)]
        yt = F(1024)
        st = F(64)
        Gam = st[:, 0:8]
        if d == 1:
            a00_rep = F(1024); alpha0 = F(1024); kd0 = F(1024)
            lnw_rep = F(1024); lnb_rep = F(1024); rk_rep = F(1024)
            g2a = F(1024); g2b = F(1024)
            sgA = F(128); sgB = F(128)
            yf_t = alpha0
            c_tok = kd0[:, 0:512].bitcast(BF16)
            cT = F(512).bitcast(BF16).rearrange("p (c t) -> p c t", c=8)
            identb = F(64).bitcast(BF16)
        ident, ones = self.ident, self.ones
        if self.dbg.get("print_regions"):
            print("rwkv_pass d=%d region usage:" % d, [(r[1], r[2]) for r in regions])

        P.dma("sync", masks, self.masks_d.rearrange("m p t -> p m t"), writes=["masks"])
        P.dma("sync", kkp_rep, self.rw_kk[l].partition_broadcast(128), writes=["kkp_rep"])
        P.dma("sync", ka_rep, self.rw_ka[l].partition_broadcast(128), writes=["ka_rep"])
        P.dma("sync", w0_rep, self.rw_w0[l, d].partition_broadcast(128), writes=["w0_rep"])
        P.dma("sync", a0_rep, self.rw_a0[l, d].partition_broadcast(128), writes=["a0_rep"])
        P.dma("sync", w2t, self.rw_w2[l].rearrange("d r c -> (d r) c"), writes=["w2t"])
        P.dma("sync", a2t, self.rw_a2[l].rearrange("d r c -> (d r) c"), writes=["a2t"])
        P.op("vector", lambda e: e.memset(H, 0.0), writes=["H"])
        if d == 1:
            P.dma("sync", a00_rep, self.rw_a0[l, 0].partition_broadcast(128), writes=["a00_rep"])
            P.dma("sync", lnw_rep, self.rw_ln_w[l].partition_broadcast(128), writes=["lnw_rep"])
            P.dma("sync", lnb_rep, self.rw_ln_b[l].partition_broadcast(128), writes=["lnb_rep"])
            P.dma("sync", rk_rep, self.rw_rk[l].rearrange("h j -> (h j)").partition_broadcast(128), writes=["rk_rep"])
            P.dma("sync", g2a, self.rw_g2[l, 0:128, :], writes=["g2a"])
            P.dma("sync", g2b[0:32, :], self.rw_g2[l, 128:160, :], writes=["g2b"])
            P.op("vector", lambda e: e.tensor_copy(out=identb, in_=ident[:]), reads=["ident"], writes=["identb"])

        order = list(range(18)) if d == 0 else [1, 0] + list(range(17, 1, -1))
        F32R = mybir.dt.float32r
        r32 = (lambda a: a.bitcast(F32R)) if self.dbg.get("fp32r") else (lambda a: a)
        stage = self.dbg.get("rw_stage", 99)
        if stage < 10:
            order = order[:1]
        R2 = [slice(0, 64), slice(64, 128)]

        def lora(dst, src_t, wt, rep, dd, keyd, wkey, rkey, extra_w=(), skey="tzt"):
            for half in range(2):
                ps = self.psum[half]; pk = ("ps", half)
                P.op("tensor", lambda e, ps=ps, half=half: e.matmul(ps[:], lhsT=src_t[R2[dd], :], rhs=wt[R2[dd], half * 512:(half + 1) * 512], start=True, stop=True),
                     reads=[skey, wkey], writes=[pk])
                P.op("vector", lambda e, ps=ps, half=half: e.tensor_tensor(out=dst[:, half * 512:(half + 1) * 512], in0=ps[:], in1=rep[:, half * 512:(half + 1) * 512], op=ALU.add),
                     reads=[pk, rkey], writes=[(keyd, half)] + list(extra_w))
                P.op("scalar", lambda e, half=half: e.activation(out=dst[:, half * 512:(half + 1) * 512], in_=dst[:, half * 512:(half + 1) * 512], func=AF.Sigmoid),
                     reads=[(keyd, half)], writes=[(keyd, half)])

        def kdcalc(dst, al, keya, keyd):
            P.op("vector", lambda e: e.scalar_tensor_tensor(out=dst, in0=al, scalar=-1.0, in1=ka_rep, op0=ALU.add, op1=ALU.mult), reads=[keya, "ka_rep"], writes=[keyd] + (["c_tok"] if keyd == "kd0" else []))
            P.op("vector", lambda e: e.scalar_tensor_tensor(out=dst, in0=dst, scalar=1.0, in1=zk, op0=ALU.add, op1=ALU.mult), reads=[keyd, "ztile"], writes=[keyd])

        for n in order:
            tt = n * 128
            P.dma("sync", ztile, self.ztok_d[tt:tt + 128, :], reads=["ztok"], writes=["ztile", "U"])
            P.dma("sync", tz_t, self.smallT_d[0, :, tt:tt + 128], writes=["tzt"])
            P.dma("sync", za_t, self.smallT_d[1, :, tt:tt + 128], writes=["zat"])
            if d == 1:
                P.dma("sync", sgA, self.smallT_d[2, :, tt:tt + 128], writes=["sgA"])
                P.dma("sync", sgB[0:32, :], self.smallT_d[3, 0:32, tt:tt + 128], writes=["sgB"])
            lora(sgw, tz_t, w2t, w0_rep, d, "sgw", "w2t", "w0_rep")
            lora(alpha, za_t, a2t, a0_rep, d, "alpha", "a2t", "a0_rep", extra_w=["Em"], skey="zat")
            if d == 1:
                lora(alpha0, za_t, a2t, a00_rep, 0, "alpha0", "a2t", "a00_rep", skey="zat")
            P.op("vector", lambda e: e.tensor_tensor(out=tbuf, in0=zk, in1=kkp_rep, op=ALU.mult), reads=["ztile", "kkp_rep"], writes=["tbuf", "Ex"])
            P.op("scalar", lambda e: e.activation(out=nkk, in_=tbuf, func=AF.Square), reads=["tbuf"], writes=["nkk"])
            P.op("vector", lambda e: e.reduce_sum(out=st[:, 16:32], in_=nkk.rearrange("p (h j) -> p h j", h=16), axis=AX.X), reads=["nkk"], writes=["st"])
            P.op("scalar", lambda e: e.activation(out=st[:, 16:32], in_=st[:, 16:32], func=AF.Sqrt), reads=["st"], writes=["st"])
            P.op("vector", lambda e: e.tensor_scalar(out=st[:, 16:32], in0=st[:, 16:32], scalar1=1e-12, scalar2=None, op0=ALU.max), reads=["st"], writes=["st"])
            P.op("vector", lambda e: e.reciprocal(out=st[:, 16:32], in_=st[:, 16:32]), reads=["st"], writes=["st"])
            P.op("vector", lambda e: e.scalar_tensor_tensor(out=nkk.rearrange("p (h j) -> p h j", h=16), in0=tbuf.rearrange("p (h j) -> p h j", h=16), scalar=-1.0,
                                                             in1=st[:, 16:32].unsqueeze(2).broadcast_to([128, 16, 64]), op0=ALU.mult, op1=ALU.mult), reads=["tbuf", "st"], writes=["nkk"])
            kdcalc(kd, alpha, "alpha", "kd")
            if d == 1:
                kdcalc(kd0, alpha0, "alpha0", "kd0")
                P.op("gpsimd", lambda e: e.tensor_tensor(out=kd0, in0=kd0, in1=kd, op=ALU.add), reads=["kd0", "kd"], writes=["kd0"])
            P.op("vector", lambda e: e.scalar_tensor_tensor(out=bb, in0=nkk, scalar=-1.0, in1=alpha, op0=ALU.mult, op1=ALU.mult), reads=["nkk", "alpha"], writes=["bb"])
            if d == 1:
                P.dma("sync", yf_t, self.yf_d[tt:tt + 128, :], reads=["yf"], writes=["alpha0"])
            if stage <= 3:
                continue
            for pr in range(8):
                P.op("tensor", lambda e, pr=pr: e.matmul(self.psum[7][:, pr:pr + 1], lhsT=sgw[:, pr * 128:(pr + 1) * 128], rhs=ones[:, 0:1], start=True, stop=True), reads=["sgw", "ones"], writes=[("ps", 7)])
            P.op("scalar", lambda e: e.activation(out=Gam, in_=self.psum[7][:, 0:8], func=AF.Exp, scale=-C0), reads=[("ps", 7)], writes=["Gam"])
            for half in range(2):
                ps = self.psum[half]; pk = ("ps", half)
                hs = slice(half * 512, (half + 1) * 512)
                P.op("tensor", lambda e, ps=ps, hs=hs: e.matmul(ps[:], lhsT=INCL, rhs=sgw[:, hs], start=True, stop=True), reads=["masks", "sgw"], writes=[pk])
                P.op("scalar", lambda e, ps=ps, hs=hs: e.activation(out=Ep[:, hs], in_=ps[:], func=AF.Exp, scale=-C0), reads=[pk], writes=[("Ep", half)])
                P.op("vector", lambda e, ps=ps, hs=hs: e.tensor_tensor(out=Ex[:, hs], in0=ps[:], in1=sgw[:, hs], op=ALU.subtract), reads=[pk, "sgw", "tbuf", "nkk"], writes=[("Ex", half)])
                P.op("scalar", lambda e, ps=ps, hs=hs: e.activation(out=Em[:, hs], in_=ps[:], func=AF.Exp, scale=C0), reads=[pk, "alpha", "bb", "kd"], writes=[("Em", half)])
                P.op("scalar", lambda e, hs=hs: e.activation(out=Ex[:, hs], in_=Ex[:, hs], func=AF.Exp, scale=-C0), reads=[("Ex", half)], writes=[("Ex", half)])
            P.op("vector", lambda e: e.tensor_tensor(out=Ex, in0=Ex, in1=nkk, op=ALU.mult), reads=["Ex", "nkk"], writes=["Ex"])
            P.op("gpsimd", lambda e: e.tensor_tensor(out=bb, in0=bb, in1=Em, op=ALU.mult), reads=["bb", "Em"], writes=["bb"])
            P.op("vector", lambda e: e.tensor_tensor(out=kd, in0=kd, in1=Em, op=ALU.mult), reads=["kd", "Em", "kd0"], writes=["kd"])
            P.op("vector", lambda e: e.tensor_tensor(out=Ep, in0=Ep, in1=zr, op=ALU.mult), reads=["Ep", "ztile"], writes=["Ep"])
            if stage <= 4:
                continue
            for xi, (src, sk) in enumerate([(Ex, "Ex"), (bb, "bb"), (kd, "kd"), (Ep, "Ep")]):
                for half in range(2):
                    bank = 2 + (xi * 2 + half) % 2
                    ps = self.psum[bank]; pk = ("ps", bank)
                    for j in range(4):
                        pr = half * 4 + j
                        P.op("tensor", lambda e, ps=ps, j=j, pr=pr, src=src: e.transpose(ps[:, j * 128:(j + 1) * 128], src[:, pr * 128:(pr + 1) * 128], ident[:]), reads=[sk, "ident"], writes=[pk])
                    dst = XT4[xi][:, half * 4:half * 4 + 4, :]
                    if half == 0:
                        P.op("vector", lambda e, ps=ps, dst=dst: e.tensor_copy(out=dst, in_=ps[:].rearrange("p (j t) -> p j t", j=4)), reads=[pk], writes=[("XT4", xi)])
                    else:
                        P.op("scalar", lambda e, ps=ps, dst=dst: e.copy(out=dst, in_=ps[:].rearrange("p (j t) -> p j t", j=4)), reads=[pk], writes=[("XT4", xi)])

            def pairmm(dst, lT, lk, rT, rk, mask, dk, banks, hg, dst2=None, mask2=None, dk2=None):
                for j in range(8):
                    h = hg * 8 + j
                    bank = banks[h % 2]
                    ps = self.psum[bank]
                    P.op("tensor", lambda e, ps=ps, j=j, h=h: e.matmul(ps[:, (j // 2) * 128:(j // 2 + 1) * 128], lhsT=r32(lT[R2[h % 2], h // 2, :]), rhs=r32(rT[R2[h % 2], h // 2, :]), start=True, stop=True),
                         reads=[("XT4", lk), ("XT4", rk)], writes=[("ps", bank)])
                for par in range(2):
                    bank = banks[par]
                    ps = self.psum[bank]
                    P.op("vector", lambda e, ps=ps, par=par: e.tensor_tensor(out=dst[:, par:8:2, :], in0=ps[:].rearrange("p (j t) -> p j t", j=4), in1=mask.unsqueeze(1).broadcast_to([128, 4, 128]), op=ALU.mult),
                         reads=[("ps", bank), "masks"], writes=[dk])
                    if dst2 is not None:
                        P.op("vector", lambda e, ps=ps, par=par: e.tensor_tensor(out=dst2[:, par:8:2, :], in0=ps[:].rearrange("p (j t) -> p j t", j=4), in1=mask2.unsqueeze(1).broadcast_to([128, 4, 128]), op=ALU.mult),
                             reads=[("ps", bank), "masks"], writes=[dk2])

            def headmm(dstbuf, dk, lbuf, lkey, rbuf, rkey, banks, mode, accbuf=None):
                for q in range(2):
                    bank = banks[q]
                    ps = self.psum[bank]
                    for jj in range(4):
                        j = q * 4 + jj
                        P.op("tensor", lambda e, ps=ps, jj=jj, j=j: e.matmul(ps[:, jj * 128:(jj + 1) * 128], lhsT=r32(lbuf[:, j, :]), rhs=r32(rbuf[:, j, :]), start=True, stop=True),
                             reads=[lkey, rkey], writes=[("ps", bank)])
                    dv = dstbuf[:, q * 4:q * 4 + 4, :]
                    pv = ps[:].rearrange("p (j t) -> p j t", j=4)
                    if mode == "copy":
                        if q == 0:
                            P.op("scalar", lambda e, dv=dv, pv=pv: e.copy(out=dv, in_=pv), reads=[("ps", bank)], writes=[(dk, q)])
                        else:
                            P.op("vector", lambda e, dv=dv, pv=pv: e.tensor_copy(out=dv, in_=pv), reads=[("ps", bank)], writes=[(dk, q)])
                    else:
                        P.op("vector", lambda e, dv=dv, pv=pv: e.tensor_tensor(out=dv, in0=pv, in1=dv, op=ALU.add), reads=[("ps", bank), (dk, q)], writes=[(dk, q)])

            Wsb, Vsb = Ep, Ex
            identb8 = ident[:].unsqueeze(1).broadcast_to([128, 8, 128])

            def solve_half(hg):
                hs = slice(hg * 512, (hg + 1) * 512)
                Xd, XTd, X2, XT2, PTm, Lo = big[hg * 6:(hg + 1) * 6]
                kX, kXT, kX2, kXT2, kPT, kLo = (f"b{hg}_{n}" for n in ("X", "XT", "X2", "XT2", "PT", "Lo"))
                Lk, kLk = X2, kX2
                bk = [2, 3] if hg == 0 else [6, 7]
                pairmm(XTd, BtT, 1, AtT, 0, MXd_st, kXT, bk, hg); yield
                pairmm(Xd, AtT, 0, BtT, 1, MXd_ts, kX, bk, hg); yield
                pairmm(Lo, AtT, 0, BtT, 1, MLo_ts, kLo, bk, hg); yield
                pairmm(Lk, KtT, 2, AtT, 0, MS_st, kLk, bk, hg); yield
                ps = self.psum[4 + hg]; pk = ("ps", 4 + hg)
                for j in range(8):
                    h = hg * 8 + j
                    P.op("tensor", lambda e, ps=ps, j=j, h=h: e.matmul(ps[:, j * 64:(j + 1) * 64], lhsT=AtT[R2[h % 2], h // 2, :], rhs=H[R2[h % 2], h // 2, :], start=True, stop=False),
                         reads=[("XT4", 0), "H"], writes=[pk])
                    P.op("tensor", lambda e, ps=ps, j=j, h=h, Lk=Lk: e.matmul(ps[:, j * 64:(j + 1) * 64], lhsT=Lk[:, j, :], rhs=zv[:, h * 64:(h + 1) * 64], start=False, stop=True),
                         reads=[kLk, "ztile"], writes=[pk])
                P.op("scalar", lambda e, ps=ps, hs=hs: e.copy(out=Wsb[:, hs], in_=ps[:]), reads=[pk, "Ep"], writes=[("Ep", hg)])
                yield
                P.op("gpsimd", lambda e, PTm=PTm, XTd=XTd: e.tensor_tensor(out=PTm, in0=XTd, in1=identb8, op=ALU.add), reads=[kXT, "ident"], writes=[kPT])
                cur = (Xd, kX, XTd, kXT)
                nxt = (X2, kX2, XT2, kXT2)
                for lev in range(3):
                    Xc, xk, XTc, xtk = cur
                    Xn, xnk, XTn, xtnk = nxt
                    headmm(Xn, xnk, XTc, xtk, Xc, xk, bk, "copy"); yield
                    if lev < 2:
                        headmm(XTn, xtnk, Xc, xk, XTc, xtk, bk, "copy"); yield
                    headmm(PTm, kPT, Xn, xnk, PTm, kPT, bk, "acc"); yield
                    cur, nxt = nxt, cur
                for j in range(8):
                    h = hg * 8 + j
                    P.op("tensor", lambda e, ps=ps, j=j, h=h, PTm=PTm: e.matmul(ps[:, j * 64:(j + 1) * 64], lhsT=PTm[:, j, :], rhs=Wsb[:, h * 64:(h + 1) * 64], start=True, stop=True),
                         reads=[kPT, ("Ep", hg)], writes=[pk])
                P.op("scalar", lambda e, ps=ps, hs=hs: e.copy(out=Vsb[:, hs], in_=ps[:]), reads=[pk, "Ex"], writes=[("Ex", hg)])
                yield
                MT = Xd
                headmm(MT, kX, Lo, kLo, PTm, kPT, bk, "copy"); yield
                for it in range(7):
                    src = Vsb if it == 0 else U
                    srck = ("Ex", hg) if it == 0 else ("U", hg)
                    for j in range(8):
                        h = hg * 8 + j
                        P.op("tensor", lambda e, ps=ps, j=j, h=h, src=src, MT=MT: e.matmul(ps[:, j * 64:(j + 1) * 64], lhsT=MT[:, j, :], rhs=src[:, h * 64:(h + 1) * 64], start=True, stop=True),
                             reads=[kX, srck], writes=[pk])
                    rd = [pk, ("Ex", hg), ("U", hg)] + (["tbuf", "kd", "kd0"] if it == 0 else [])
                    P.op("vector", lambda e, ps=ps, hs=hs: e.tensor_tensor(out=U[:, hs], in0=ps[:], in1=Vsb[:, hs], op=ALU.add), reads=rd, writes=[("U", hg)])
                    yield
                Mb, Mk = X2, XT2
                pairmm(Mb, BtT, 1, RtT, 3, MI_st, kX2, bk, hg); yield
                pairmm(Mk, KtT, 2, RtT, 3, MI_st, kXT2, bk, hg); yield
                for j in range(8):
                    h = hg * 8 + j
                    P.op("tensor", lambda e, ps=ps, j=j, h=h: e.matmul(ps[:, j * 64:(j + 1) * 64], lhsT=RtT[R2[h % 2], h // 2, :], rhs=H[R2[h % 2], h // 2, :], start=True, stop=False),
                         reads=[("XT4", 3), "H"], writes=[pk])
                    P.op("tensor", lambda e, ps=ps, j=j, h=h, Mb=Mb: e.matmul(ps[:, j * 64:(j + 1) * 64], lhsT=Mb[:, j, :], rhs=U[:, h * 64:(h + 1) * 64], start=False, stop=False),
                         reads=[kX2, ("U", hg)], writes=[pk])
                    P.op("tensor", lambda e, ps=ps, j=j, h=h, Mk=Mk: e.matmul(ps[:, j * 64:(j + 1) * 64], lhsT=Mk[:, j, :], rhs=zv[:, h * 64:(h + 1) * 64], start=False, stop=True),
                         reads=[kXT2, "ztile"], writes=[pk])
                if d == 0:
                    P.op("scalar", lambda e, ps=ps, hs=hs: e.copy(out=yt[:, hs], in_=ps[:]), reads=[pk], writes=[("yt", hg)])
                else:
                    P.op("vector", lambda e, ps=ps, hs=hs: e.tensor_tensor(out=yt[:, hs], in0=ps[:], in1=yf_t[:, hs], op=ALU.add), reads=[pk, "alpha0"], writes=[("yt", hg)])

            gens = [solve_half(0), solve_half(1)]
            while gens:
                for g in list(gens):
                    try:
                        next(g)
                    except StopIteration:
                        gens.remove(g)
            if d == 0:
                P.dma("sync", self.yf_d[tt:tt + 128, :], yt, reads=["yt"], writes=["yf"])
            for half in range(2):
                ps = self.psum[half]; pk = ("ps", half)
                for j in range(4):
                    pr = half * 4 + j
                    cs_ = slice(pr * 128, (pr + 1) * 128)
                    P.op("tensor", lambda e, ps=ps, j=j, cs_=cs_: e.matmul(ps[:, j * 128:(j + 1) * 128], lhsT=r32(bb[:, cs_]), rhs=r32(U[:, cs_]), start=True, stop=False), reads=["bb", "U"], writes=[pk])
                    P.op("tensor", lambda e, ps=ps, j=j, cs_=cs_: e.matmul(ps[:, j * 128:(j + 1) * 128], lhsT=r32(kd[:, cs_]), rhs=r32(zv[:, cs_]), start=False, stop=True), reads=["kd", "ztile"], writes=[pk])
                for hh in range(2):
                    P.op("vector", lambda e, ps=ps, hh=hh, half=half: e.tensor_tensor(out=H[R2[hh], half * 4:half * 4 + 4, :], in0=ps[R2[hh], :].rearrange("p (j c) -> p j c", j=4)[:, :, hh * 64:(hh + 1) * 64],
                                                                                  in1=H[R2[hh], half * 4:half * 4 + 4, :], op=ALU.add), reads=[pk, "H"], writes=["H"])
            P.op("vector", lambda e: e.tensor_tensor(out=H, in0=H, in1=Gam.unsqueeze(2).broadcast_to([128, 8, 64]), op=ALU.mult), reads=["H", "Gam"], writes=["H"])
            if d == 1 and (ctx_out or n >= 2):
                y3 = yt.rearrange("p (h i) -> p h i", h=16)
                P.op("vector", lambda e: e.reduce_sum(out=st[:, 32:48], in_=y3, axis=AX.X), reads=["yt"], writes=["st2"])
                P.op("scalar", lambda e: e.activation(out=Ep, in_=yt, func=AF.Square), reads=["yt", "Ep"], writes=["Ep"])
                P.op("vector", lambda e: e.reduce_sum(out=st[:, 48:64], in_=Ep.rearrange("p (h i) -> p h i", h=16), axis=AX.X), reads=["Ep"], writes=["st2"])
                P.op("vector", lambda e: e.tensor_scalar(out=st[:, 32:48], in0=st[:, 32:48], scalar1=1.0 / 64, scalar2=None, op0=ALU.mult), reads=["st2"], writes=["st2"])
                P.op("vector", lambda e: e.tensor_tensor(out=st[:, 0:16], in0=st[:, 32:48], in1=st[:, 32:48], op=ALU.mult), reads=["st2", "Gam"], writes=["Gam"])
                P.op("vector", lambda e: e.scalar_tensor_tensor(out=st[:, 48:64], in0=st[:, 48:64], scalar=1.0 / 64, in1=st[:, 0:16], op0=ALU.mult, op1=ALU.subtract), reads=["st2", "Gam"], writes=["st2"])
                P.op("vector", lambda e: e.tensor_scalar(out=st[:, 48:64], in0=st[:, 48:64], scalar1=64e-5, scalar2=None, op0=ALU.add), reads=["st2"], writes=["st2"])
                P.op("scalar", lambda e: e.activation(out=st[:, 48:64], in_=st[:, 48:64], func=AF.Sqrt), reads=["st2"], writes=["st2"])
                P.op("vector", lambda e: e.reciprocal(out=st[:, 48:64], in_=st[:, 48:64]), reads=["st2"], writes=["st2"])
                P.op("vector", lambda e: e.tensor_tensor(out=y3, in0=y3, in1=st[:, 32:48].unsqueeze(2).broadcast_to([128, 16, 64]), op=ALU.subtract), reads=["yt", "st2"], writes=["yt"])
                P.op("vector", lambda e: e.tensor_tensor(out=y3, in0=y3, in1=st[:, 48:64].unsqueeze(2).broadcast_to([128, 16, 64]), op=ALU.mult), reads=["yt", "st2"], writes=["yt"])
                P.op("vector", lambda e: e.tensor_tensor(out=yt, in0=yt, in1=lnw_rep, op=ALU.mult), reads=["yt", "lnw_rep"], writes=["yt"])
                P.op("vector", lambda e: e.tensor_tensor(out=yt, in0=yt, in1=lnb_rep, op=ALU.add), reads=["yt", "lnb_rep"], writes=["yt"])
                P.op("vector", lambda e: e.tensor_tensor(out=kd0, in0=kd0, in1=zr, op=ALU.mult), reads=["kd0", "ztile"], writes=["kd0"])
                P.op("vector", lambda e: e.tensor_tensor(out=kd0, in0=kd0, in1=rk_rep, op=ALU.mult), reads=["kd0", "rk_rep"], writes=["kd0"])
                P.op("vector", lambda e: e.reduce_sum(out=st[:, 32:48], in_=kd0.rearrange("p (h j) -> p h j", h=16), axis=AX.X), reads=["kd0", "yt"], writes=["st2"])
                P.op("vector", lambda e: e.tensor_tensor(out=kd0.rearrange("p (h j) -> p h j", h=16), in0=zv.rearrange("p (h j) -> p h j", h=16), in1=st[:, 32:48].unsqueeze(2).broadcast_to([128, 16, 64]), op=ALU.mult),
                     reads=["ztile", "st2"], writes=["kd0"])
                P.op("vector", lambda e: e.tensor_tensor(out=yt, in0=yt, in1=kd0, op=ALU.add), reads=["yt", "kd0"], writes=["yt"])
                for half in range(2):
                    ps = self.psum[half]; pk = ("ps", half)
                    hs = slice(half * 512, (half + 1) * 512)
                    P.op("tensor", lambda e, ps=ps, hs=hs: e.matmul(ps[:], lhsT=sgA, rhs=g2a[:, hs], start=True, stop=False), reads=["sgA", "g2a"], writes=[pk])
                    P.op("tensor", lambda e, ps=ps, hs=hs: e.matmul(ps[:], lhsT=sgB[0:32, :], rhs=g2b[0:32, hs], start=False, stop=True), reads=["sgB", "g2b"], writes=[pk])
                    P.op("vector", lambda e, ps=ps, hs=hs: e.tensor_tensor(out=c_tok[:, hs], in0=ps[:], in1=yt[:, hs], op=ALU.mult), reads=[pk, "yt"], writes=[("c_tok", half), "kd0"])
                for half in range(2):
                    ps = self.psum[2 + half]; pk = ("ps", 2 + half)
                    for j in range(4):
                        fc = half * 4 + j
                        P.op("tensor", lambda e, ps=ps, j=j, fc=fc: e.matmul(ps[:, j * 128:(j + 1) * 128], lhsT=c_tok[:, fc * 128:(fc + 1) * 128], rhs=identb, start=True, stop=True), reads=["c_tok", "identb"], writes=[pk])
                    P.op("scalar", lambda e, ps=ps, half=half: e.copy(out=cT[:, half * 4:half * 4 + 4, :], in_=ps[:].rearrange("p (j t) -> p j t", j=4)), reads=[pk], writes=[("cT", half)])
                P.dma("sync", self.brT_d[2][:, :, tt:tt + 128].rearrange("c p t -> p c t"), cT, reads=["cT"], writes=["brT2"])

    def merge(self, l, blocks_out):
        P = self.P
        hT, xT, lv = self.hT, self.xT, self.lv
        ab, af = self.arena_b, self.arena
        brb = [ab[:, i * 4096:(i + 1) * 4096].rearrange("p (c t) -> p c t", c=DC) for i in range(3)]
        mT = ab[:, 12288:16384].rearrange("p (c t) -> p c t", c=DC)
        NW = 14
        wt = [ab[:, 16384 + i * 1024:16384 + (i + 1) * 1024].rearrange("p (c n) -> p c n", c=DC) for i in range(NW)]
        o = (16384 + NW * 1024) // 2
        sig = [af[:, o + i * 512:o + (i + 1) * 512] for i in range(3)]; o += 1536
        macc = af[:, o:o + 512]; o += 512
        assert o <= 17408
        wi = [0]

        def wload(src):
            i = wi[0] % NW
            wi[0] += 1
            P.dma("gpsimd", wt[i], src.rearrange("(c p) n -> p c n", p=128), writes=[("wt", i)])
            return wt[i], ("wt", i)

        for bi in blocks_out:
            t0, n = TB[bi]
            who = 1 if bi == 0 else 0
            for i in range(3):
                P.dma("sync", brb[i][:, :, 0:n], self.brT_d[i][:, :, t0:t0 + n].rearrange("c p t -> p c t"), reads=[f"brT{i}"], writes=[("brb", i)])
            for j in range(DC):
                for i in range(3):
                    wb, wbk = wload(self.w_br[i][l, :, j * 128:(j + 1) * 128])
                    wg, wgk = wload(self.w_in[l, :, GT_OFF + i * D + j * 128:GT_OFF + i * D + (j + 1) * 128])
                    pb, pg = self.psum[2 * i], self.psum[2 * i + 1]
                    kb, kg = ("ps", 2 * i), ("ps", 2 * i + 1)
                    for k in range(DC):
                        P.op("tensor", lambda e, pb=pb, wb=wb, k=k, i=i, n=n: e.matmul(pb[:, 0:n], lhsT=wb[:, k, :], rhs=brb[i][:, k, 0:n], start=(k == 0), stop=(k == DC - 1)),
                             reads=[wbk, ("brb", i)], writes=[kb])
                    for k in range(DC):
                        P.op("tensor", lambda e, pg=pg, wg=wg, k=k, t0=t0, n=n: e.matmul(pg[:, 0:n], lhsT=wg[:, k, :], rhs=hT[:, k, t0:t0 + n], start=(k == 0), stop=(k == DC - 1)),
                             reads=[wgk, ("hT", bi)], writes=[kg])
                    P.op("scalar", lambda e, pg=pg, i=i, n=n: e.activation(out=sig[i][:, 0:n], in_=pg[:, 0:n], func=AF.Sigmoid), reads=[kg], writes=[("sig", i)])
                    if i == 0:
                        P.op("vector", lambda e, pb=pb, n=n: e.tensor_tensor(out=macc[:, 0:n], in0=pb[:, 0:n], in1=sig[0][:, 0:n], op=ALU.mult),
                             reads=[kb, ("sig", 0)], writes=["macc"])
                    else:
                        P.op("vector", lambda e, pb=pb, i=i, n=n: e.tensor_tensor(out=sig[i][:, 0:n], in0=pb[:, 0:n], in1=sig[i][:, 0:n], op=ALU.mult),
                             reads=[kb, ("sig", i)], writes=[("sig", i)])
                        if i == 1:
                            P.op("vector", lambda e, n=n: e.tensor_tensor(out=macc[:, 0:n], in0=macc[:, 0:n], in1=sig[1][:, 0:n], op=ALU.add),
                                 reads=["macc", ("sig", 1)], writes=["macc"])
                        else:
                            P.op("vector", lambda e, n=n, j=j: e.tensor_tensor(out=mT[:, j, 0:n], in0=macc[:, 0:n], in1=sig[2][:, 0:n], op=ALU.add),
                                 reads=["macc", ("sig", 2)], writes=[("mT", j)])
            for j in range(DC):
                wo, wok = wload(self.w_o[l, :, j * 128:(j + 1) * 128])
                ps = self.psum[6 + j % 2]
                pk = ("ps", 6 + j % 2)
                for k in range(DC):
                    P.op("tensor", lambda e, ps=ps, wo=wo, k=k, n=n: e.matmul(ps[:, 0:n], lhsT=wo[:, k, :], rhs=mT[:, k, 0:n], start=(k == 0), stop=(k == DC - 1)),
                         reads=[wok, ("mT", k)], writes=[pk])
                P.op("vector", lambda e, ps=ps, j=j, t0=t0, n=n, who=who: e.scalar_tensor_tensor(
                    out=xT[:, j, t0:t0 + n], in0=ps[:, 0:n], scalar=lv[:, 2, j, who:who + 1], in1=xT[:, j, t0:t0 + n], op0=ALU.mult, op1=ALU.add),
                    reads=[pk, "lv", ("xT", bi)], writes=[("xT", bi)])


def m_names_ok(inputs):
    return inputs.keys()


def make_inputs_for_core(inputs, b, consts):
    m = {}
    for k, v in inputs.items():
        v = np.asarray(v)
        if k not in m_names_ok(inputs):
            continue
        if k in ("x", "c", "ctx"):
            m[k] = np.ascontiguousarray(v[b])
        else:
            m[k] = np.ascontiguousarray(v)
    m.update(consts)
    return m


_CACHE = {}


def kernel(**inputs):
    if "nc" not in _CACHE:
        m = Model()
        _CACHE["nc"] = m.build()
        _CACHE["names"] = list(m.in_names)
    nc = _CACHE["nc"]
    consts = host_consts()
    in_maps = []
    for b in range(8):
        d = {}
        for k in _CACHE["names"]:
            if k in consts:
                d[k] = consts[k]
            elif k in ("x", "c", "ctx"):
                d[k] = np.ascontiguousarray(np.asarray(inputs[k])[b], dtype=np.float32)
            else:
                d[k] = np.ascontiguousarray(np.asarray(inputs[k]), dtype=np.float32)
        in_maps.append(d)
    res = run_bass_kernel_spmd(nc, in_maps, core_ids=list(range(8)))
    return np.stack([np.asarray(r["out"], dtype=np.float32) for r in res.results], axis=0)
```

```python
import numpy as np
import concourse.bass as bass
import concourse.mybir as mybir
from concourse.bass_utils import run_bass_kernel_spmd
from contextlib import ExitStack

F32 = mybir.dt.float32
BF16 = mybir.dt.bfloat16
AF = mybir.ActivationFunctionType
ALU = mybir.AluOpType
AX = mybir.AxisListType

ENGS = ["tensor", "vector", "scalar", "gpsimd", "sync"]
DMA_QUEUES = ["sync", "scalar", "gpsimd"]
EPOCH = 8000
NPOOL = 6


class Prog:
    def __init__(self, nc, stack):
        self.nc = nc
        self.stack = stack
        self.q = {e: [] for e in ENGS}
        self.sems = {}
        self.tick = {e: 0 for e in ENGS}
        self.waited = {e: {} for e in ENGS}
        self.lastw = {}
        self.readers = {}
        self.dpool = {}
        self.dpool_next = {q: 0 for q in DMA_QUEUES}
        for q in DMA_QUEUES:
            self.dpool[q] = []
            for k in range(NPOOL):
                s = stack.enter_context(nc.semaphore(f"d_{q}_{k}"))
                self.dpool[q].append([s, 0])
        self.n_instr = 0

    def sb(self, name, shape, dt):
        return self.stack.enter_context(self.nc.sbuf_tensor(name, list(shape), dt))

    def ps(self, name, shape, dt=F32):
        return self.stack.enter_context(self.nc.psum_tensor(name, list(shape), dt))

    def _esem(self, e, epoch):
        key = (e, epoch)
        if key not in self.sems:
            self.sems[key] = self.stack.enter_context(self.nc.semaphore(f"s_{e}_{epoch}"))
        return self.sems[key]

    def _deps(self, reads, writes):
        toks = []
        for (n, sl) in reads:
            d = self.lastw.get(n)
            if d:
                for k, t in d.items():
                    if sl is None or k is None or k == sl:
                        toks.append(t)
        for (n, sl) in writes:
            d = self.lastw.get(n)
            if d:
                for k, t in d.items():
                    if sl is None or k is None or k == sl:
                        toks.append(t)
            d = self.readers.get(n)
            if d:
                for k, dd in d.items():
                    if sl is None or k is None or k == sl:
                        toks.extend(dd.values())
        return toks

    def _record(self, reads, writes, tok):
        skey = tok[0]
        for (n, sl) in writes:
            d = self.lastw.setdefault(n, {})
            r = self.readers.setdefault(n, {})
            if sl is None:
                d.clear()
                r.clear()
            else:
                r.pop(sl, None)
            d[sl] = tok
        for (n, sl) in reads:
            dd = self.readers.setdefault(n, {}).setdefault(sl, {})
            dd[skey] = tok

    def _waits(self, eng, toks):
        need = {}
        for (skey, sem, val) in toks:
            if self.waited[eng].get(skey, -1) >= val:
                continue
            if skey not in need or need[skey][1] < val:
                need[skey] = (sem, val)
        out = []
        for skey, (sem, val) in need.items():
            self.waited[eng][skey] = val
            out.append((sem, val))
        return out

    def _k(self, x):
        if isinstance(x, tuple):
            return x if len(x) == 2 else (x[0], tuple(x[1:]))
        return (x, None)

    def op(self, eng, fn, reads=(), writes=(), pe_acc=False):
        reads = [self._k(r) for r in reads]
        writes = [self._k(w) for w in writes]
        if eng != "tensor":
            writes = writes + [r for r in reads if r[0] == "ps" and r not in writes]
        toks = self._deps(reads, writes)
        if eng == "tensor":
            toks = [t for t in toks if t[0][0] != "tensor"]
        waits = self._waits(eng, toks)
        self.tick[eng] += 1
        t = self.tick[eng]
        epoch, val = divmod(t, EPOCH)
        if val == 0:
            epoch, val = epoch - 1, EPOCH
        sem = self._esem(eng, epoch)
        tok = ((eng, epoch), sem, val)
        self._record(reads, writes, tok)
        self.n_instr += 1

        def emit(e, waits=waits, fn=fn, sem=sem):
            for (s, v) in waits:
                e.wait_ge(s, v)
            fn(e).then_inc(sem, 1)
        self.q[eng].append(emit)
        return tok

    def dma(self, queue, out, in_, reads=(), writes=(), **kw):
        reads = [self._k(r) for r in reads]
        writes = [self._k(w) for w in writes]
        toks = self._deps(reads, writes)
        i = self.dpool_next[queue]
        self.dpool_next[queue] = (i + 1) % NPOOL
        slot = self.dpool[queue][i]
        sem, prev = slot
        skey = ("dma", queue, i)
        if prev > 0:
            toks.append((skey, sem, prev))
        waits = self._waits(queue, toks)
        val = prev + 16
        slot[1] = val
        tok = (skey, sem, val)
        self._record(reads, writes, tok)
        self.n_instr += 1

        def emit(e, waits=waits, sem=sem):
            for (s, v) in waits:
                e.wait_ge(s, v)
            e.dma_start(out=out, in_=in_, **kw).then_inc(sem, 16)
        self.q[queue].append(emit)
        return tok

    def wait_all(self, eng, keys):
        toks = []
        toks = self._deps([self._k(k) for k in keys], [])
        waits = self._waits(eng, toks)

        def emit(e, waits=waits):
            for (s, v) in waits:
                e.wait_ge(s, v)
        self.q[eng].append(emit)

    def barrier(self):
        toks = []
        for e in ENGS:
            t = self.tick[e]
            if t == 0:
                continue
            epoch, val = divmod(t, EPOCH)
            if val == 0:
                epoch, val = epoch - 1, EPOCH
            toks.append(((e, epoch), self._esem(e, epoch), val))
        for q in DMA_QUEUES:
            for i, (sem, val) in enumerate(self.dpool[q]):
                if val > 0:
                    toks.append((("dma", q, i), sem, val))
        for e in ENGS:
            mine = [t for t in toks if not (t[0][0] == e and e == "tensor")]
            waits = self._waits(e, mine)

            def emit(eng, waits=waits):
                for (s, v) in waits:
                    eng.wait_ge(s, v)
            self.q[e].append(emit)
        self.lastw.clear()
        self.readers.clear()

    def finish(self):
        nc = self.nc
        with nc.Block() as block:
            @block.tensor
            def _(e):
                for f in self.q["tensor"]:
                    f(e)

            @block.vector
            def _(e):
                for f in self.q["vector"]:
                    f(e)

            @block.scalar
            def _(e):
                for f in self.q["scalar"]:
                    f(e)

            @block.gpsimd
            def _(e):
                for f in self.q["gpsimd"]:
                    f(e)

            @block.sync
            def _(e):
                for f in self.q["sync"]:
                    f(e)


D = 1024
DC = 8
NCTX = 256
NLAT = 2048
T = NCTX + NLAT
DFF = 2816
FC = DFF // 128
PTOT = 11680
GM_OFF, DA_OFF, RW_OFF, GT_OFF = 0, 2048, 5120, 8608
TB = [(0, 256), (256, 512), (768, 512), (1280, 512), (1792, 512)]
RMS_EPS = 1e-6
NL = 4


def host_consts():
    c = {}
    c["ident"] = np.eye(128, dtype=np.float32)
    c["ones"] = np.ones((128, 128), dtype=np.float32)
    bd = np.zeros((128, 128), np.float32); bd[:64, :64] = 1; bd[64:, 64:] = 1
    c["bd64"] = bd
    rp = np.zeros((128, 128), np.float32)
    for m in range(128):
        if (m % 64) < 32:
            rp[m + 32, m] = -1.0
        else:
            rp[m - 32, m] = 1.0
    c["rperm"] = rp
    t = np.arange(NLAT)
    row = (t // 64).astype(np.float32); col = (t % 64).astype(np.float32)
    inv = (10000.0 ** (-np.arange(0, 32, 2, dtype=np.float32) / 32)).astype(np.float32)
    ang = np.concatenate([row[:, None] * inv, col[:, None] * inv], -1)
    ang = np.concatenate([ang, ang], -1)
    cosT = np.ones((128, T), np.float32); sinT = np.zeros((128, T), np.float32)
    cosT[:, NCTX:] = np.tile(np.cos(ang).T, (2, 1)); sinT[:, NCTX:] = np.tile(np.sin(ang).T, (2, 1))
    c["cosT"] = cosT.astype(np.float32); c["sinT"] = sinT.astype(np.float32)
    i = np.arange(128)
    IU_, IL_, SU_, SL_ = (i[:, None] <= i[None, :]), (i[:, None] >= i[None, :]), (i[:, None] < i[None, :]), (i[:, None] > i[None, :])
    BD_ = (i[:, None] // 16) == (i[None, :] // 16)
    c["masks"] = np.stack([IU_, IL_, SU_, SL_, SU_ & BD_, SL_ & BD_, SU_ & ~BD_, SL_ & ~BD_]).astype(np.float32)
    return c


class Model:
    def __init__(self, nlayers=NL, do_mix=True, do_ffn=True, dbg=None, nl_w=NL, branches=(1, 1, 1)):
        self.nlayers = nlayers
        self.nl_w = nl_w
        self.branches = branches
        self.do_mix = do_mix
        self.do_ffn = do_ffn
        self.dbg = dbg or {}

    def declare(self, nc):
        L = self.nl_w
        self.in_names = []

        def I(n, s):
            self.in_names.append(n)
            return nc.dram_tensor(n, list(s), F32, kind="ExternalInput").ap()
        self.x = I("x", [NLAT, D]); self.c = I("c", [D]); self.ctx = I("ctx", [NCTX, D]); self.c_ctx = I("c_ctx", [D])
        self.ada_w = I("ada_w", [L, D, 6 * D]); self.ada_b = I("ada_b", [L, 6 * D])
        self.norm1_g = I("norm1_g", [L, D]); self.norm2_g = I("norm2_g", [L, D])
        self.w_in = I("w_in", [L, D, PTOT])
        self.ffn_wi = I("ffn_wi", [L, D, 2 * DFF]); self.ffn_wo = I("ffn_wo", [L, DFF, D])
        self.gm_v_g = I("gm_v_g", [L, D]); self.gm_ws = I("gm_ws", [L, 8, 128, 128]); self.gm_bs = I("gm_bs", [L, 8, 128])
        self.w_br = [I("w_br_a", [L, D, D]), I("w_br_b", [L, D, D]), I("w_br_c", [L, D, D])]
        self.w_o = I("w_o", [L, D, D])
        self.da_q_g = I("da_q_g", [L, 64]); self.da_k_g = I("da_k_g", [L, 64]); self.da_lambda = I("da_lambda", [L, 4, 64]); self.da_subln_g = I("da_subln_g", [L, 128])
        self.rw_mu = I("rw_mu", [L, 3488]); self.rw_w0 = I("rw_w0", [L, 2, D]); self.rw_w2 = I("rw_w2", [L, 2, 64, D]); self.rw_a0 = I("rw_a0", [L, 2, D]); self.rw_a2 = I("rw_a2", [L, 2, 64, D])
        self.rw_g2 = I("rw_g2", [L, 160, D]); self.rw_kk = I("rw_kk", [L, D]); self.rw_ka = I("rw_ka", [L, D]); self.rw_rk = I("rw_rk", [L, 16, 64]); self.rw_ln_w = I("rw_ln_w", [L, D]); self.rw_ln_b = I("rw_ln_b", [L, D])
        self.masks_d = I("masks", [8, 128, 128])
        S = lambda n, sh, dt=F32: nc.dram_tensor(n, list(sh), dt, kind="Internal").ap()
        SD = (lambda n, sh: nc.dram_tensor(n, list(sh), F32, kind="ExternalOutput").ap()) if self.dbg.get("dump_yf") else S
        self.ztok_d = SD("ztok", [T, 3072]); self.smallT_d = S("smallT", [4, 128, T]); self.yf_d = SD("yf", [T, D]); self.xT_d = S("xTspill", [128, DC * T]); self.hT_d = S("hTspill", [128, DC * T], BF16)
        self.ident_d = I("ident", [128, 128]); self.ones_d = I("ones", [128, 128])
        self.bd64_d = I("bd64", [128, 128]); self.rperm_d = I("rperm", [128, 128]); self.cosT_d = I("cosT", [128, T]); self.sinT_d = I("sinT", [128, T])
        self.brT_d = [nc.dram_tensor(f"brT{i}", [DC, 128, T], BF16, kind="Internal").ap() for i in range(3)]
        self.out = nc.dram_tensor("out", [NLAT, D], F32, kind="ExternalOutput").ap()
        if self.dbg.get("dump_ctx"):
            self.outc = nc.dram_tensor("outc", [NCTX, D], F32, kind="ExternalOutput").ap()

    def build(self):
        nc = bass.Bass("TRN2", target_bir_lowering=False)
        self.nc = nc
        self.declare(nc)
        with ExitStack() as st:
            P = Prog(nc, st)
            self.P = P
            self.alloc()
            self.setup()
            for l in range(self.nlayers):
                self.layer(l)
            self.final()
            P.finish()
        return nc

    def alloc(self):
        P = self.P
        self.xT = P.sb("xT", [128, DC, T], F32)
        self.hT = P.sb("hT", [128, DC, T], BF16)
        self.ident = P.sb("identt", [128, 128], F32)
        self.ones = P.sb("onest", [128, 128], F32)
        self.vec = P.sb("vec", [128, 64], F32)
        self.modv = P.sb("modv", [128, 6, DC, 2], F32)
        self.lv = P.sb("lv", [128, 6, DC, 2], F32)
        self.cs = P.sb("cs", [128, DC, 2], F32)
        self.rstd = P.sb("rstd", [128, 512], F32)
        self.ntmp = [P.sb(f"ntmp{i}", [128, 512], F32) for i in range(2)]
        self.arena = P.sb("arena", [128, 17408], F32)
        self.arena_b = self.arena[:].bitcast(BF16)
        self.psum = [P.ps(f"ps{i}", [128, 512]) for i in range(8)]

    def setup(self):
        P = self.P
        xT, ident = self.xT, self.ident
        P.dma("sync", ident[:], self.ident_d[:, :], writes=["ident"])
        P.dma("sync", self.ones[:], self.ones_d[:, :], writes=["ones"])
        P.dma("sync", self.cs[:, :, 0], self.c.rearrange("(c p) -> p c", p=128), writes=["cs"], allow_slow_non_contiguous=True)
        P.dma("sync", self.cs[:, :, 1], self.c_ctx.rearrange("(c p) -> p c", p=128), writes=["cs"], allow_slow_non_contiguous=True)
        P.op("scalar", lambda e: e.activation(out=self.cs[:], in_=self.cs[:], func=AF.Silu), reads=["cs"], writes=["cs"])
        stage = [self.arena[:, i * 1024:(i + 1) * 1024] for i in range(2)]
        for ti in range(T // 128):
            src = self.ctx[ti * 128:(ti + 1) * 128, :] if ti < 2 else self.x[(ti - 2) * 128:(ti - 1) * 128, :]
            sg = stage[ti % 2]
            P.dma("sync", sg, src, writes=[("stage", ti % 2)])
            for half in range(2):
                ps = self.psum[(ti * 2 + half) % 8]
                pk = ("ps", (ti * 2 + half) % 8)
                for j in range(4):
                    dc = half * 4 + j
                    P.op("tensor", lambda e, ps=ps, sg=sg, dc=dc, j=j: e.transpose(ps[:, j * 128:(j + 1) * 128], sg[:, dc * 128:(dc + 1) * 128], ident[:]),
                         reads=[("stage", ti % 2), "ident"], writes=[pk])
                eng = "vector" if half == 0 else "scalar"
                dst = xT[:, half * 4:half * 4 + 4, ti * 128:(ti + 1) * 128]
                srcp = ps[:].rearrange("p (j t) -> p j t", j=4)
                if eng == "vector":
                    P.op("vector", lambda e, dst=dst, srcp=srcp: e.tensor_copy(out=dst, in_=srcp), reads=[pk], writes=[("xT", ti)])
                else:
                    P.op("scalar", lambda e, dst=dst, srcp=srcp: e.copy(out=dst, in_=srcp), reads=[pk], writes=[("xT", ti)])
        P.barrier()

    def final(self):
        P = self.P
        xT, ident = self.xT, self.ident
        stage = [self.arena[:, i * 1024:(i + 1) * 1024] for i in range(2)]
        for ti in range(0 if self.dbg.get("dump_ctx") else 2, T // 128):
            sg = stage[ti % 2]
            for half in range(2):
                ps = self.psum[(ti * 2 + half) % 8]
                pk = ("ps", (ti * 2 + half) % 8)
                for j in range(4):
                    dc = half * 4 + j
                    P.op("tensor", lambda e, ps=ps, dc=dc, j=j, ti=ti: e.transpose(ps[:, j * 128:(j + 1) * 128], xT[:, dc, ti * 128:(ti + 1) * 128], ident[:]),
                         reads=["xT", "ident"], writes=[pk])
                dst = sg[:, half * 512:(half + 1) * 512]
                if half == 0:
                    P.op("vector", lambda e, dst=dst, ps=ps: e.tensor_copy(out=dst, in_=ps[:]), reads=[pk], writes=[("stage", ti % 2)])
                else:
                    P.op("scalar", lambda e, dst=dst, ps=ps: e.copy(out=dst, in_=ps[:]), reads=[pk], writes=[("stage", ti % 2)])
            dst = self.outc[ti * 128:(ti + 1) * 128, :] if ti < 2 else self.out[(ti - 2) * 128:(ti - 1) * 128, :]
            P.dma("sync", dst, sg, reads=[("stage", ti % 2)], writes=["out"])
        P.wait_all("sync", ["out"])

    def load_vec(self, dst, src_1d, key):
        self.P.dma("sync", dst, src_1d.rearrange("(c p) -> p c", p=128), writes=[key], allow_slow_non_contiguous=True)

    def ada(self, l):
        P = self.P
        NB = 256
        wt = [self.arena[:, i * DC * NB:(i + 1) * DC * NB].rearrange("p (c n) -> p c n", c=DC) for i in range(2)]
        brow = self.arena[0:1, 2 * DC * NB:2 * DC * NB + 6 * D]
        P.dma("sync", brow, self.ada_b[l:l + 1, :], writes=["brow"])
        ps = self.psum[0]
        for blk in range(6 * D // NB):
            w = wt[blk % 2]
            wk = ("adaw", blk % 2)
            P.dma("sync", w, self.ada_w[l, :, blk * NB:(blk + 1) * NB].rearrange("(c p) n -> p c n", p=128), writes=[wk])
            for s in range(NB // 128):
                fch = blk * (NB // 128) + s
                o = ps[:, fch * 2:fch * 2 + 2]
                for dc in range(DC):
                    P.op("tensor", lambda e, o=o, w=w, s=s, dc=dc: e.matmul(o, lhsT=w[:, dc, s * 128:(s + 1) * 128], rhs=self.cs[:, dc, :], start=(dc == 0), stop=False),
                         reads=[wk, "cs"], writes=[("ps", 0)])
                P.op("tensor", lambda e, o=o, fch=fch: e.matmul(o, lhsT=brow[:, fch * 128:(fch + 1) * 128], rhs=self.ones[0:1, 0:2], start=False, stop=True),
                     reads=["brow", "ones"], writes=[("ps", 0)])
        P.op("vector", lambda e: e.tensor_copy(out=self.modv[:].rearrange("p k j w -> p (k j w)"), in_=ps[:, 0:96]), reads=[("ps", 0)], writes=["modv"])
        vec = self.vec
        self.load_vec(vec[:, 0:8], self.norm1_g[l], "vec")
        self.load_vec(vec[:, 8:16], self.norm2_g[l], "vec")
        lv, modv = self.lv, self.modv
        for (dst, sc, g0) in [(0, 1, 0), (3, 4, 8)]:
            P.op("vector", lambda e, dst=dst, sc=sc, g0=g0: e.scalar_tensor_tensor(
                out=lv[:, dst], in0=modv[:, sc], scalar=1.0, in1=vec[:, g0:g0 + 8].unsqueeze(2).broadcast_to([128, 8, 2]),
                op0=ALU.add, op1=ALU.mult), reads=["modv", "vec"], writes=["lv"])
        for (dst, src) in [(1, 0), (2, 2), (4, 3), (5, 5)]:
            P.op("vector", lambda e, dst=dst, src=src: e.tensor_copy(out=lv[:, dst], in_=modv[:, src]), reads=["modv"], writes=["lv"])
        P.barrier()

    def norm_mod(self, which, blocks):
        P = self.P
        xT, hT, lv = self.xT, self.hT, self.lv
        for bi in blocks:
            t0, n = TB[bi]
            who = 1 if bi == 0 else 0
            ps = self.psum[bi % 2]
            pk = ("ps", bi % 2)
            for dc in range(DC):
                tmp = self.ntmp[dc % 2]
                P.op("scalar", lambda e, tmp=tmp, dc=dc, t0=t0, n=n: e.activation(out=tmp[:, 0:n], in_=xT[:, dc, t0:t0 + n], func=AF.Square),
                     reads=[("xT", bi)], writes=[("ntmp", dc % 2)])
                P.op("tensor", lambda e, tmp=tmp, dc=dc, n=n, ps=ps: e.matmul(ps[:, 0:n], lhsT=self.ones[:], rhs=tmp[:, 0:n], start=(dc == 0), stop=(dc == DC - 1)),
                     reads=[("ntmp", dc % 2), "ones"], writes=[pk])
            rstd = self.rstd
            P.op("vector", lambda e, n=n, ps=ps: e.tensor_scalar(out=rstd[:, 0:n], in0=ps[:, 0:n], scalar1=1.0 / D, scalar2=RMS_EPS, op0=ALU.mult, op1=ALU.add),
                 reads=[pk], writes=["rstd"])
            P.op("scalar", lambda e, n=n: e.activation(out=rstd[:, 0:n], in_=rstd[:, 0:n], func=AF.Sqrt), reads=["rstd"], writes=["rstd"])
            P.op("vector", lambda e, n=n: e.reciprocal(out=rstd[:, 0:n], in_=rstd[:, 0:n]), reads=["rstd"], writes=["rstd"])
            for dc in range(DC):
                tmp = self.ntmp[dc % 2]
                P.op("vector", lambda e, tmp=tmp, dc=dc, t0=t0, n=n: e.tensor_tensor(out=tmp[:, 0:n], in0=xT[:, dc, t0:t0 + n], in1=rstd[:, 0:n], op=ALU.mult),
                     reads=[("xT", bi), "rstd"], writes=[("ntmp", dc % 2)])
                P.op("scalar", lambda e, tmp=tmp, dc=dc, t0=t0, n=n, who=who: e.activation(
                    out=hT[:, dc, t0:t0 + n], in_=tmp[:, 0:n], func=AF.Identity,
                    scale=lv[:, which * 3 + 0, dc, who:who + 1], bias=lv[:, which * 3 + 1, dc, who:who + 1]),
                    reads=[("ntmp", dc % 2), "lv"], writes=[("hT", bi)])

    def ffn(self, l, blocks):
        P = self.P
        xT, hT, lv = self.xT, self.hT, self.lv
        ab = self.arena_b
        act = [ab[:, g * FC * 512:(g + 1) * FC * 512].rearrange("p (f t) -> p f t", f=FC) for g in range(2)]
        o = 2 * FC * 512
        wi_t = [ab[:, o + i * 2048:o + (i + 1) * 2048].rearrange("p (c n) -> p c n", c=DC) for i in range(2)]
        o += 2 * 2048
        wo_t = [ab[:, o + i * FC * 128:o + (i + 1) * FC * 128].rearrange("p (f n) -> p f n", f=FC) for i in range(2)]
        o += 2 * FC * 128
        assert o <= 34816, o
        sg = [self.ntmp[0], self.ntmp[1]]
        groups = [blocks[i:i + 2] for i in range(0, len(blocks), 2)]
        for grp in groups:
            for f in range(FC):
                w = wi_t[f % 2]
                wk = ("wi", f % 2)
                P.dma("gpsimd", w[:, :, 0:128], self.ffn_wi[l, :, f * 128:(f + 1) * 128].rearrange("(c p) n -> p c n", p=128), writes=[wk])
                P.dma("gpsimd", w[:, :, 128:256], self.ffn_wi[l, :, DFF + f * 128:DFF + (f + 1) * 128].rearrange("(c p) n -> p c n", p=128), writes=[wk])
                for gi, bi in enumerate(grp):
                    t0, n = TB[bi]
                    b0 = (f % 2) * 4 + gi * 2
                    pg, pu = self.psum[b0], self.psum[b0 + 1]
                    kg, ku = ("ps", b0), ("ps", b0 + 1)
                    for (ps, pk, c0) in [(pg, kg, 0), (pu, ku, 128)]:
                        for dc in range(DC):
                            P.op("tensor", lambda e, ps=ps, w=w, c0=c0, dc=dc, t0=t0, n=n: e.matmul(ps[:, 0:n], lhsT=w[:, dc, c0:c0 + 128], rhs=hT[:, dc, t0:t0 + n], start=(dc == 0), stop=(dc == DC - 1)),
                                 reads=[wk, ("hT", bi)], writes=[pk])
                    s_ = sg[gi]
                    a_ = act[gi]
                    P.op("scalar", lambda e, s_=s_, pg=pg, n=n: e.activation(out=s_[:, 0:n], in_=pg[:, 0:n], func=AF.Silu), reads=[kg], writes=[("ntmp", gi)])
                    P.op("vector", lambda e, s_=s_, pu=pu, n=n, f=f, a_=a_: e.tensor_tensor(out=a_[:, f, 0:n], in0=s_[:, 0:n], in1=pu[:, 0:n], op=ALU.mult),
                         reads=[ku, ("ntmp", gi)], writes=[("act", gi, f)])
            for j in range(DC):
                w = wo_t[j % 2]
                wk = ("wo", j % 2)
                P.dma("gpsimd", w, self.ffn_wo[l, :, j * 128:(j + 1) * 128].rearrange("(f p) n -> p f n", p=128), writes=[wk])
                for gi, bi in enumerate(grp):
                    t0, n = TB[bi]
                    who = 1 if bi == 0 else 0
                    b0 = (j % 2) * 2 + gi
                    ps = self.psum[b0]
                    pk = ("ps", b0)
                    a_ = act[gi]
                    for f in range(FC):
                        P.op("tensor", lambda e, ps=ps, w=w, f=f, n=n, a_=a_: e.matmul(ps[:, 0:n], lhsT=w[:, f, :], rhs=a_[:, f, 0:n], start=(f == 0), stop=(f == FC - 1)),
                             reads=[wk, ("act", gi, f)], writes=[pk])
                    P.op("vector", lambda e, ps=ps, j=j, t0=t0, n=n, who=who: e.scalar_tensor_tensor(
                        out=xT[:, j, t0:t0 + n], in0=ps[:, 0:n], scalar=lv[:, 5, j, who:who + 1], in1=xT[:, j, t0:t0 + n], op0=ALU.mult, op1=ALU.add),
                        reads=[pk, "lv", ("xT", bi)], writes=[("xT", bi)])

    def layer(self, l):
        P = self.P
        last = (l == NL - 1) and not self.dbg.get('ctx_always')
        blocks_all = list(range(5))
        blocks_out = [1, 2, 3, 4] if last else blocks_all
        self.ada(l)
        if self.do_mix:
            self.norm_mod(0, blocks_all)
            P.barrier()
            self.mixers(l, blocks_out)
            P.barrier()
        if self.do_ffn:
            self.norm_mod(1, blocks_out)
            P.barrier()
            self.ffn(l, blocks_out)
            P.barrier()

    def mixers(self, l, blocks_out):
        P = self.P
        if self.branches[0]:
            self.gmlp(l, blocks_out)
        else:
            self.zero_branch(0)
        P.barrier()
        if self.branches[1]:
            self.attention(l, blocks_out)
        else:
            self.zero_branch(1)
        P.barrier()
        if self.branches[2]:
            self.rwkv(l, blocks_out)
        else:
            self.zero_branch(2)
        P.barrier()
        self.merge(l, blocks_out)

    def zero_branch(self, i):
        P = self.P
        z = self.arena_b[:, 0:T]
        P.op("vector", lambda e: e.memset(z, 0.0), writes=["z"])
        for k in range(DC):
            P.dma("sync", self.brT_d[i][k], z, reads=["z"], writes=[f"brT{i}"])

    def gmlp(self, l, blocks_out):
        P = self.P
        hT = self.hT
        ab, af = self.arena_b, self.arena
        Wu = ab[:, 0:8192].rearrange("p (c n) -> p c n", c=DC)
        Wv = ab[:, 8192:16384].rearrange("p (c n) -> p c n", c=DC)
        wsT = ab[:, 16384:17408].rearrange("p (g t) -> p g t", g=8)
        vn = ab[:, 17408:18432]
        aT = [ab[:, 18432 + i * 1024:18432 + (i + 1) * 1024].rearrange("p (g t) -> p g t", g=8) for i in range(2)]
        o = 20480 // 2
        gv = af[:, o:o + 1024]; o += 1024
        gu = [af[:, o + i * 512:o + (i + 1) * 512] for i in range(2)]; o += 1024
        ft = [af[:, o + i * 512:o + (i + 1) * 512] for i in range(2)]; o += 1024
        vg_rep = af[:, o:o + 1024]; o += 1024
        bs_rep = af[:, o:o + 1024]; o += 1024
        ws_ld = af[:, o:o + 1024].rearrange("p (g s) -> p g s", g=8); o += 1024
        sq = af[:, o:o + 1024]; o += 1024
        st = self.vec[:, 32:40]
        assert o <= 17408
        for c in range(2):
            P.dma("gpsimd", Wu[:, :, c * 512:(c + 1) * 512], self.w_in[l, :, GM_OFF + c * 512:GM_OFF + (c + 1) * 512].rearrange("(c p) n -> p c n", p=128), writes=["Wu"])
            P.dma("gpsimd", Wv[:, :, c * 512:(c + 1) * 512], self.w_in[l, :, GM_OFF + D + c * 512:GM_OFF + D + (c + 1) * 512].rearrange("(c p) n -> p c n", p=128), writes=["Wv"])
        P.dma("sync", vg_rep, self.gm_v_g[l].partition_broadcast(128), writes=["vg_rep"])
        P.dma("sync", bs_rep, self.gm_bs[l].rearrange("g t -> (g t)").partition_broadcast(128), writes=["bs_rep"])
        P.dma("sync", ws_ld, self.gm_ws[l].rearrange("g t s -> t g s"), writes=["ws_ld"])
        for half in range(2):
            ps = self.psum[half]
            for j in range(4):
                g = half * 4 + j
                P.op("tensor", lambda e, ps=ps, j=j, g=g: e.transpose(ps[:, j * 128:(j + 1) * 128], ws_ld[:, g, :], self.ident[:]),
                     reads=["ws_ld", "ident"], writes=[("ps", half)])
            P.op("vector", lambda e, ps=ps, half=half: e.tensor_copy(out=wsT[:, half * 4:half * 4 + 4, :], in_=ps[:].rearrange("p (j t) -> p j t", j=4)),
                 reads=[("ps", half)], writes=["wsT"])
        tiles = []
        for bi in blocks_out:
            t0, n = TB[bi]
            tiles += [(bi, t0 + i * 128) for i in range(n // 128)]
        for it, (bi, tt) in enumerate(tiles):
            for half in range(2):
                ps = self.psum[half]
                for dc in range(DC):
                    P.op("tensor", lambda e, ps=ps, dc=dc, tt=tt, half=half: e.matmul(ps[:], lhsT=hT[:, dc, tt:tt + 128], rhs=Wv[:, dc, half * 512:(half + 1) * 512], start=(dc == 0), stop=(dc == DC - 1)),
                         reads=["Wv", ("hT", bi)], writes=[("ps", half)])
                P.op("scalar", lambda e, ps=ps, half=half: e.activation(out=gv[:, half * 512:(half + 1) * 512], in_=ps[:], func=AF.Gelu),
                     reads=[("ps", half)], writes=[("gv", half)])
            P.op("scalar", lambda e: e.activation(out=sq, in_=gv, func=AF.Square, accum_out=st[:, 0:1]), reads=["gv"], writes=["sq", "st"])
            P.op("vector", lambda e: e.tensor_scalar(out=st[:, 1:2], in0=st[:, 0:1], scalar1=1.0 / D, scalar2=RMS_EPS, op0=ALU.mult, op1=ALU.add), reads=["st"], writes=["st"])
            P.op("scalar", lambda e: e.activation(out=st[:, 2:3], in_=st[:, 1:2], func=AF.Sqrt), reads=["st"], writes=["st"])
            P.op("vector", lambda e: e.reciprocal(out=st[:, 3:4], in_=st[:, 2:3]), reads=["st"], writes=["st"])
            P.op("vector", lambda e: e.scalar_tensor_tensor(out=vn, in0=gv, scalar=st[:, 3:4], in1=vg_rep, op0=ALU.mult, op1=ALU.mult),
                 reads=["gv", "st", "vg_rep"], writes=["vn"])
            a_t = aT[it % 2]
            ak = ("aT", it % 2)
            for half in range(2):
                pu, pf = self.psum[2 + half], self.psum[4 + half]
                ku, kf = ("ps", 2 + half), ("ps", 4 + half)
                for j in range(4):
                    g = half * 4 + j
                    for dc in range(DC):
                        P.op("tensor", lambda e, pu=pu, j=j, g=g, dc=dc, tt=tt: e.matmul(pu[:, j * 128:(j + 1) * 128], lhsT=Wu[:, dc, g * 128:(g + 1) * 128], rhs=hT[:, dc, tt:tt + 128], start=(dc == 0), stop=(dc == DC - 1)),
                             reads=["Wu", ("hT", bi)], writes=[ku])
                    P.op("tensor", lambda e, pf=pf, j=j, g=g: e.matmul(pf[:, j * 128:(j + 1) * 128], lhsT=vn[:, g * 128:(g + 1) * 128], rhs=wsT[:, g, :], start=True, stop=True),
                         reads=["vn", "wsT"], writes=[kf])
                P.op("scalar", lambda e, pu=pu, half=half: e.activation(out=gu[half], in_=pu[:], func=AF.Gelu), reads=[ku], writes=[("gu", half)])
                P.op("vector", lambda e, pf=pf, half=half: e.tensor_tensor(out=ft[half], in0=pf[:], in1=bs_rep[:, half * 512:(half + 1) * 512], op=ALU.add),
                     reads=[kf, "bs_rep"], writes=[("ft", half)])
                P.op("vector", lambda e, half=half, a_t=a_t: e.tensor_tensor(out=a_t[:, half * 4:half * 4 + 4, :], in0=ft[half].rearrange("p (j t) -> p j t", j=4), in1=gu[half].rearrange("p (j t) -> p j t", j=4), op=ALU.mult),
                     reads=[("ft", half), ("gu", half)], writes=[ak])
            P.dma("sync", self.brT_d[0][:, :, tt:tt + 128].rearrange("g p t -> p g t"), a_t, reads=[ak], writes=["brT0"])

    def attention(self, l, blocks_out):
        import math
        P = self.P
        hT = self.hT
        ab, af = self.arena_b, self.arena
        lam_init = 0.8 - 0.6 * math.exp(-0.3 * l)
        ctx_out = 0 in blocks_out
        o = 0
        def F(n):
            nonlocal o
            r = af[:, o:o + n]; o += n
            return r
        def B(n):
            nonlocal o
            r = ab[:, 2 * o:2 * o + n]; o += (n + 1) // 2
            return r
        cosT = F(T); sinT = F(T); qf = F(T)
        qT = B(T); kT = B(T)
        Vext = B(18 * 130).rearrange("p (k e) -> p k e", k=18)
        Wq = B(1024).rearrange("p (c n) -> p c n", c=DC); Wk = B(1024).rearrange("p (c n) -> p c n", c=DC); Wv = B(1024).rearrange("p (c n) -> p c n", c=DC)
        Et4 = [B(512) for _ in range(4)]
        tmp = [F(512) for _ in range(2)]
        o_sb = F(512).rearrange("p (s e) -> p s e", s=4)
        b_all = F(512).rearrange("p (s e) -> p s e", s=4)
        bT = B(512)
        g_rep = F(128)
        lamt = F(256)
        bd64 = F(128); rperm = F(128)
        zb = B(512)
        sc = F(32)
        gq = sc[:, 0:1]; gk = sc[:, 1:2]; lamv = sc[:, 2:3]; nlam = sc[:, 3:4]
        assert o <= 17408, o
        P.dma("sync", cosT, self.cosT_d[:, :], writes=["cosT"])
        P.dma("sync", sinT, self.sinT_d[:, :], writes=["sinT"])
        P.dma("sync", bd64, self.bd64_d[:, :], writes=["bd64"])
        P.dma("sync", rperm, self.rperm_d[:, :], writes=["rperm"])
        for h2 in range(2):
            P.dma("sync", sc[h2 * 64:(h2 + 1) * 64, 0:1], self.da_q_g[l].rearrange("(d o) -> d o", o=1), writes=["sc"])
            P.dma("sync", sc[h2 * 64:(h2 + 1) * 64, 1:2], self.da_k_g[l].rearrange("(d o) -> d o", o=1), writes=["sc"])
        P.dma("sync", g_rep, self.da_subln_g[l].partition_broadcast(128), writes=["g_rep"])
        P.dma("sync", lamt, self.da_lambda[l].rearrange("a d -> (a d)").partition_broadcast(128), writes=["lamt"])
        P.op("vector", lambda e: e.memset(zb, 0.0), writes=["zb"])
        P.op("vector", lambda e: e.memset(Vext[:, :, 128:129], 1.0), writes=["Vext"])
        P.op("vector", lambda e: e.tensor_scalar(out=g_rep, in0=g_rep, scalar1=(1.0 - lam_init), scalar2=None, op0=ALU.mult), reads=["g_rep"], writes=["g_rep"])
        for i in range(2):
            P.op("vector", lambda e, i=i: e.tensor_tensor(out=tmp[0][:, i * 64:(i + 1) * 64], in0=lamt[:, i * 128:i * 128 + 64], in1=lamt[:, i * 128 + 64:i * 128 + 128], op=ALU.mult),
                 reads=["lamt"], writes=[("tmp", 0)])
            P.op("vector", lambda e, i=i: e.reduce_sum(out=sc[:, 4 + i:5 + i], in_=tmp[0][:, i * 64:(i + 1) * 64], axis=AX.X), reads=[("tmp", 0)], writes=["sc"])
        P.op("scalar", lambda e: e.activation(out=sc[:, 6:8], in_=sc[:, 4:6], func=AF.Exp), reads=["sc"], writes=["sc"])
        P.op("vector", lambda e: e.tensor_tensor(out=sc[:, 8:9], in0=sc[:, 6:7], in1=sc[:, 7:8], op=ALU.subtract), reads=["sc"], writes=["sc"])
        P.op("vector", lambda e: e.tensor_scalar(out=nlam, in0=sc[:, 8:9], scalar1=lam_init, scalar2=-1.0, op0=ALU.add, op1=ALU.mult), reads=["sc"], writes=["sc"])

        qblocks = [(TB[bi][0], TB[bi][1], list(range(18))) for bi in (1, 2, 3, 4)]
        if ctx_out:
            qblocks.append((0, 256, [0, 1]))
        def load_w(hd):
            for (W, off, nm) in [(Wq, 0, "Wq"), (Wk, 1024, "Wk"), (Wv, 2048, "Wv")]:
                c0 = DA_OFF + off + hd * 128
                P.dma("gpsimd", W, self.w_in[l, :, c0:c0 + 128].rearrange("(c p) n -> p c n", p=128), writes=[nm])

        load_w(0)
        for hd in range(8):
            for (W, nm, gcol, dst, dnm) in [(Wq, "Wq", gq, qT, "qT"), (Wk, "Wk", gk, kT, "kT")]:
                for bi in range(5):
                    t0, n = TB[bi]
                    ps = self.psum[4 + bi % 2]; pk = ("ps", 4 + bi % 2)
                    for dc in range(DC):
                        P.op("tensor", lambda e, ps=ps, W=W, dc=dc, t0=t0, n=n: e.matmul(ps[:, 0:n], lhsT=W[:, dc, :], rhs=hT[:, dc, t0:t0 + n], start=(dc == 0), stop=(dc == DC - 1)),
                             reads=[nm, ("hT", bi)], writes=[pk])
                    P.op("scalar", lambda e, ps=ps, t0=t0, n=n: e.copy(out=qf[:, t0:t0 + n], in_=ps[:, 0:n]), reads=[pk], writes=[("qf", bi)])
                    tq = tmp[bi % 2]; tk = ("tmp", bi % 2)
                    P.op("scalar", lambda e, tq=tq, t0=t0, n=n: e.activation(out=tq[:, 0:n], in_=qf[:, t0:t0 + n], func=AF.Square), reads=[("qf", bi)], writes=[tk])
                    p2 = self.psum[6]; k2 = ("ps", 6)
                    P.op("tensor", lambda e, p2=p2, tq=tq, n=n: e.matmul(p2[:, 0:n], lhsT=bd64, rhs=tq[:, 0:n], start=True, stop=True), reads=[tk, "bd64"], writes=[k2])
                    P.op("vector", lambda e, p2=p2, tq=tq, n=n: e.tensor_scalar(out=tq[:, 0:n], in0=p2[:, 0:n], scalar1=1.0 / 64, scalar2=RMS_EPS, op0=ALU.mult, op1=ALU.add), reads=[k2], writes=[tk])
                    P.op("scalar", lambda e, tq=tq, n=n: e.activation(out=tq[:, 0:n], in_=tq[:, 0:n], func=AF.Sqrt), reads=[tk], writes=[tk])
                    P.op("vector", lambda e, tq=tq, n=n: e.reciprocal(out=tq[:, 0:n], in_=tq[:, 0:n]), reads=[tk], writes=[tk])
                    P.op("vector", lambda e, tq=tq, t0=t0, n=n, gcol=gcol: e.scalar_tensor_tensor(out=qf[:, t0:t0 + n], in0=qf[:, t0:t0 + n], scalar=gcol, in1=tq[:, 0:n], op0=ALU.mult, op1=ALU.mult),
                         reads=[("qf", bi), tk, "sc"], writes=[("qf", bi)])
                    p3 = self.psum[7]; k3 = ("ps", 7)
                    P.op("tensor", lambda e, p3=p3, t0=t0, n=n: e.matmul(p3[:, 0:n], lhsT=rperm, rhs=qf[:, t0:t0 + n], start=True, stop=True), reads=[("qf", bi), "rperm"], writes=[k3])
                    P.op("vector", lambda e, p3=p3, tq=tq, t0=t0, n=n: e.tensor_tensor(out=tq[:, 0:n], in0=p3[:, 0:n], in1=sinT[:, t0:t0 + n], op=ALU.mult), reads=[k3, "sinT"], writes=[tk])
                    P.op("vector", lambda e, t0=t0, n=n: e.tensor_tensor(out=qf[:, t0:t0 + n], in0=qf[:, t0:t0 + n], in1=cosT[:, t0:t0 + n], op=ALU.mult), reads=[("qf", bi), "cosT"], writes=[("qf", bi)])
                    P.op("vector", lambda e, tq=tq, dst=dst, t0=t0, n=n: e.tensor_tensor(out=dst[:, t0:t0 + n], in0=qf[:, t0:t0 + n], in1=tq[:, 0:n], op=ALU.add), reads=[("qf", bi), tk], writes=[(dnm, bi)])
            for g4 in range(5):
                tiles = list(range(g4 * 4, min(18, g4 * 4 + 4)))
                ps = self.psum[4 + g4 % 2]; pk = ("ps", 4 + g4 % 2)
                for j, ti in enumerate(tiles):
                    for dc in range(DC):
                        P.op("tensor", lambda e, ps=ps, j=j, ti=ti, dc=dc: e.matmul(ps[:, j * 128:(j + 1) * 128], lhsT=hT[:, dc, ti * 128:(ti + 1) * 128], rhs=Wv[:, dc, :], start=(dc == 0), stop=(dc == DC - 1)),
                             reads=["Wv", "hT"], writes=[pk])
                nt = len(tiles)
                P.op("vector", lambda e, ps=ps, g4=g4, nt=nt: e.tensor_copy(out=Vext[:, g4 * 4:g4 * 4 + nt, 0:128], in_=ps[:, 0:nt * 128].rearrange("p (j e) -> p j e", j=nt)),
                     reads=[pk], writes=["Vext"])
            if hd + 1 < 8:
                load_w(hd + 1)
            for (q0, nq, kts) in qblocks:
                nsub = nq // 128
                accs2 = [[self.psum[2 + 2 * c], self.psum[3 + 2 * c]][:(nsub + 1) // 2] for c in range(2)]
                for c in range(2):
                    for ai, A in enumerate(accs2[c]):
                        P.op("tensor", lambda e, A=A: e.matmul(A[:, 0:512], lhsT=zb[0:1, 0:128], rhs=zb[0:1, 0:512], start=True, stop=False, skip_group_check=True), reads=["zb"], writes=[("ps", 2 + 2 * c + ai)])
                sbank = [[0, 1], [6, 7]]

                def emit_qk(ki, q0=q0, nq=nq, kts=kts):
                    kt = kts[ki]
                    for c in range(2):
                        bnk = sbank[ki % 2][c]
                        pS = self.psum[bnk]
                        P.op("tensor", lambda e, pS=pS, kt=kt, c=c: e.matmul(pS[:, 0:nq], lhsT=kT[c * 64:(c + 1) * 64, kt * 128:(kt + 1) * 128], rhs=qT[c * 64:(c + 1) * 64, q0:q0 + nq], start=True, stop=True),
                             reads=["kT", "qT"], writes=[("ps", bnk)])

                def emit_pv(ki, nq=nq, kts=kts, nsub=nsub, accs2=accs2):
                    kt = kts[ki]
                    for c in range(2):
                        bnk = sbank[ki % 2][c]
                        pS = self.psum[bnk]
                        E = Et4[(ki % 2) * 2 + c]; kE = ("Et", (ki % 2) * 2 + c)
                        P.op("scalar", lambda e, pS=pS, E=E: e.activation(out=E[:, 0:nq], in_=pS[:, 0:nq], func=AF.Exp, scale=0.125), reads=[("ps", bnk)], writes=[kE])
                        for qs in range(nsub):
                            A = accs2[c][qs // 2]; col = (qs % 2) * 129
                            P.op("tensor", lambda e, A=A, col=col, E=E, qs=qs, kt=kt, last=(ki == len(kts) - 1): e.matmul(A[:, col:col + 129], lhsT=E[:, qs * 128:(qs + 1) * 128], rhs=Vext[:, kt, 0:129], start=False, stop=last, skip_group_check=True),
                                 reads=[kE, "Vext"], writes=[("ps", 2 + 2 * c + qs // 2)])

                emit_qk(0)
                for ki in range(len(kts)):
                    if ki + 1 < len(kts):
                        emit_qk(ki + 1)
                    emit_pv(ki)
                for c in range(2):
                    ab0 = 2 + 2 * c
                    accs = accs2[c]
                    for ai, A in enumerate(accs):
                        kA = ("ps", ab0 + ai)
                        Av = A[:, 0:258].rearrange("p (s e) -> p s e", s=2)
                        rz = sc[:, 10 + 2 * ai:12 + 2 * ai]
                        rzb = rz.unsqueeze(2).broadcast_to([128, 2, 128])
                        ov = o_sb[:, 2 * ai:2 * ai + 2, :]
                        P.op("vector", lambda e, Av=Av, rz=rz: e.reciprocal(out=rz.unsqueeze(2), in_=Av[:, :, 128:129]), reads=[kA], writes=[("rz", ai)])
                        if c == 0:
                            P.op("vector", lambda e, Av=Av, rzb=rzb, ov=ov: e.tensor_tensor(out=ov, in0=Av[:, :, 0:128], in1=rzb, op=ALU.mult), reads=[kA, ("rz", ai)], writes=[("o_sb", ai)])
                        else:
                            t2v = tmp[ai][:, 0:256].rearrange("p (s e) -> p s e", s=2)
                            P.op("vector", lambda e, rz=rz: e.tensor_scalar(out=rz, in0=rz, scalar1=nlam, scalar2=None, op0=ALU.mult), reads=[("rz", ai), "sc"], writes=[("rz", ai)])
                            P.op("vector", lambda e, Av=Av, rzb=rzb, t2v=t2v: e.tensor_tensor(out=t2v, in0=Av[:, :, 0:128], in1=rzb, op=ALU.mult), reads=[kA, ("rz", ai)], writes=[("tmp", ai)])
                            P.op("gpsimd", lambda e, ov=ov, t2v=t2v: e.tensor_tensor(out=ov, in0=ov, in1=t2v, op=ALU.add), reads=[("tmp", ai), ("o_sb", ai)], writes=[("o_sb", ai)])
                pT = self.psum[6]; kT_ = ("ps", 6)
                ov = o_sb[:, 0:nsub, :]
                sqv = tmp[0][:, 0:nsub * 128]
                ssv = sc[:, 16:16 + nsub]
                bv = b_all[:, 0:nsub, :]
                P.op("scalar", lambda e, ov=ov, sqv=sqv, nsub=nsub: e.activation(out=sqv.rearrange("p (s e) -> p s e", s=nsub), in_=ov, func=AF.Square), reads=["o_sb"], writes=[("tmp", 0)])
                P.op("vector", lambda e, sqv=sqv, ssv=ssv, nsub=nsub: e.reduce_sum(out=ssv, in_=sqv.rearrange("p (s e) -> p s e", s=nsub), axis=AX.X), reads=[("tmp", 0)], writes=["ss"])
                P.op("vector", lambda e, ssv=ssv: e.tensor_scalar(out=ssv, in0=ssv, scalar1=1.0 / 128, scalar2=RMS_EPS, op0=ALU.mult, op1=ALU.add), reads=["ss"], writes=["ss"])
                P.op("scalar", lambda e, ssv=ssv: e.activation(out=ssv, in_=ssv, func=AF.Sqrt), reads=["ss"], writes=["ss"])
                P.op("vector", lambda e, ssv=ssv: e.reciprocal(out=ssv, in_=ssv), reads=["ss"], writes=["ss"])
                P.op("vector", lambda e, ov=ov, bv=bv, ssv=ssv, nsub=nsub: e.tensor_tensor(out=bv, in0=ov, in1=ssv.unsqueeze(2).broadcast_to([128, nsub, 128]), op=ALU.mult), reads=["o_sb", "ss"], writes=["b_all"])
                P.op("vector", lambda e, bv=bv, nsub=nsub: e.tensor_tensor(out=bv, in0=bv, in1=g_rep.unsqueeze(1).broadcast_to([128, nsub, 128]), op=ALU.mult), reads=["b_all", "g_rep"], writes=["b_all"])
                for qs in range(nsub):
                    P.op("tensor", lambda e, pT=pT, qs=qs: e.transpose(pT[:, qs * 128:(qs + 1) * 128], b_all[:, qs, :], self.ident[:]), reads=["b_all", "ident"], writes=[kT_])
                P.op("vector", lambda e, pT=pT, nq=nq: e.tensor_copy(out=bT[:, 0:nq], in_=pT[:, 0:nq]), reads=[kT_], writes=["bT"])
                P.dma("sync", self.brT_d[1][hd, :, q0:q0 + nq], bT[:, 0:nq], reads=["bT"], writes=["brT1"])
        if not ctx_out:
            pass

    def rwkv(self, l, blocks_out):
        P = self.P
        stage = self.dbg.get("rw_stage", 99)
        self.rwkv_proj(l)
        P.barrier()
        if stage <= 1:
            return
        for dc in range(DC):
            P.dma("sync", self.xT_d[:, dc * T:(dc + 1) * T], self.xT[:, dc, :], reads=["xT"], writes=["xT_d"])
        P.dma("sync", self.hT_d[:, :], self.hT[:].rearrange("p c t -> p (c t)"), reads=["hT"], writes=["hT_d"])
        P.barrier()
        if stage >= 3:
            self.rwkv_pass(l, 0, blocks_out)
            P.barrier()
        if stage >= 11:
            self.rwkv_pass(l, 1, blocks_out)
            P.barrier()
        for dc in range(DC):
            P.dma("sync", self.xT[:, dc, :], self.xT_d[:, dc * T:(dc + 1) * T], reads=["xT_d"], writes=["xT"])
        P.dma("sync", self.hT[:].rearrange("p c t -> p (c t)"), self.hT_d[:, :], reads=["hT_d"], writes=["hT"])
        P.barrier()

    def rwkv_proj(self, l):
        P = self.P
        hT = self.hT
        ab, af = self.arena_b, self.arena
        hsT = ab[:, 0:DC * T].rearrange("p (c t) -> p c t", c=DC)
        o = DC * T // 2
        W = [ab[:, 2 * o + i * 4096:2 * o + (i + 1) * 4096].rearrange("p (c n) -> p c n", c=DC) for i in range(2)]; o += 4096
        mu_rep = [af[:, o + i * 512:o + (i + 1) * 512] for i in range(2)]; o += 1024
        t1 = [af[:, o + i * 512:o + (i + 1) * 512] for i in range(2)]; o += 1024
        t2 = [af[:, o + i * 512:o + (i + 1) * 512] for i in range(2)]; o += 1024
        mucol = af[:, o:o + 4]; o += 4
        assert o <= 17408, o
        for (s0, n) in [(0, NCTX), (NCTX, NLAT)]:
            P.op("vector", lambda e, s0=s0, n=n: e.tensor_tensor(out=hsT[:, :, s0 + 1:s0 + n - 1], in0=hT[:, :, s0:s0 + n - 2], in1=hT[:, :, s0 + 2:s0 + n], op=ALU.add), reads=["hT"], writes=["hsT"])
            P.op("vector", lambda e, s0=s0: e.tensor_copy(out=hsT[:, :, s0:s0 + 1], in_=hT[:, :, s0 + 1:s0 + 2]), reads=["hT"], writes=["hsT"])
            P.op("vector", lambda e, s0=s0, n=n: e.tensor_copy(out=hsT[:, :, s0 + n - 1:s0 + n], in_=hT[:, :, s0 + n - 2:s0 + n - 1]), reads=["hT"], writes=["hsT"])
        P.op("scalar", lambda e: e.mul(out=hsT, in_=hsT, mul=0.5), reads=["hsT"], writes=["hsT"])
        for cb in range(6):
            w = W[cb % 2]; wk = ("W", cb % 2)
            P.dma("gpsimd", w, self.w_in[l, :, RW_OFF + cb * 512:RW_OFF + (cb + 1) * 512].rearrange("(c p) n -> p c n", p=128), writes=[wk])
            mr = mu_rep[cb % 2]; mk = ("mu", cb % 2)
            P.dma("sync", mr, self.rw_mu[l, cb * 512:(cb + 1) * 512].partition_broadcast(128), writes=[mk])
            for ti in range(T // 128):
                pp, pS = self.psum[(ti % 2) * 2], self.psum[(ti % 2) * 2 + 1]
                kp, kS = ("ps", (ti % 2) * 2), ("ps", (ti % 2) * 2 + 1)
                for (ps, pk, src, sk) in [(pp, kp, hT, "hT"), (pS, kS, hsT, "hsT")]:
                    for dc in range(DC):
                        P.op("tensor", lambda e, ps=ps, src=src, dc=dc, ti=ti, w=w: e.matmul(ps[:], lhsT=src[:, dc, ti * 128:(ti + 1) * 128], rhs=w[:, dc, :], start=(dc == 0), stop=(dc == DC - 1)),
                             reads=[sk, wk], writes=[pk])
                a, b_ = t1[ti % 2], t2[ti % 2]
                ka, kb = ("t1", ti % 2), ("t2", ti % 2)
                P.op("scalar", lambda e, a=a, pp=pp: e.copy(out=a, in_=pp[:]), reads=[kp], writes=[ka])
                P.op("vector", lambda e, a=a, b_=b_, pS=pS: e.tensor_tensor(out=b_, in0=pS[:], in1=a, op=ALU.subtract), reads=[kS, ka], writes=[kb])
                P.op("gpsimd", lambda e, b_=b_, mr=mr: e.tensor_tensor(out=b_, in0=b_, in1=mr, op=ALU.mult), reads=[kb, mk], writes=[kb])
                P.op("vector", lambda e, a=a, b_=b_: e.tensor_tensor(out=a, in0=a, in1=b_, op=ALU.add), reads=[ka, kb], writes=[ka])
                P.dma("sync", self.ztok_d[ti * 128:(ti + 1) * 128, cb * 512:(cb + 1) * 512], a, reads=[ka], writes=["ztok"])
        for ci, (c0, ncol) in enumerate([(3072, 128), (3200, 128), (3328, 128), (3456, 32)]):
            w = W[ci % 2]; wk = ("W", ci % 2)
            P.dma("gpsimd", w[:, :, 0:ncol], self.w_in[l, :, RW_OFF + c0:RW_OFF + c0 + ncol].rearrange("(c p) n -> p c n", p=128), writes=[wk])
            P.dma("sync", mucol[0:ncol, ci:ci + 1], self.rw_mu[l, c0:c0 + ncol].rearrange("(d o) -> d o", o=1), writes=[("mucol", ci)])
            func = [AF.Tanh, AF.Identity, AF.Sigmoid, AF.Sigmoid][ci]
            for bi in range(5):
                t0, n = TB[bi]
                pp, pS = self.psum[4 + (bi % 2) * 2], self.psum[5 + (bi % 2) * 2]
                kp, kS = ("ps", 4 + (bi % 2) * 2), ("ps", 5 + (bi % 2) * 2)
                for (ps, pk, src, sk) in [(pp, kp, hT, "hT"), (pS, kS, hsT, "hsT")]:
                    for dc in range(DC):
                        P.op("tensor", lambda e, ps=ps, src=src, dc=dc, t0=t0, n=n, w=w, ncol=ncol: e.matmul(ps[0:ncol, 0:n], lhsT=w[:, dc, 0:ncol], rhs=src[:, dc, t0:t0 + n], start=(dc == 0), stop=(dc == DC - 1)),
                             reads=[sk, wk], writes=[pk])
                a, b_ = t1[bi % 2], t2[bi % 2]
                ka, kb = ("t1", bi % 2), ("t2", bi % 2)
                P.op("scalar", lambda e, a=a, pp=pp, n=n, ncol=ncol: e.copy(out=a[0:ncol, 0:n], in_=pp[0:ncol, 0:n]), reads=[kp], writes=[ka])
                P.op("vector", lambda e, a=a, b_=b_, pS=pS, n=n, ncol=ncol: e.tensor_tensor(out=b_[0:ncol, 0:n], in0=pS[0:ncol, 0:n], in1=a[0:ncol, 0:n], op=ALU.subtract), reads=[kS, ka], writes=[kb])
                P.op("vector", lambda e, a=a, b_=b_, n=n, ncol=ncol, ci=ci: e.scalar_tensor_tensor(out=a[0:ncol, 0:n], in0=b_[0:ncol, 0:n], scalar=mucol[0:ncol, ci:ci + 1], in1=a[0:ncol, 0:n], op0=ALU.mult, op1=ALU.add),
                     reads=[ka, kb, ("mucol", ci)], writes=[ka])
                P.op("scalar", lambda e, a=a, n=n, ncol=ncol, func=func: e.activation(out=a[0:ncol, 0:n], in_=a[0:ncol, 0:n], func=func), reads=[ka], writes=[ka])
                P.dma("sync", self.smallT_d[ci, 0:ncol, t0:t0 + n], a[0:ncol, 0:n], reads=[ka], writes=["smallT"])

    def rwkv_pass(self, l, d, blocks_out):
        import math
        P = self.P
        C0 = math.exp(-0.5)
        ctx_out = 0 in blocks_out
        regions = [[self.arena, 0, 17408], [self.hT[:].rearrange("p c t -> p (c t)").bitcast(F32), 0, DC * T // 2], [self.xT[:].rearrange("p c t -> p (c t)"), 0, DC * T]]

        def F(n):
            for r in regions:
                if r[1] + n <= r[2]:
                    v = r[0][:, r[1]:r[1] + n]; r[1] += n
                    return v
            raise RuntimeError("rwkv scratch exhausted")
        H = F(512).rearrange("p (k i) -> p k i", k=8)
        kkp_rep = F(1024); ka_rep = F(1024); w0_rep = F(1024); a0_rep = F(1024)
        w2t = F(1024); a2t = F(1024)
        masks = F(1024).rearrange("p (m t) -> p m t", m=8)
        IU, IL, SU, SL, SUd, SLd, SUo, SLo = (masks[:, i, :] for i in range(8))
        INCL = IU if d == 0 else IL
        MS_st = SU if d == 0 else SL
        MI_st = INCL
        MXd_ts = SLd if d == 0 else SUd
        MXd_st = SUd if d == 0 else SLd
        MLo_ts = SLo if d == 0 else SUo
        ztile = F(3072); zr = ztile[:, 0:1024]; zk = ztile[:, 1024:2048]; zv = ztile[:, 2048:3072]
        tz_t = F(128); za_t = F(128)
        sgw = F(1024); alpha = F(1024); tbuf = F(1024); nkk = F(1024); kd = F(1024); bb = F(1024); Ep = F(1024)
        Em = alpha; Ex = tbuf; U = zk
        XT4 = [F(1024).rearrange("p (k t) -> p k t", k=8) for _ in range(4)]
        AtT, BtT, KtT, RtT = XT4
        big = [F(1024).rearrange("p (h t) -> p h t", h=8) for _ in range(12)]
        yt = F(1024)
        st = F(64)
        Gam = st[:, 0:8]
        if d == 1:
            a00_rep = F(1024); alpha0 = F(1024); kd0 = F(1024)
            lnw_rep = F(1024); lnb_rep = F(1024); rk_rep = F(1024)
            g2a = F(1024); g2b = F(1024)
            sgA = F(128); sgB = F(128)
            yf_t = alpha0
            c_tok = kd0[:, 0:512].bitcast(BF16)
            cT = F(512).bitcast(BF16).rearrange("p (c t) -> p c t", c=8)
            identb = F(64).bitcast(BF16)
        ident, ones = self.ident, self.ones
        if self.dbg.get("print_regions"):
            print("rwkv_pass d=%d region usage:" % d, [(r[1], r[2]) for r in regions])

        P.dma("sync", masks, self.masks_d.rearrange("m p t -> p m t"), writes=["masks"])
        P.dma("sync", kkp_rep, self.rw_kk[l].partition_broadcast(128), writes=["kkp_rep"])
        P.dma("sync", ka_rep, self.rw_ka[l].partition_broadcast(128), writes=["ka_rep"])
        P.dma("sync", w0_rep, self.rw_w0[l, d].partition_broadcast(128), writes=["w0_rep"])
        P.dma("sync", a0_rep, self.rw_a0[l, d].partition_broadcast(128), writes=["a0_rep"])
        P.dma("sync", w2t, self.rw_w2[l].rearrange("d r c -> (d r) c"), writes=["w2t"])
        P.dma("sync", a2t, self.rw_a2[l].rearrange("d r c -> (d r) c"), writes=["a2t"])
        P.op("vector", lambda e: e.memset(H, 0.0), writes=["H"])
        if d == 1:
            P.dma("sync", a00_rep, self.rw_a0[l, 0].partition_broadcast(128), writes=["a00_rep"])
            P.dma("sync", lnw_rep, self.rw_ln_w[l].partition_broadcast(128), writes=["lnw_rep"])
            P.dma("sync", lnb_rep, self.rw_ln_b[l].partition_broadcast(128), writes=["lnb_rep"])
            P.dma("sync", rk_rep, self.rw_rk[l].rearrange("h j -> (h j)").partition_broadcast(128), writes=["rk_rep"])
            P.dma("sync", g2a, self.rw_g2[l, 0:128, :], writes=["g2a"])
            P.dma("sync", g2b[0:32, :], self.rw_g2[l, 128:160, :], writes=["g2b"])
            P.op("vector", lambda e: e.tensor_copy(out=identb, in_=ident[:]), reads=["ident"], writes=["identb"])

        order = list(range(18)) if d == 0 else [1, 0] + list(range(17, 1, -1))
        F32R = mybir.dt.float32r
        r32 = (lambda a: a.bitcast(F32R)) if self.dbg.get("fp32r") else (lambda a: a)
        stage = self.dbg.get("rw_stage", 99)
        if stage < 10:
            order = order[:1]
        R2 = [slice(0, 64), slice(64, 128)]

        def lora(dst, src_t, wt, rep, dd, keyd, wkey, rkey, extra_w=(), skey="tzt"):
            for half in range(2):
                ps = self.psum[half]; pk = ("ps", half)
                P.op("tensor", lambda e, ps=ps, half=half: e.matmul(ps[:], lhsT=src_t[R2[dd], :], rhs=wt[R2[dd], half * 512:(half + 1) * 512], start=True, stop=True),
                     reads=[skey, wkey], writes=[pk])
                P.op("vector", lambda e, ps=ps, half=half: e.tensor_tensor(out=dst[:, half * 512:(half + 1) * 512], in0=ps[:], in1=rep[:, half * 512:(half + 1) * 512], op=ALU.add),
                     reads=[pk, rkey], writes=[(keyd, half)] + list(extra_w))
                P.op("scalar", lambda e, half=half: e.activation(out=dst[:, half * 512:(half + 1) * 512], in_=dst[:, half * 512:(half + 1) * 512], func=AF.Sigmoid),
                     reads=[(keyd, half)], writes=[(keyd, half)])

        def kdcalc(dst, al, keya, keyd):
            P.op("vector", lambda e: e.scalar_tensor_tensor(out=dst, in0=al, scalar=-1.0, in1=ka_rep, op0=ALU.add, op1=ALU.mult), reads=[keya, "ka_rep"], writes=[keyd] + (["c_tok"] if keyd == "kd0" else []))
            P.op("vector", lambda e: e.scalar_tensor_tensor(out=dst, in0=dst, scalar=1.0, in1=zk, op0=ALU.add, op1=ALU.mult), reads=[keyd, "ztile"], writes=[keyd])

        for n in order:
            tt = n * 128
            P.dma("sync", ztile, self.ztok_d[tt:tt + 128, :], reads=["ztok"], writes=["ztile", "U"])
            P.dma("sync", tz_t, self.smallT_d[0, :, tt:tt + 128], writes=["tzt"])
            P.dma("sync", za_t, self.smallT_d[1, :, tt:tt + 128], writes=["zat"])
            if d == 1:
                P.dma("sync", sgA, self.smallT_d[2, :, tt:tt + 128], writes=["sgA"])
                P.dma("sync", sgB[0:32, :], self.smallT_d[3, 0:32, tt:tt + 128], writes=["sgB"])
            lora(sgw, tz_t, w2t, w0_rep, d, "sgw", "w2t", "w0_rep")
            lora(alpha, za_t, a2t, a0_rep, d, "alpha", "a2t", "a0_rep", extra_w=["Em"], skey="zat")
            if d == 1:
                lora(alpha0, za_t, a2t, a00_rep, 0, "alpha0", "a2t", "a00_rep", skey="zat")
            P.op("vector", lambda e: e.tensor_tensor(out=tbuf, in0=zk, in1=kkp_rep, op=ALU.mult), reads=["ztile", "kkp_rep"], writes=["tbuf", "Ex"])
            P.op("scalar", lambda e: e.activation(out=nkk, in_=tbuf, func=AF.Square), reads=["tbuf"], writes=["nkk"])
            P.op("vector", lambda e: e.reduce_sum(out=st[:, 16:32], in_=nkk.rearrange("p (h j) -> p h j", h=16), axis=AX.X), reads=["nkk"], writes=["st"])
            P.op("scalar", lambda e: e.activation(out=st[:, 16:32], in_=st[:, 16:32], func=AF.Sqrt), reads=["st"], writes=["st"])
            P.op("vector", lambda e: e.tensor_scalar(out=st[:, 16:32], in0=st[:, 16:32], scalar1=1e-12, scalar2=None, op0=ALU.max), reads=["st"], writes=["st"])
            P.op("vector", lambda e: e.reciprocal(out=st[:, 16:32], in_=st[:, 16:32]), reads=["st"], writes=["st"])
            P.op("vector", lambda e: e.scalar_tensor_tensor(out=nkk.rearrange("p (h j) -> p h j", h=16), in0=tbuf.rearrange("p (h j) -> p h j", h=16), scalar=-1.0,
                                                             in1=st[:, 16:32].unsqueeze(2).broadcast_to([128, 16, 64]), op0=ALU.mult, op1=ALU.mult), reads=["tbuf", "st"], writes=["nkk"])
            kdcalc(kd, alpha, "alpha", "kd")
            if d == 1:
                kdcalc(kd0, alpha0, "alpha0", "kd0")
                P.op("gpsimd", lambda e: e.tensor_tensor(out=kd0, in0=kd0, in1=kd, op=ALU.add), reads=["kd0", "kd"], writes=["kd0"])
            P.op("vector", lambda e: e.scalar_tensor_tensor(out=bb, in0=nkk, scalar=-1.0, in1=alpha, op0=ALU.mult, op1=ALU.mult), reads=["nkk", "alpha"], writes=["bb"])
            if d == 1:
                P.dma("sync", yf_t, self.yf_d[tt:tt + 128, :], reads=["yf"], writes=["alpha0"])
            if stage <= 3:
                continue
            for pr in range(8):
                P.op("tensor", lambda e, pr=pr: e.matmul(self.psum[7][:, pr:pr + 1], lhsT=sgw[:, pr * 128:(pr + 1) * 128], rhs=ones[:, 0:1], start=True, stop=True), reads=["sgw", "ones"], writes=[("ps", 7)])
            P.op("scalar", lambda e: e.activation(out=Gam, in_=self.psum[7][:, 0:8], func=AF.Exp, scale=-C0), reads=[("ps", 7)], writes=["Gam"])
            for half in range(2):
                ps = self.psum[half]; pk = ("ps", half)
                hs = slice(half * 512, (half + 1) * 512)
                P.op("tensor", lambda e, ps=ps, hs=hs: e.matmul(ps[:], lhsT=INCL, rhs=sgw[:, hs], start=True, stop=True), reads=["masks", "sgw"], writes=[pk])
                P.op("scalar", lambda e, ps=ps, hs=hs: e.activation(out=Ep[:, hs], in_=ps[:], func=AF.Exp, scale=-C0), reads=[pk], writes=[("Ep", half)])
                P.op("vector", lambda e, ps=ps, hs=hs: e.tensor_tensor(out=Ex[:, hs], in0=ps[:], in1=sgw[:, hs], op=ALU.subtract), reads=[pk, "sgw", "tbuf", "nkk"], writes=[("Ex", half)])
                P.op("scalar", lambda e, ps=ps, hs=hs: e.activation(out=Em[:, hs], in_=ps[:], func=AF.Exp, scale=C0), reads=[pk, "alpha", "bb", "kd"], writes=[("Em", half)])
                P.op("scalar", lambda e, hs=hs: e.activation(out=Ex[:, hs], in_=Ex[:, hs], func=AF.Exp, scale=-C0), reads=[("Ex", half)], writes=[("Ex", half)])
            P.op("vector", lambda e: e.tensor_tensor(out=Ex, in0=Ex, in1=nkk, op=ALU.mult), reads=["Ex", "nkk"], writes=["Ex"])
            P.op("gpsimd", lambda e: e.tensor_tensor(out=bb, in0=bb, in1=Em, op=ALU.mult), reads=["bb", "Em"], writes=["bb"])
            P.op("vector", lambda e: e.tensor_tensor(out=kd, in0=kd, in1=Em, op=ALU.mult), reads=["kd", "Em", "kd0"], writes=["kd"])
            P.op("vector", lambda e: e.tensor_tensor(out=Ep, in0=Ep, in1=zr, op=ALU.mult), reads=["Ep", "ztile"], writes=["Ep"])
            if stage <= 4:
                continue
            for xi, (src, sk) in enumerate([(Ex, "Ex"), (bb, "bb"), (kd, "kd"), (Ep, "Ep")]):
                for half in range(2):
                    bank = 2 + (xi * 2 + half) % 2
                    ps = self.psum[bank]; pk = ("ps", bank)
                    for j in range(4):
                        pr = half * 4 + j
                        P.op("tensor", lambda e, ps=ps, j=j, pr=pr, src=src: e.transpose(ps[:, j * 128:(j + 1) * 128], src[:, pr * 128:(pr + 1) * 128], ident[:]), reads=[sk, "ident"], writes=[pk])
                    dst = XT4[xi][:, half * 4:half * 4 + 4, :]
                    if half == 0:
                        P.op("vector", lambda e, ps=ps, dst=dst: e.tensor_copy(out=dst, in_=ps[:].rearrange("p (j t) -> p j t", j=4)), reads=[pk], writes=[("XT4", xi)])
                    else:
                        P.op("scalar", lambda e, ps=ps, dst=dst: e.copy(out=dst, in_=ps[:].rearrange("p (j t) -> p j t", j=4)), reads=[pk], writes=[("XT4", xi)])

            def pairmm(dst, lT, lk, rT, rk, mask, dk, banks, hg, dst2=None, mask2=None, dk2=None):
                for j in range(8):
                    h = hg * 8 + j
                    bank = banks[h % 2]
                    ps = self.psum[bank]
                    P.op("tensor", lambda e, ps=ps, j=j, h=h: e.matmul(ps[:, (j // 2) * 128:(j // 2 + 1) * 128], lhsT=r32(lT[R2[h % 2], h // 2, :]), rhs=r32(rT[R2[h % 2], h // 2, :]), start=True, stop=True),
                         reads=[("XT4", lk), ("XT4", rk)], writes=[("ps", bank)])
                for par in range(2):
                    bank = banks[par]
                    ps = self.psum[bank]
                    P.op("vector", lambda e, ps=ps, par=par: e.tensor_tensor(out=dst[:, par:8:2, :], in0=ps[:].rearrange("p (j t) -> p j t", j=4), in1=mask.unsqueeze(1).broadcast_to([128, 4, 128]), op=ALU.mult),
                         reads=[("ps", bank), "masks"], writes=[dk])
                    if dst2 is not None:
                        P.op("vector", lambda e, ps=ps, par=par: e.tensor_tensor(out=dst2[:, par:8:2, :], in0=ps[:].rearrange("p (j t) -> p j t", j=4), in1=mask2.unsqueeze(1).broadcast_to([128, 4, 128]), op=ALU.mult),
                             reads=[("ps", bank), "masks"], writes=[dk2])

            def headmm(dstbuf, dk, lbuf, lkey, rbuf, rkey, banks, mode, accbuf=None):
                for q in range(2):
                    bank = banks[q]
                    ps = self.psum[bank]
                    for jj in range(4):
                        j = q * 4 + jj
                        P.op("tensor", lambda e, ps=ps, jj=jj, j=j: e.matmul(ps[:, jj * 128:(jj + 1) * 128], lhsT=r32(lbuf[:, j, :]), rhs=r32(rbuf[:, j, :]), start=True, stop=True),
                             reads=[lkey, rkey], writes=[("ps", bank)])
                    dv = dstbuf[:, q * 4:q * 4 + 4, :]
                    pv = ps[:].rearrange("p (j t) -> p j t", j=4)
                    if mode == "copy":
                        if q == 0:
                            P.op("scalar", lambda e, dv=dv, pv=pv: e.copy(out=dv, in_=pv), reads=[("ps", bank)], writes=[(dk, q)])
                        else:
                            P.op("vector", lambda e, dv=dv, pv=pv: e.tensor_copy(out=dv, in_=pv), reads=[("ps", bank)], writes=[(dk, q)])
                    else:
                        P.op("vector", lambda e, dv=dv, pv=pv: e.tensor_tensor(out=dv, in0=pv, in1=dv, op=ALU.add), reads=[("ps", bank), (dk, q)], writes=[(dk, q)])

            Wsb, Vsb = Ep, Ex
            identb8 = ident[:].unsqueeze(1).broadcast_to([128, 8, 128])

            def solve_half(hg):
                hs = slice(hg * 512, (hg + 1) * 512)
                Xd, XTd, X2, XT2, PTm, Lo = big[hg * 6:(hg + 1) * 6]
                kX, kXT, kX2, kXT2, kPT, kLo = (f"b{hg}_{n}" for n in ("X", "XT", "X2", "XT2", "PT", "Lo"))
                Lk, kLk = X2, kX2
                bk = [2, 3] if hg == 0 else [6, 7]
                pairmm(XTd, BtT, 1, AtT, 0, MXd_st, kXT, bk, hg); yield
                pairmm(Xd, AtT, 0, BtT, 1, MXd_ts, kX, bk, hg); yield
                pairmm(Lo, AtT, 0, BtT, 1, MLo_ts, kLo, bk, hg); yield
                pairmm(Lk, KtT, 2, AtT, 0, MS_st, kLk, bk, hg); yield
                ps = self.psum[4 + hg]; pk = ("ps", 4 + hg)
                for j in range(8):
                    h = hg * 8 + j
                    P.op("tensor", lambda e, ps=ps, j=j, h=h: e.matmul(ps[:, j * 64:(j + 1) * 64], lhsT=AtT[R2[h % 2], h // 2, :], rhs=H[R2[h % 2], h // 2, :], start=True, stop=False),
                         reads=[("XT4", 0), "H"], writes=[pk])
                    P.op("tensor", lambda e, ps=ps, j=j, h=h, Lk=Lk: e.matmul(ps[:, j * 64:(j + 1) * 64], lhsT=Lk[:, j, :], rhs=zv[:, h * 64:(h + 1) * 64], start=False, stop=True),
                         reads=[kLk, "ztile"], writes=[pk])
                P.op("scalar", lambda e, ps=ps, hs=hs: e.copy(out=Wsb[:, hs], in_=ps[:]), reads=[pk, "Ep"], writes=[("Ep", hg)])
                yield
                P.op("vector", lambda e, PTm=PTm, XTd=XTd: e.tensor_tensor(out=PTm, in0=XTd, in1=identb8, op=ALU.add), reads=[kXT, "ident"], writes=[kPT])
                cur = (Xd, kX, XTd, kXT)
                nxt = (X2, kX2, XT2, kXT2)
                for lev in range(3):
                    Xc, xk, XTc, xtk = cur
                    Xn, xnk, XTn, xtnk = nxt
                    headmm(Xn, xnk, XTc, xtk, Xc, xk, bk, "copy"); yield
                    if lev < 2:
                        headmm(XTn, xtnk, Xc, xk, XTc, xtk, bk, "copy"); yield
                    headmm(PTm, kPT, Xn, xnk, PTm, kPT, bk, "acc"); yield
                    cur, nxt = nxt, cur
                for j in range(8):
                    h = hg * 8 + j
                    P.op("tensor", lambda e, ps=ps, j=j, h=h, PTm=PTm: e.matmul(ps[:, j * 64:(j + 1) * 64], lhsT=PTm[:, j, :], rhs=Wsb[:, h * 64:(h + 1) * 64], start=True, stop=True),
                         reads=[kPT, ("Ep", hg)], writes=[pk])
                P.op("scalar", lambda e, ps=ps, hs=hs: e.copy(out=Vsb[:, hs], in_=ps[:]), reads=[pk, "Ex"], writes=[("Ex", hg)])
                yield
                MT = Xd
                headmm(MT, kX, Lo, kLo, PTm, kPT, bk, "copy"); yield
                for it in range(7):
                    src = Vsb if it == 0 else U
                    srck = ("Ex", hg) if it == 0 else ("U", hg)
                    for j in range(8):
                        h = hg * 8 + j
                        P.op("tensor", lambda e, ps=ps, j=j, h=h, src=src, MT=MT: e.matmul(ps[:, j * 64:(j + 1) * 64], lhsT=MT[:, j, :], rhs=src[:, h * 64:(h + 1) * 64], start=True, stop=True),
                             reads=[kX, srck], writes=[pk])
                    rd = [pk, ("Ex", hg), ("U", hg)] + (["tbuf", "kd", "kd0"] if it == 0 else [])
                    P.op("vector", lambda e, ps=ps, hs=hs: e.tensor_tensor(out=U[:, hs], in0=ps[:], in1=Vsb[:, hs], op=ALU.add), reads=rd, writes=[("U", hg)])
                    yield
                Mb, Mk = X2, XT2
                pairmm(Mb, BtT, 1, RtT, 3, MI_st, kX2, bk, hg); yield
                pairmm(Mk, KtT, 2, RtT, 3, MI_st, kXT2, bk, hg); yield
                for j in range(8):
                    h = hg * 8 + j
                    P.op("tensor", lambda e, ps=ps, j=j, h=h: e.matmul(ps[:, j * 64:(j + 1) * 64], lhsT=RtT[R2[h % 2], h // 2, :], rhs=H[R2[h % 2], h // 2, :], start=True, stop=False),
                         reads=[("XT4", 3), "H"], writes=[pk])
                    P.op("tensor", lambda e, ps=ps, j=j, h=h, Mb=Mb: e.matmul(ps[:, j * 64:(j + 1) * 64], lhsT=Mb[:, j, :], rhs=U[:, h * 64:(h + 1) * 64], start=False, stop=False),
                         reads=[kX2, ("U", hg)], writes=[pk])
                    P.op("tensor", lambda e, ps=ps, j=j, h=h, Mk=Mk: e.matmul(ps[:, j * 64:(j + 1) * 64], lhsT=Mk[:, j, :], rhs=zv[:, h * 64:(h + 1) * 64], start=False, stop=True),
                         reads=[kXT2, "ztile"], writes=[pk])
                if d == 0:
                    P.op("scalar", lambda e, ps=ps, hs=hs: e.copy(out=yt[:, hs], in_=ps[:]), reads=[pk], writes=[("yt", hg)])
                else:
                    P.op("vector", lambda e, ps=ps, hs=hs: e.tensor_tensor(out=yt[:, hs], in0=ps[:], in1=yf_t[:, hs], op=ALU.add), reads=[pk, "alpha0"], writes=[("yt", hg)])

            gens = [solve_half(0), solve_half(1)]
            while gens:
                for g in list(gens):
                    try:
                        next(g)
                    except StopIteration:
                        gens.remove(g)
            if d == 0:
                P.dma("sync", self.yf_d[tt:tt + 128, :], yt, reads=["yt"], writes=["yf"])
            for half in range(2):
                ps = self.psum[half]; pk = ("ps", half)
                for j in range(4):
                    pr = half * 4 + j
                    cs_ = slice(pr * 128, (pr + 1) * 128)
                    P.op("tensor", lambda e, ps=ps, j=j, cs_=cs_: e.matmul(ps[:, j * 128:(j + 1) * 128], lhsT=r32(bb[:, cs_]), rhs=r32(U[:, cs_]), start=True, stop=False), reads=["bb", "U"], writes=[pk])
                    P.op("tensor", lambda e, ps=ps, j=j, cs_=cs_: e.matmul(ps[:, j * 128:(j + 1) * 128], lhsT=r32(kd[:, cs_]), rhs=r32(zv[:, cs_]), start=False, stop=True), reads=["kd", "ztile"], writes=[pk])
                for hh in range(2):
                    P.op("vector", lambda e, ps=ps, hh=hh, half=half: e.tensor_tensor(out=H[R2[hh], half * 4:half * 4 + 4, :], in0=ps[R2[hh], :].rearrange("p (j c) -> p j c", j=4)[:, :, hh * 64:(hh + 1) * 64],
                                                                                  in1=H[R2[hh], half * 4:half * 4 + 4, :], op=ALU.add), reads=[pk, "H"], writes=["H"])
            P.op("vector", lambda e: e.tensor_tensor(out=H, in0=H, in1=Gam.unsqueeze(2).broadcast_to([128, 8, 64]), op=ALU.mult), reads=["H", "Gam"], writes=["H"])
            if d == 1 and (ctx_out or n >= 2):
                y3 = yt.rearrange("p (h i) -> p h i", h=16)
                P.op("vector", lambda e: e.reduce_sum(out=st[:, 32:48], in_=y3, axis=AX.X), reads=["yt"], writes=["st2"])
                P.op("scalar", lambda e: e.activation(out=Ep, in_=yt, func=AF.Square), reads=["yt", "Ep"], writes=["Ep"])
                P.op("vector", lambda e: e.reduce_sum(out=st[:, 48:64], in_=Ep.rearrange("p (h i) -> p h i", h=16), axis=AX.X), reads=["Ep"], writes=["st2"])
                P.op("vector", lambda e: e.tensor_scalar(out=st[:, 32:48], in0=st[:, 32:48], scalar1=1.0 / 64, scalar2=None, op0=ALU.mult), reads=["st2"], writes=["st2"])
                P.op("vector", lambda e: e.tensor_tensor(out=st[:, 0:16], in0=st[:, 32:48], in1=st[:, 32:48], op=ALU.mult), reads=["st2", "Gam"], writes=["Gam"])
                P.op("vector", lambda e: e.scalar_tensor_tensor(out=st[:, 48:64], in0=st[:, 48:64], scalar=1.0 / 64, in1=st[:, 0:16], op0=ALU.mult, op1=ALU.subtract), reads=["st2", "Gam"], writes=["st2"])
                P.op("vector", lambda e: e.tensor_scalar(out=st[:, 48:64], in0=st[:, 48:64], scalar1=64e-5, scalar2=None, op0=ALU.add), reads=["st2"], writes=["st2"])
                P.op("scalar", lambda e: e.activation(out=st[:, 48:64], in_=st[:, 48:64], func=AF.Sqrt), reads=["st2"], writes=["st2"])
                P.op("vector", lambda e: e.reciprocal(out=st[:, 48:64], in_=st[:, 48:64]), reads=["st2"], writes=["st2"])
                P.op("vector", lambda e: e.tensor_tensor(out=y3, in0=y3, in1=st[:, 32:48].unsqueeze(2).broadcast_to([128, 16, 64]), op=ALU.subtract), reads=["yt", "st2"], writes=["yt"])
                P.op("vector", lambda e: e.tensor_tensor(out=y3, in0=y3, in1=st[:, 48:64].unsqueeze(2).broadcast_to([128, 16, 64]), op=ALU.mult), reads=["yt", "st2"], writes=["yt"])
                P.op("vector", lambda e: e.tensor_tensor(out=yt, in0=yt, in1=lnw_rep, op=ALU.mult), reads=["yt", "lnw_rep"], writes=["yt"])
                P.op("vector", lambda e: e.tensor_tensor(out=yt, in0=yt, in1=lnb_rep, op=ALU.add), reads=["yt", "lnb_rep"], writes=["yt"])
                P.op("vector", lambda e: e.tensor_tensor(out=kd0, in0=kd0, in1=zr, op=ALU.mult), reads=["kd0", "ztile"], writes=["kd0"])
                P.op("vector", lambda e: e.tensor_tensor(out=kd0, in0=kd0, in1=rk_rep, op=ALU.mult), reads=["kd0", "rk_rep"], writes=["kd0"])
                P.op("vector", lambda e: e.reduce_sum(out=st[:, 32:48], in_=kd0.rearrange("p (h j) -> p h j", h=16), axis=AX.X), reads=["kd0", "yt"], writes=["st2"])
                P.op("vector", lambda e: e.tensor_tensor(out=kd0.rearrange("p (h j) -> p h j", h=16), in0=zv.rearrange("p (h j) -> p h j", h=16), in1=st[:, 32:48].unsqueeze(2).broadcast_to([128, 16, 64]), op=ALU.mult),
                     reads=["ztile", "st2"], writes=["kd0"])
                P.op("vector", lambda e: e.tensor_tensor(out=yt, in0=yt, in1=kd0, op=ALU.add), reads=["yt", "kd0"], writes=["yt"])
                for half in range(2):
                    ps = self.psum[half]; pk = ("ps", half)
                    hs = slice(half * 512, (half + 1) * 512)
                    P.op("tensor", lambda e, ps=ps, hs=hs: e.matmul(ps[:], lhsT=sgA, rhs=g2a[:, hs], start=True, stop=False), reads=["sgA", "g2a"], writes=[pk])
                    P.op("tensor", lambda e, ps=ps, hs=hs: e.matmul(ps[:], lhsT=sgB[0:32, :], rhs=g2b[0:32, hs], start=False, stop=True), reads=["sgB", "g2b"], writes=[pk])
                    P.op("vector", lambda e, ps=ps, hs=hs: e.tensor_tensor(out=c_tok[:, hs], in0=ps[:], in1=yt[:, hs], op=ALU.mult), reads=[pk, "yt"], writes=[("c_tok", half), "kd0"])
                for half in range(2):
                    ps = self.psum[2 + half]; pk = ("ps", 2 + half)
                    for j in range(4):
                        fc = half * 4 + j
                        P.op("tensor", lambda e, ps=ps, j=j, fc=fc: e.matmul(ps[:, j * 128:(j + 1) * 128], lhsT=c_tok[:, fc * 128:(fc + 1) * 128], rhs=identb, start=True, stop=True), reads=["c_tok", "identb"], writes=[pk])
                    P.op("scalar", lambda e, ps=ps, half=half: e.copy(out=cT[:, half * 4:half * 4 + 4, :], in_=ps[:].rearrange("p (j t) -> p j t", j=4)), reads=[pk], writes=[("cT", half)])
                P.dma("sync", self.brT_d[2][:, :, tt:tt + 128].rearrange("c p t -> p c t"), cT, reads=["cT"], writes=["brT2"])

    def merge(self, l, blocks_out):
        P = self.P
        hT, xT, lv = self.hT, self.xT, self.lv
        ab, af = self.arena_b, self.arena
        brb = [ab[:, i * 4096:(i + 1) * 4096].rearrange("p (c t) -> p c t", c=DC) for i in range(3)]
        mT = ab[:, 12288:16384].rearrange("p (c t) -> p c t", c=DC)
        NW = 14
        wt = [ab[:, 16384 + i * 1024:16384 + (i + 1) * 1024].rearrange("p (c n) -> p c n", c=DC) for i in range(NW)]
        o = (16384 + NW * 1024) // 2
        sig = [af[:, o + i * 512:o + (i + 1) * 512] for i in range(3)]; o += 1536
        macc = af[:, o:o + 512]; o += 512
        assert o <= 17408
        wi = [0]

        def wload(src):
            i = wi[0] % NW
            wi[0] += 1
            P.dma("gpsimd", wt[i], src.rearrange("(c p) n -> p c n", p=128), writes=[("wt", i)])
            return wt[i], ("wt", i)

        for bi in blocks_out:
            t0, n = TB[bi]
            who = 1 if bi == 0 else 0
            for i in range(3):
                P.dma("sync", brb[i][:, :, 0:n], self.brT_d[i][:, :, t0:t0 + n].rearrange("c p t -> p c t"), reads=[f"brT{i}"], writes=[("brb", i)])
            for j in range(DC):
                for i in range(3):
                    wb, wbk = wload(self.w_br[i][l, :, j * 128:(j + 1) * 128])
                    wg, wgk = wload(self.w_in[l, :, GT_OFF + i * D + j * 128:GT_OFF + i * D + (j + 1) * 128])
                    pb, pg = self.psum[2 * i], self.psum[2 * i + 1]
                    kb, kg = ("ps", 2 * i), ("ps", 2 * i + 1)
                    for k in range(DC):
                        P.op("tensor", lambda e, pb=pb, wb=wb, k=k, i=i, n=n: e.matmul(pb[:, 0:n], lhsT=wb[:, k, :], rhs=brb[i][:, k, 0:n], start=(k == 0), stop=(k == DC - 1)),
                             reads=[wbk, ("brb", i)], writes=[kb])
                    for k in range(DC):
                        P.op("tensor", lambda e, pg=pg, wg=wg, k=k, t0=t0, n=n: e.matmul(pg[:, 0:n], lhsT=wg[:, k, :], rhs=hT[:, k, t0:t0 + n], start=(k == 0), stop=(k == DC - 1)),
                             reads=[wgk, ("hT", bi)], writes=[kg])
                    P.op("scalar", lambda e, pg=pg, i=i, n=n: e.activation(out=sig[i][:, 0:n], in_=pg[:, 0:n], func=AF.Sigmoid), reads=[kg], writes=[("sig", i)])
                    if i == 0:
                        P.op("vector", lambda e, pb=pb, n=n: e.tensor_tensor(out=macc[:, 0:n], in0=pb[:, 0:n], in1=sig[0][:, 0:n], op=ALU.mult),
                             reads=[kb, ("sig", 0)], writes=["macc"])
                    else:
                        P.op("vector", lambda e, pb=pb, i=i, n=n: e.tensor_tensor(out=sig[i][:, 0:n], in0=pb[:, 0:n], in1=sig[i][:, 0:n], op=ALU.mult),
                             reads=[kb, ("sig", i)], writes=[("sig", i)])
                        if i == 1:
                            P.op("vector", lambda e, n=n: e.tensor_tensor(out=macc[:, 0:n], in0=macc[:, 0:n], in1=sig[1][:, 0:n], op=ALU.add),
                                 reads=["macc", ("sig", 1)], writes=["macc"])
                        else:
                            P.op("vector", lambda e, n=n, j=j: e.tensor_tensor(out=mT[:, j, 0:n], in0=macc[:, 0:n], in1=sig[2][:, 0:n], op=ALU.add),
                                 reads=["macc", ("sig", 2)], writes=[("mT", j)])
            for j in range(DC):
                wo, wok = wload(self.w_o[l, :, j * 128:(j + 1) * 128])
                ps = self.psum[6 + j % 2]
                pk = ("ps", 6 + j % 2)
                for k in range(DC):
                    P.op("tensor", lambda e, ps=ps, wo=wo, k=k, n=n: e.matmul(ps[:, 0:n], lhsT=wo[:, k, :], rhs=mT[:, k, 0:n], start=(k == 0), stop=(k == DC - 1)),
                         reads=[wok, ("mT", k)], writes=[pk])
                P.op("vector", lambda e, ps=ps, j=j, t0=t0, n=n, who=who: e.scalar_tensor_tensor(
                    out=xT[:, j, t0:t0 + n], in0=ps[:, 0:n], scalar=lv[:, 2, j, who:who + 1], in1=xT[:, j, t0:t0 + n], op0=ALU.mult, op1=ALU.add),
                    reads=[pk, "lv", ("xT", bi)], writes=[("xT", bi)])


def m_names_ok(inputs):
    return inputs.keys()


def make_inputs_for_core(inputs, b, consts):
    m = {}
    for k, v in inputs.items():
        v = np.asarray(v)
        if k not in m_names_ok(inputs):
            continue
        if k in ("x", "c", "ctx"):
            m[k] = np.ascontiguousarray(v[b])
        else:
            m[k] = np.ascontiguousarray(v)
    m.update(consts)
    return m


_CACHE = {}


def kernel(**inputs):
    if "nc" not in _CACHE:
        m = Model()
        _CACHE["nc"] = m.build()
        _CACHE["names"] = list(m.in_names)
    nc = _CACHE["nc"]
    consts = host_consts()
    in_maps = []
    for b in range(8):
        d = {}
        for k in _CACHE["names"]:
            if k in consts:
                d[k] = consts[k]
            elif k in ("x", "c", "ctx"):
                d[k] = np.ascontiguousarray(np.asarray(inputs[k])[b], dtype=np.float32)
            else:
                d[k] = np.ascontiguousarray(np.asarray(inputs[k]), dtype=np.float32)
        in_maps.append(d)
    res = run_bass_kernel_spmd(nc, in_maps, core_ids=list(range(8)))
    return np.stack([np.asarray(r["out"], dtype=np.float32) for r in res.results], axis=0)
```
